# Optimizing a Trainium2 kernel written in Bass

```python
import jax, jax.numpy as jnp
from jax import lax
import numpy as np

D_MODEL = 2048
BATCH = 2
SEQ = 8192
DEPTH = 4

CTX_LEN = 256
GRID_W = 64
N_MIXERS = 2
N_A_LAYERS = (DEPTH + N_MIXERS - 1) // N_MIXERS
N_B_LAYERS = DEPTH // N_MIXERS
HG_EXPAND = 128
HG_HEADS = D_MODEL // HG_EXPAND
HG_DK = HG_EXPAND
HG_DV = D_MODEL // HG_HEADS
HG_CHUNK = 16
RNN_WIDTH = D_MODEL
RG_BLOCK = 256
RG_BLOCKS = RNN_WIDTH // RG_BLOCK
RG_C = 8.0
CONV_W = 4
CONV_LEFT = 2
D_FF = 4 * D_MODEL
N_MOD = 6
EPS = 1e-6

kernel_name = 'hybrid_hgrn2_rglru_prefix_dit'


def _rmsnorm(x, gain):
    x32 = x.astype(jnp.float32)
    y = x32 * lax.rsqrt(jnp.mean(jnp.square(x32), axis=-1, keepdims=True) + EPS)
    return (y * gain.astype(jnp.float32)).astype(x.dtype)


def _modulate(h, shift, scale):
    return h * (1 + scale) + shift


def _seg_reverse(t, n_ctx):
    return jnp.concatenate([jnp.flip(t[:, :n_ctx], 1), jnp.flip(t[:, n_ctx:], 1)], axis=1)


def _gla_chunk_scan(q, k, logf, v):
    bsz, T, H, dk = q.shape
    dv = v.shape[-1]
    n = T // HG_CHUNK

    def chunks(t):
        return jnp.swapaxes(t.reshape(bsz, n, HG_CHUNK, H, t.shape[-1]), 0, 1)

    tri = jnp.tril(jnp.ones((HG_CHUNK, HG_CHUNK), dtype=bool))[None, :, :, None, None]

    def step(S, inp):
        qc, kc, gc, vc = inp
        b = jnp.cumsum(gc, axis=1)
        w = jnp.exp(jnp.where(tri, b[:, :, None] - b[:, None, :], -jnp.inf))
        A = jnp.einsum('bthd,bshd,btshd->bhts', qc, kc, w)
        o = jnp.einsum('bhts,bshv->bthv', A, vc) + jnp.einsum('bthd,bhdv->bthv', qc * jnp.exp(b), S)
        b_last = b[:, -1]
        S = jnp.exp(b_last)[..., None] * S + jnp.einsum(
            'bshd,bshv->bhdv', kc * jnp.exp(b_last[:, None] - b), vc)
        return S, o

    S0 = jnp.zeros((bsz, H, dk, dv), jnp.float32)
    _, o = lax.scan(step, S0, (chunks(q), chunks(k), chunks(logf), chunks(v)))
    return jnp.swapaxes(o, 0, 1).reshape(bsz, T, H, dv)


def _hgrn2_mixer(h_ctx, h_lat, w_in, lb, onorm, w_out):
    n_ctx = h_ctx.shape[1]
    h = jnp.concatenate([h_ctx, h_lat], axis=1)
    bsz, T, _ = h.shape
    q, v, g, z_f, z_b = jnp.split(h @ w_in, 5, axis=-1)

    def heads(t, d):
        return t.astype(jnp.float32).reshape(bsz, T, HG_HEADS, d)

    ident = lambda t: t
    rev = lambda t: _seg_reverse(t, n_ctx)
    o = jnp.zeros((bsz, T, HG_HEADS, HG_DV), jnp.float32)
    for z, lb_d, order in ((z_f, lb[0], ident), (z_b, lb[1], rev)):
        z32 = order(z).astype(jnp.float32)
        logf = jnp.logaddexp(jnp.log(lb_d), jnp.log1p(-lb_d) + jax.nn.log_sigmoid(z32))
        k = (1.0 - lb_d) * jax.nn.sigmoid(-z32)
        o_d = _gla_chunk_scan(heads(order(q), HG_DK), heads(k, HG_DK),
                              heads(logf, HG_DK), heads(order(v), HG_DV))
        o = o + order(o_d)
    o = o * lax.rsqrt(jnp.mean(jnp.square(o), axis=-1, keepdims=True) + EPS)
    o = o.reshape(bsz, T, D_MODEL) * onorm.astype(jnp.float32) * jax.nn.silu(g.astype(jnp.float32))
    out = o.astype(h.dtype) @ w_out
    return out[:, :n_ctx], out[:, n_ctx:]


def _centred_dwconv(u, w, b):
    T = u.shape[1]
    up = jnp.pad(u, ((0, 0), (CONV_LEFT, CONV_W - 1 - CONV_LEFT), (0, 0)))
    return sum(w[j] * up[:, j:j + T] for j in range(CONV_W)) + b


def _linear_scan(a, b):
    def combine(l, r):
        return l[0] * r[0], r[0] * l[1] + r[1]
    return lax.associative_scan(combine, (a, b), axis=1)[1]


def _rglru_mixer(h_ctx, h_lat, w_in, conv_w, conv_b, gate_w, gate_b, lam, w_out):
    n_ctx = h_ctx.shape[1]
    h = jnp.concatenate([h_ctx, h_lat], axis=1)
    bsz, T, _ = h.shape
    y, u = jnp.split(h @ w_in, 2, axis=-1)
    u = jnp.concatenate([_centred_dwconv(u[:, :n_ctx], conv_w, conv_b),
                         _centred_dwconv(u[:, n_ctx:], conv_w, conv_b)], axis=1)
    u_blk = u.reshape(bsz, T, RG_BLOCKS, RG_BLOCK)
    u32 = u.astype(jnp.float32)

    def block_gate(d, gi):
        z = jnp.einsum('btnj,njk->btnk', u_blk, gate_w[d, gi]).reshape(bsz, T, RNN_WIDTH) + gate_b[d, gi]
        return jax.nn.sigmoid(z.astype(jnp.float32))

    ident = lambda t: t
    rev = lambda t: _seg_reverse(t, n_ctx)
    hsum = jnp.zeros((bsz, T, RNN_WIDTH), jnp.float32)
    for d, order in ((0, ident), (1, rev)):
        r_gate = block_gate(d, 0)
        i_gate = block_gate(d, 1)
        log_a = -RG_C * jax.nn.softplus(-lam[d].astype(jnp.float32)) * r_gate
        x_in = jnp.sqrt(-jnp.expm1(2.0 * log_a)) * (i_gate * u32)
        hsum = hsum + order(_linear_scan(order(jnp.exp(log_a)), order(x_in)))
    out = (jax.nn.gelu(y.astype(jnp.float32), approximate=True) * hsum).astype(h.dtype) @ w_out
    return out[:, :n_ctx], out[:, n_ctx:]


def _mlp(h, w1, w2):
    return jnp.square(jax.nn.relu(h @ w1)) @ w2


def setup_inputs(seed: int = 0) -> dict:
    key = jax.random.key(seed)
    ks = jax.random.split(key, 24)
    D = D_MODEL

    def nrm(k, shape, scale):
        return jax.random.normal(k, shape, jnp.float32) * scale

    r = jax.random.uniform(ks[17], (N_B_LAYERS, 2, RNN_WIDTH), jnp.float32, 0.9, 0.999)
    a_root = r ** (1.0 / RG_C)
    b_lambda = jnp.log(a_root) - jnp.log1p(-a_root)
    return {
        'x': nrm(ks[0], (BATCH, SEQ, D), 1.0),
        'c': nrm(ks[1], (BATCH, D), 1.0),
        'ctx': nrm(ks[2], (BATCH, CTX_LEN, D), 1.0),
        'c_ctx': nrm(ks[3], (D,), 1.0),
        'w_mod': nrm(ks[4], (DEPTH, D, N_MOD * D), 0.5 * D ** -0.5),
        'b_mod': nrm(ks[5], (DEPTH, N_MOD * D), 0.02),
        'norm1': 1.0 + nrm(ks[6], (DEPTH, D), 0.02),
        'norm2': 1.0 + nrm(ks[7], (DEPTH, D), 0.02),
        'a_w_in': nrm(ks[8], (N_A_LAYERS, D, 5 * D), D ** -0.5),
        'a_lb_logits': nrm(ks[9], (N_A_LAYERS, 2, HG_HEADS * HG_DK), 0.5),
        'a_onorm': 1.0 + nrm(ks[10], (N_A_LAYERS, D), 0.02),
        'a_w_out': nrm(ks[11], (N_A_LAYERS, D, D), D ** -0.5),
        'b_w_in': nrm(ks[12], (N_B_LAYERS, D, 2 * RNN_WIDTH), D ** -0.5),
        'b_conv_w': nrm(ks[13], (N_B_LAYERS, CONV_W, RNN_WIDTH), CONV_W ** -0.5),
        'b_conv_b': nrm(ks[14], (N_B_LAYERS, RNN_WIDTH), 0.02),
        'b_gate_w': nrm(ks[15], (N_B_LAYERS, 2, 2, RG_BLOCKS, RG_BLOCK, RG_BLOCK), RG_BLOCK ** -0.5),
        'b_gate_b': nrm(ks[16], (N_B_LAYERS, 2, 2, RNN_WIDTH), 0.02),
        'b_lambda': b_lambda,
        'b_w_out': nrm(ks[18], (N_B_LAYERS, RNN_WIDTH, D), RNN_WIDTH ** -0.5),
        'mlp_w1': nrm(ks[19], (DEPTH, D, D_FF), D ** -0.5),
        'mlp_w2': nrm(ks[20], (DEPTH, D_FF, D), D_FF ** -0.5),
        'final_norm': 1.0 + nrm(ks[21], (D,), 0.02),
    }


def reference(x, c, ctx, c_ctx, w_mod, b_mod, norm1, norm2, a_w_in, a_lb_logits, a_onorm, a_w_out,
              b_w_in, b_conv_w, b_conv_b, b_gate_w, b_gate_b, b_lambda, b_w_out, mlp_w1, mlp_w2,
              final_norm):
    p = jax.nn.softmax(a_lb_logits.astype(jnp.float32), axis=0)
    lb_all = jnp.cumsum(p, axis=0) - p[0]
    silu_c = jax.nn.silu(c)
    silu_cc = jax.nn.silu(c_ctx)
    for layer in range(DEPTH):
        sh1, sc1, g1, sh2, sc2, g2 = jnp.split((silu_c @ w_mod[layer] + b_mod[layer])[:, None, :], N_MOD, axis=-1)
        csh1, csc1, cg1, csh2, csc2, cg2 = jnp.split(silu_cc @ w_mod[layer] + b_mod[layer], N_MOD, axis=-1)
        h_lat = _modulate(_rmsnorm(x, norm1[layer]), sh1, sc1)
        h_ctx = _modulate(_rmsnorm(ctx, norm1[layer]), csh1, csc1)
        j = layer // N_MIXERS
        if layer % N_MIXERS == 0:
            o_ctx, o_lat = _hgrn2_mixer(h_ctx, h_lat, a_w_in[j], lb_all[j], a_onorm[j], a_w_out[j])
        else:
            o_ctx, o_lat = _rglru_mixer(h_ctx, h_lat, b_w_in[j], b_conv_w[j], b_conv_b[j], b_gate_w[j],
                                        b_gate_b[j], b_lambda[j], b_w_out[j])
        x = x + g1 * o_lat
        x = x + g2 * _mlp(_modulate(_rmsnorm(x, norm2[layer]), sh2, sc2), mlp_w1[layer], mlp_w2[layer])
        if layer < DEPTH - 1:
            ctx = ctx + cg1 * o_ctx
            ctx = ctx + cg2 * _mlp(_modulate(_rmsnorm(ctx, norm2[layer]), csh2, csc2),
                                   mlp_w1[layer], mlp_w2[layer])
    return _rmsnorm(x, final_norm)
```

```python
from contextlib import ExitStack
import numpy as np
import concourse.bass as bass
import concourse.mybir as mybir
from concourse.bass import ds
from concourse.bass_utils import run_bass_kernel_spmd

F32 = mybir.dt.float32
BF16 = mybir.dt.bfloat16
AF = mybir.ActivationFunctionType
ALU = mybir.AluOpType

D = 2048
KC = 16
DFF = 8192
EPS = 1e-6
NCORES = 8
B_, SEQ_, CTX_ = 2, 8192, 256
DEPTH = 4
NLAYERS = 4
DBG_MOD = ''
DBG_STOP = None
GROUPS4 = [[0, 1, 2, 3], [4, 5, 6, 7]]


class Bld:
    ENG = ['pe', 'act', 'dve', 'pool', 'sp']
    NDS = 8

    def __init__(self, nc, es):
        self.nc = nc
        self.es = es
        self.q = {e: [] for e in self.ENG}
        self.cnt = {e: 0 for e in self.ENG + ['cc']}
        self.sem = {e: es.enter_context(nc.semaphore('s_' + e)) for e in ['pe', 'act', 'dve', 'pool', 'cc']}
        self.dsem = {e: [es.enter_context(nc.semaphore('d_%s%d' % (e, i))) for i in range(self.NDS)]
                     for e in ['sp', 'pool', 'act']}
        self.dcnt = {e: 0 for e in ['sp', 'pool', 'act']}
        self.lastw = {}
        self.readers = {}
        self.waited = {e: {} for e in self.ENG}
        self.uid = 0

    def sb(self, name, shape, dtype):
        self.uid += 1
        return self.es.enter_context(self.nc.sbuf_tensor('sb%d_%s' % (self.uid, name), shape, dtype))

    def ps(self, name, shape, dtype=F32):
        self.uid += 1
        return self.es.enter_context(self.nc.psum_tensor('ps%d_%s' % (self.uid, name), shape, dtype))

    def _wait(self, eng, tok):
        kind, e2, n = tok
        if kind == 'c':
            if e2 == eng and eng == 'pe':
                return
            sem = self.sem[e2]
            val = n
            key = ('c', e2)
        else:
            sem = self.dsem[e2][n % self.NDS]
            val = 16 * (n // self.NDS + 1)
            key = ('d', e2, n % self.NDS)
        if self.waited[eng].get(key, 0) >= val:
            return
        self.waited[eng][key] = val
        self.q[eng].append(lambda E, sem=sem, val=val: E.wait_ge(sem, val))

    def _deps(self, eng, reads, writes):
        toks = set()
        for k in reads:
            if k in self.lastw:
                toks.add(self.lastw[k])
        for k in writes:
            if k in self.lastw:
                toks.add(self.lastw[k])
            for t in self.readers.get(k, {}).values():
                if t[0] == 'c' and t[1] == eng:
                    continue
                toks.add(t)
        for t in toks:
            self._wait(eng, t)

    def _commit(self, tok, reads, writes):
        for k in writes:
            self.lastw[k] = tok
            self.readers[k] = {}
        for k in reads:
            r = self.readers.setdefault(k, {})
            if tok[0] == 'c':
                r[('c', tok[1])] = tok
            else:
                r[tok] = tok

    def op(self, eng, fn, reads=(), writes=()):
        self._deps(eng, reads, writes)
        self.cnt[eng] += 1
        n = self.cnt[eng]
        sem = self.sem[eng]
        self.q[eng].append(lambda E, fn=fn, sem=sem: fn(E).then_inc(sem, 1))
        self._commit(('c', eng, n), reads, writes)

    def dma(self, qe, out, in_, reads=(), writes=()):
        i = self.dcnt[qe]
        self.dcnt[qe] += 1
        if i >= self.NDS:
            self._wait(qe, ('d', qe, i - self.NDS))
        self._deps(qe, reads, writes)
        sem = self.dsem[qe][i % self.NDS]

        def f(E, out=out, in_=in_, sem=sem):
            o = out(E) if callable(out) else out
            s = in_(E) if callable(in_) else in_
            try:
                E.dma_start(out=o, in_=s).then_inc(sem, 16)
            except Exception:
                print("DMA build failed: out=", o, " in=", s)
                raise
        self.q[qe].append(f)
        self._commit(('d', qe, i), reads, writes)

    def cc(self, kind, groups, in_ap, out_ap, reads=(), writes=()):
        self._deps('pool', reads, writes)
        self.cnt['cc'] += 1
        n = self.cnt['cc']
        sem = self.sem['cc']
        self.q['pool'].append(lambda E: E.collective_compute(
            kind, ALU.bypass, replica_groups=groups, ins=[in_ap], outs=[out_ap]).then_inc(sem, 1))
        self._commit(('c', 'cc', n), reads, writes)

    def barrier(self):
        for e in self.ENG:
            for e2 in ['pe', 'act', 'dve', 'pool', 'cc']:
                if self.cnt[e2] > 0:
                    self._wait(e, ('c', e2, self.cnt[e2]))
            for qe in ['sp', 'pool', 'act']:
                for i in range(max(0, self.dcnt[qe] - self.NDS), self.dcnt[qe]):
                    self._wait(e, ('d', qe, i))

    def flush(self):
        self.barrier()
        _PID.clear()
        q = self.q
        with self.nc.Block() as block:
            @block.tensor
            def _(E):
                for f in q['pe']:
                    f(E)

            @block.scalar
            def _(E):
                for f in q['act']:
                    f(E)

            @block.vector
            def _(E):
                for f in q['dve']:
                    f(E)

            @block.gpsimd
            def _(E):
                for f in q['pool']:
                    f(E)

            @block.sync
            def _(E):
                for f in q['sp']:
                    f(E)
        self.q = {e: [] for e in self.ENG}
        for a in ('wbuf',):
            if hasattr(self, a):
                delattr(self, a)


_PID = {}
_XQ = {'i': 0}


def _pid(E):
    key = id(E)
    if key not in _PID:
        _PID[key] = {'p': E.partition_id()}
    return _PID[key]


def phase_extract(b, items):
    qe = ['sp', 'act'][_XQ['i'] % 2]
    _XQ['i'] += 1
    for (gat, mine, nsrc, rps, rm, T, which, kin, kout) in items:
        def src(E, gat=gat, nsrc=nsrc, rps=rps, rm=rm, T=T, which=which):
            d = _pid(E)
            bk = (which, rm * T)
            if bk not in d:
                idv = (d['p'] % 4) if which == 'k' else (d['p'] // 4)
                d[bk] = E.compute_val(idv * (rm * T))
            return bass.AP(tensor=gat, offset=d[bk], ap=[[rps * T, nsrc], [T, rm], [1, T]])
        b.dma(qe, mine.rearrange("(i r) t -> i r t", i=nsrc), src, reads=[kin], writes=[kout])
    b.flush()


CH = 64


def gather_rows(b, send, gall, c0, c1, rkeys, wkey):
    for c in range(c0, c1):
        b.cc("AllGather", GROUPS4, send.ap()[c * CH:(c + 1) * CH, :], gall.ap()[c * 4 * CH:(c + 1) * 4 * CH, :],
             reads=rkeys, writes=[wkey])


def mine_rows(ap2d, i, lo, half):
    r0 = ((lo // CH + half) * 4 + i) * CH
    return ap2d[r0:r0 + CH, :]


def write_nat(b, MS, f0, t0, n, src, rkeys, TLc, TCc):
    if t0 < CTX_:
        for i in range(4):
            b.dma('sp', MS[i * 512 + f0:i * 512 + f0 + 128, TLc:TLc + TCc], src[:, i * TCc:(i + 1) * TCc],
                  reads=rkeys, writes=['Msend'])
    else:
        a_ = t0 - CTX_
        i, loc = a_ // TLc, a_ % TLc
        b.dma('sp', MS[i * 512 + f0:i * 512 + f0 + 128, loc:loc + n], src[:, :n], reads=rkeys, writes=['Msend'])


def blocks_for(T, TL):
    bl = []
    t = 0
    while t < TL:
        n = min(512, TL - t)
        bl.append((t, n, 0))
        t += n
    while t < T:
        n = min(512, T - t)
        bl.append((t, n, 1))
        t += n
    return bl


class NormMod:
    def __init__(self, b, tag, ones32, epsb, gain, sc, sh, pkeys):
        self.b = b
        self.tag = tag
        self.ones32 = ones32
        self.epsb = epsb
        self.gain = gain
        self.sh = sh
        self.pkeys = list(pkeys)
        self.sq = b.sb('sq_' + tag, [128, KC, 512], F32)
        self.rs = b.sb('rs_' + tag, [128, 512], F32)
        self.pss = b.ps('pss_' + tag, [128, 512])
        self.gm = b.sb('gm_' + tag, [128, 2, KC], F32)
        for g in range(2):
            b.op('dve', lambda E, g=g: E.scalar_tensor_tensor(
                out=self.gm[:, g, :], in0=sc[g], scalar=1.0, in1=gain, op0=ALU.add, op1=ALU.mult),
                reads=self.pkeys, writes=['gm_' + tag])

    def emit(self, xs, xkey, n, g, hout, hkey, plain=False):
        b = self.b
        tag = self.tag
        sq, rs, pss = self.sq, self.rs, self.pss
        tmp = sq
        b.op('act', lambda E: E.activation(out=sq[:, :, :n], in_=xs, func=AF.Square),
             reads=[xkey], writes=[('sq_' + tag, kc) for kc in range(KC)])
        for kc in range(KC):
            b.op('pe', lambda E, kc=kc: E.matmul(pss[:, :n], lhsT=self.ones32[:, :], rhs=sq[:, kc, :n],
                                                 start=(kc == 0), stop=(kc == KC - 1)),
                 reads=[('sq_' + tag, kc), 'ones32'], writes=['pss_' + tag])
        b.op('act', lambda E: E.activation(out=rs[:, :n], in_=pss[:, :n], func=AF.Sqrt,
                                           scale=1.0 / D, bias=self.epsb[:, 0:1]),
             reads=['pss_' + tag, 'epsb'], writes=['rs_' + tag])
        b.op('dve', lambda E: E.reciprocal(out=rs[:, :n], in_=rs[:, :n]),
             reads=['rs_' + tag], writes=['rs_' + tag])
        for kc in range(KC):
            if plain:
                gcol = self.gain[:, kc:kc + 1]
                b.op('dve', lambda E, kc=kc, gcol=gcol: E.scalar_tensor_tensor(
                    out=hout[:, kc, :], in0=xs[:, kc, :], scalar=gcol, in1=rs[:, :n],
                    op0=ALU.mult, op1=ALU.mult),
                    reads=[xkey, 'rs_' + tag] + self.pkeys, writes=[hkey])
                continue
            gcol = self.gm[:, g, kc:kc + 1]
            shcol = self.sh[g][:, kc:kc + 1]
            b.op('dve', lambda E, kc=kc, gcol=gcol: E.scalar_tensor_tensor(
                out=tmp[:, kc, :n], in0=xs[:, kc, :], scalar=gcol, in1=rs[:, :n],
                op0=ALU.mult, op1=ALU.mult),
                reads=[xkey, 'rs_' + tag, 'gm_' + tag], writes=[('sq_' + tag, kc)])
            b.op('act', lambda E, kc=kc, shcol=shcol: E.activation(
                out=hout[:, kc, :], in_=tmp[:, kc, :n], func=AF.Identity, bias=shcol, scale=1.0),
                reads=[('sq_' + tag, kc)] + self.pkeys, writes=[hkey])


def gemm_stream(b, w_dram, K, N, ngrp, rhs_fn, rhs_keys, tblocks, epi, psums, after_group=None, cache=None):
    kcs = K // 128
    if not hasattr(b, 'wbuf'):
        b.wbuf = [b.sb('wbuf%d' % i, [128, 8192], BF16) for i in range(2)]
        b.wi = 0
        b.pi = 0
    wv = w_dram.rearrange("(kc p) n -> p kc n", p=128)
    for gi in range(N // ngrp):
        wflat = b.wbuf[b.wi % 2]
        wt = wflat[:, :kcs * ngrp].rearrange("p (k n) -> p k n", k=kcs)
        wk = ('wbuf', b.wi % 2)
        b.wi += 1
        if cache is not None and cache[2] == 'use':
            ck = ('wcache', cache[1] + gi)
            b.dma('sp', wflat[:, :], cache[0][cache[1] + gi], reads=[ck], writes=[wk])
        else:
            b.dma('pool', wt[:], wv[:, :, gi * ngrp:(gi + 1) * ngrp], writes=[wk])
            if cache is not None:
                ck = ('wcache', cache[1] + gi)
                b.dma('sp', cache[0][cache[1] + gi], wflat[:, :], reads=[wk], writes=[ck])
        for (t0, n, g) in tblocks:
            for c in range(ngrp // 128):
                ps, pk = psums[b.pi % len(psums)]
                b.pi += 1
                for kc in range(kcs):
                    b.op('pe', lambda E, kc=kc, c=c, ps=ps, wt=wt, t0=t0, n=n: E.matmul(
                        ps[:, :n], lhsT=wt[:, kc, c * 128:(c + 1) * 128], rhs=rhs_fn(kc, t0, n),
                        start=(kc == 0), stop=(kc == kcs - 1)),
                        reads=[wk] + rhs_keys, writes=[pk])
                epi(gi * (ngrp // 128) + c, t0, n, g, ps, pk)
        if after_group is not None:
            after_group(gi)


class Ctx:
    pass


def phase_mod(b, C):
    NCH = 24
    with ExitStack() as es:
        b.es = es
        c_sb = b.sb('c', [128, KC, 4], F32)
        s_sb = b.sb('s', [128, KC, 4], F32)
        bm_sb = b.sb('bm', [128, DEPTH * NCH], F32)
        o_sb = b.sb('o', [128, 4, DEPTH * NCH], F32)
        b.dma('sp', c_sb[:], C.cT.rearrange("(kc p) n -> p kc n", p=128), writes=['c'])
        b.dma('sp', bm_sb[:], C.bm, writes=['bm'])
        b.dma('sp', C.gains_sb[:], C.gains, writes=['gains'])
        b.op('act', lambda E: E.activation(out=s_sb[:], in_=c_sb[:], func=AF.Silu), reads=['c'], writes=['s'])
        wb = [b.sb('wm%d' % i, [128, KC, 512], F32) for i in range(2)]
        pss = [b.ps('pm%d' % i, [128, 4]) for i in range(4)]
        wv = C.wm.rearrange("(l kc p) n -> p l kc n", p=128, kc=KC)
        gi = 0
        for l in range(DEPTH):
            for gq in range(NCH // 4):
                wt = wb[gi % 2]
                wk = ('wm', gi % 2)
                gi += 1
                b.dma('sp', wt[:], wv[:, l, :, gq * 512:(gq + 1) * 512], writes=[wk])
                for c in range(4):
                    j = l * NCH + gq * 4 + c
                    ps = pss[j % 4]
                    for kc in range(KC):
                        b.op('pe', lambda E, kc=kc, c=c, wt=wt, ps=ps: E.matmul(
                            ps[:, :], lhsT=wt[:, kc, c * 128:(c + 1) * 128], rhs=s_sb[:, kc, :],
                            start=(kc == 0), stop=(kc == KC - 1)), reads=[wk, 's'], writes=[('pm', j % 4)])
                    b.op('dve', lambda E, j=j, ps=ps: E.tensor_scalar(
                        out=o_sb[:, :, j], in0=ps[:, :], scalar1=bm_sb[:, j:j + 1], scalar2=None, op0=ALU.add),
                        reads=[('pm', j % 4), 'bm'], writes=['o'])
        b.dma('sp', C.modsend.ap().rearrange("(q p) j -> p q j", p=128), o_sb[:], reads=['o'], writes=['modsend'])
        b.cc("AllGather", GROUPS4, C.modsend.ap().opt(), C.modall.ap().opt(), reads=['modsend'], writes=['modall'])
        b.flush()
    phase_extract(b, [(C.modall, C.modmine.ap(), 4, 512, 128, DEPTH * NCH, 'b', 'modall', 'modmine')])
    MA = C.modall.ap()
    MM = C.modmine.ap()
    for l in range(DEPTH):
        for i in range(4):
            b.dma('sp', C.modtab[:, 0, l, i * NCH:(i + 1) * NCH], MM[i * 128:(i + 1) * 128, l * NCH:(l + 1) * NCH],
                  reads=['modmine'], writes=['modtab'])
            b.dma('sp', C.modtab[:, 1, l, i * NCH:(i + 1) * NCH], MA[i * 512 + 256:i * 512 + 384, l * NCH:(l + 1) * NCH],
                  reads=['modall'], writes=['modtab'])
    b.flush()


def mod_ap(C, st, layer, m):
    return C.modtab[:, st, layer, m * 16:(m + 1) * 16]


def phase_proj(b, C, layer, x_dram, w, NP):
    T, TL = C.Tc, C.TLc
    tbl = blocks_for(T, TL)
    with ExitStack() as es:
        b.es = es
        nm = NormMod(b, 'n1', C.ones32, C.epsb, C.gains_sb[:, layer, :],
                     [mod_ap(C, 0, layer, 1), mod_ap(C, 1, layer, 1)],
                     [mod_ap(C, 0, layer, 0), mod_ap(C, 1, layer, 0)], ['modtab', 'gains'])
        h = b.sb('h', [128, KC, T], BF16)
        xb = [b.sb('xb%d' % i, [128, KC, 512], F32) for i in range(1)]
        xv = x_dram.rearrange("(kc p) t -> p kc t", p=128)
        for bi, (t0, n, g) in enumerate(tbl):
            xs = xb[0]
            b.dma('sp', xs[:, :, :n], xv[:, :, t0:t0 + n], reads=['xdram'], writes=[('xb', 0)])
            nm.emit(xs[:, :, :n], ('xb', 0), n, g, h[:, :, t0:t0 + n], 'h')
        psums = [(b.ps('pp%d' % i, [128, 512]), ('pp', i)) for i in range(4)]
        ob = [b.sb('ob%d' % i, [128, 512], F32) for i in range(4)]
        pvs = [C.Psend[s_].ap().rearrange("(c p) t -> p c t", p=128) for s_ in range(5)]
        st = {'i': 0}

        def epi(nch, t0, n, g, ps, pk):
            i = st['i'] % 4
            st['i'] += 1
            o = ob[i]
            if i % 2:
                b.op('act', lambda E: E.activation(out=o[:, :n], in_=ps[:, :n], func=AF.Copy),
                     reads=[pk], writes=[('ob', i)])
            else:
                b.op('dve', lambda E: E.tensor_copy(out=o[:, :n], in_=ps[:, :n]),
                     reads=[pk], writes=[('ob', i)])
            b.dma('sp', pvs[nch // 16][:, nch % 16, t0:t0 + n], o[:, :n], reads=[('ob', i)],
                  writes=[('Psend', nch // 16)])

        def after_group(gi):
            if gi % 4 == 3:
                s_ = gi // 4
                gather_rows(b, C.Psend[s_], C.Pall[s_], 0, 2048 // CH, [('Psend', s_)], ('Pall', s_))

        gemm_stream(b, w, D, NP, 512, lambda kc, t0, n: h[:, kc, t0:t0 + n], ['h'], tbl, epi, psums, after_group)
        b.flush()


def phase_l3(b, C, layer, x_dram, wo, w1, w2, out_dram, final):
    T, TL = (C.TLc, C.TLc) if final else (C.Tc, C.TLc)
    tbl = blocks_for(T, TL)
    with ExitStack() as es:
        b.es = es
        nm = NormMod(b, 'n2', C.ones32, C.epsb, C.gains_sb[:, 4 + layer, :],
                     [mod_ap(C, 0, layer, 4), mod_ap(C, 1, layer, 4)],
                     [mod_ap(C, 0, layer, 3), mod_ap(C, 1, layer, 3)], ['modtab', 'gains'])
        if final:
            nf = NormMod.__new__(NormMod)
            nf.__dict__.update(nm.__dict__)
            nf.gain = C.gains_sb[:, 8, :]
        gate = {2: [mod_ap(C, 0, layer, 2), mod_ap(C, 1, layer, 2)],
                5: [mod_ap(C, 0, layer, 5), mod_ap(C, 1, layer, 5)]}
        xs = b.sb('xs', [128, KC, 512], F32)
        hb = b.sb('hb', [128, KC, 512], BF16)
        hid = b.sb('hid', [128, 32, 512], BF16)
        r32 = [b.sb('r32_%d' % i, [128, 512], F32) for i in range(2)]
        psums = [(b.ps('pp%d' % i, [128, 512]), ('pp', i)) for i in range(4)]
        xv = x_dram.rearrange("(kc p) t -> p kc t", p=128)
        mv = C.Mmine.ap().rearrange("(fc hf kk r) t -> hf r kk fc t", fc=4, hf=2, kk=4, r=CH)
        ov = out_dram.rearrange("(kc p) t -> p kc t", p=128)
        for bix, (t0, n, g) in enumerate(tbl):
            cm_ = 'fill' if bix == 0 else 'use'
            b.dma('sp', xs[:, :, :n], xv[:, :, t0:t0 + n], reads=['xdram'], writes=['xs'])
            for half in range(2):
                for kk in range(4):
                    b.dma('pool', hb[half * CH:(half + 1) * CH, kk * 4:(kk + 1) * 4, :n],
                          mv[half][:, kk, :, t0:t0 + n], reads=['Mmine'], writes=['hb'])

            def epi_res(m):
                def epi(nch, t0_, n_, g_, ps, pk):
                    gcol = gate[m][g_][:, nch:nch + 1]
                    b.op('dve', lambda E: E.scalar_tensor_tensor(
                        out=xs[:, nch, :n_], in0=ps[:, :n_], scalar=gcol, in1=xs[:, nch, :n_],
                        op0=ALU.mult, op1=ALU.add), reads=[pk, 'modtab', 'xs'], writes=['xs'])
                return epi
            blk = [(0, n, g)]
            gemm_stream(b, wo, D, D, 512, lambda kc, t0_, n_: hb[:, kc, :n_], ['hb'], blk, epi_res(2), psums,
                        cache=(C.wcache, 0, cm_))
            nm.emit(xs[:, :, :n], 'xs', n, g, hb[:, :, :n], 'hb')
            for half in range(2):
                def epi_h(nch, t0_, n_, g_, ps, pk):
                    ri = b.pi % 2
                    r = r32[ri]
                    b.op('act', lambda E: E.activation(out=r[:, :n_], in_=ps[:, :n_], func=AF.Relu),
                         reads=[pk], writes=[('r32', ri)])
                    b.op('pool', lambda E: E.tensor_tensor(out=hid[:, nch, :n_], in0=r[:, :n_], in1=r[:, :n_],
                                                           op=ALU.mult),
                         reads=[('r32', ri)], writes=[('hid', nch)])
                gemm_stream(b, w1[:, half * 4096:(half + 1) * 4096], D, 4096, 512,
                            lambda kc, t0_, n_: hb[:, kc, :n_], ['hb'], blk, epi_h, psums,
                            cache=(C.wcache, 4 + half * 8, cm_))
                gemm_stream(b, w2[half * 4096:(half + 1) * 4096, :], 4096, D, 256,
                            lambda kc, t0_, n_: hid[:, kc, :n_], [('hid', i) for i in range(32)], blk,
                            epi_res(5), psums, cache=(C.wcache, 20 + half * 8, cm_))
            if final:
                nf.emit(xs[:, :, :n], 'xs', n, g, xs[:, :, :n], 'xs', plain=True)
            b.dma('sp', ov[:, :, t0:t0 + n], xs[:, :, :n], reads=['xs'], writes=['xdram'])
        b.flush()


def phase_scan_b(b, C, j):
    SC, SL = CTX_, SEQ_
    TS = SC + SL
    LP = TS + 6
    NU = 2
    blks = [(0, SC, 0)]
    t = 0
    while t < SL:
        blks.append((SC + 3 + t, 512, SC + t))
        t += 512
    nctx = 1
    PAs = [C.Pmine[s_].ap() for s_ in range(2)]
    TLc, TCc = C.TLc, C.TCc


    with ExitStack() as es:
        b.es = es
        pv_sb = b.sb('pv', [128, NU * 22], F32)
        b.dma('sp', pv_sb[:], C.pvB[j], writes=['pv'])
        one = b.sb('one', [128, 1], F32)
        b.op('dve', lambda E: E.memset(one[:], 1.0), writes=['one'])
        zt = b.sb('zt', [128, 4], F32)
        b.op('dve', lambda E: E.memset(zt[:], 0.0), writes=['zt'])
        cdec = b.sb('cdec', [128, NU * 4], F32)
        for u in range(NU):
            cs = slice(u * 4, u * 4 + 4)
            b.op('act', lambda E, u=u, cs=cs: E.activation(out=cdec[:, cs], in_=pv_sb[:, u * 22 + 18:u * 22 + 22],
                                                          func=AF.Exp, scale=-1.0), reads=['pv'], writes=['cdec'])
        b.op('act', lambda E: E.activation(out=cdec[:], in_=cdec[:], func=AF.Ln, bias=one[:, 0:1], scale=1.0),
             reads=['cdec', 'one'], writes=['cdec'])
        b.op('dve', lambda E: E.tensor_scalar(out=cdec[:], in0=cdec[:], scalar1=-8.0, scalar2=None, op0=ALU.mult),
             reads=['cdec'], writes=['cdec'])
        gwb = b.sb('gwb', [128, NU * 4, 2, 256], BF16)
        b.dma('pool', gwb[:], C.gwB[j].rearrange("g (kc p) n -> p g kc n", p=128), writes=['gwb'])
        UP = C.Upad.ap()
        for u in range(NU):
            for cc in range(2):
                for (c0, w_) in ((0, 2), (SC + 2, 3), (LP - 1, 2)):
                    b.dma('sp', UP[u, cc, :, c0:c0 + w_], zt[:, :w_], reads=['zt'], writes=['Upad'])
                for i in range(4):
                    for half in range(2):
                        ps_ = slice(half * CH, (half + 1) * CH)
                        src_ = mine_rows(PAs[1], i, u * 256 + cc * 128, half)
                        b.dma('sp', UP[u, cc, ps_, 2 + i * TCc:2 + (i + 1) * TCc], src_[:, TLc:TLc + TCc],
                              reads=[('Pmine', 1)], writes=['Upad'])
                        b.dma('sp', UP[u, cc, ps_, SC + 5 + i * TLc:SC + 5 + (i + 1) * TLc], src_[:, 0:TLc],
                              reads=[('Pmine', 1)], writes=['Upad'])
        uc32 = b.sb('uc32', [128, 2, TS], F32)
        uc16 = b.sb('uc16', [128, 2, TS], BF16)
        Hf = b.sb('Hf', [128, TS], F32)
        ub = [b.sb('ub%d' % i, [128, 2, 515], F32) for i in range(2)]
        W = {}
        for nm_ in ['r', 'i', 'a', 'a2', 'xin', 'y', 's', 'mo']:
            W[nm_] = [b.sb('w_%s%d' % (nm_, i), [128, 512], F32) for i in range(2)]
        hbr = [b.sb('hbr%d' % i, [128, 512], F32) for i in range(2)]
        pz = [b.ps('pz%d' % i, [128, 512]) for i in range(4)]
        cnt = {'k': 0}
        MS = C.Msend.ap()
        for u in range(NU):
            pb = u * 22
            for bi, (p0, n, t0) in enumerate(blks):
                ub_ = ub[bi % 2]
                uk = ('ub', bi % 2)
                for cc in range(2):
                    b.dma('sp', ub_[:, cc, :n + 3], UP[u, cc, :, p0:p0 + n + 3], reads=['Upad'], writes=[uk])
                for cc in range(2):
                    dst = uc32[:, cc, t0:t0 + n]
                    b.op('dve', lambda E, ub_=ub_, cc=cc, n=n, dst=dst, pb=pb: E.tensor_scalar(
                        out=dst, in0=ub_[:, cc, 0:n], scalar1=pv_sb[:, pb + cc * 4:pb + cc * 4 + 1],
                        scalar2=pv_sb[:, pb + 8 + cc:pb + 9 + cc], op0=ALU.mult, op1=ALU.add),
                        reads=[uk, 'pv'], writes=[('uc32', cc)])
                    for jj in range(1, 4):
                        b.op('dve', lambda E, ub_=ub_, cc=cc, n=n, jj=jj, dst=dst, pb=pb: E.scalar_tensor_tensor(
                            out=dst, in0=ub_[:, cc, jj:jj + n], scalar=pv_sb[:, pb + cc * 4 + jj:pb + cc * 4 + jj + 1],
                            in1=dst, op0=ALU.mult, op1=ALU.add), reads=[uk, 'pv', ('uc32', cc)], writes=[('uc32', cc)])
                    b.op('act', lambda E, cc=cc, t0=t0, n=n, dst=dst: E.activation(
                        out=uc16[:, cc, t0:t0 + n], in_=dst, func=AF.Copy),
                        reads=[('uc32', cc)], writes=[('uc16', cc)])
            for oc in range(2):
                for d in range(2):
                    if d == 0:
                        order = list(range(len(blks)))
                    else:
                        order = list(range(nctx - 1, -1, -1)) + list(range(len(blks) - 1, nctx - 1, -1))
                    prev = None
                    for bi in order:
                        p0, n, t0 = blks[bi]
                        k = cnt['k'] % 2
                        cnt['k'] += 1
                        zr, zi = pz[2 * k], pz[2 * k + 1]
                        for gi_, zp in ((0, zr), (1, zi)):
                            for kc in range(2):
                                b.op('pe', lambda E, zp=zp, gi_=gi_, kc=kc, t0=t0, n=n, d=d, oc=oc, u=u: E.matmul(
                                    zp[:, :n], lhsT=gwb[:, u * 4 + d * 2 + gi_, kc, oc * 128:(oc + 1) * 128],
                                    rhs=uc16[:, kc, t0:t0 + n], start=(kc == 0), stop=(kc == 1)),
                                    reads=['gwb', ('uc16', 0), ('uc16', 1)], writes=[('pz', 2 * k + gi_)])
                        r, ig, a, a2, xin = W['r'][k], W['i'][k], W['a'][k], W['a2'][k], W['xin'][k]
                        c_r = pb + 10 + (d * 2 + 0) * 2 + oc
                        c_i = pb + 10 + (d * 2 + 1) * 2 + oc
                        br = pv_sb[:, c_r:c_r + 1]
                        bi_ = pv_sb[:, c_i:c_i + 1]
                        cd = cdec[:, u * 4 + d * 2 + oc: u * 4 + d * 2 + oc + 1]
                        b.op('act', lambda E, r=r, zr=zr, n=n, br=br: E.activation(
                            out=r[:, :n], in_=zr[:, :n], func=AF.Sigmoid, bias=br, scale=1.0),
                            reads=[('pz', 2 * k), 'pv'], writes=[('r', k)])
                        b.op('act', lambda E, ig=ig, zi=zi, n=n, bi_=bi_: E.activation(
                            out=ig[:, :n], in_=zi[:, :n], func=AF.Sigmoid, bias=bi_, scale=1.0),
                            reads=[('pz', 2 * k + 1), 'pv'], writes=[('i', k)])
                        b.op('act', lambda E, a=a, r=r, n=n, cd=cd: E.activation(
                            out=a[:, :n], in_=r[:, :n], func=AF.Exp, scale=cd),
                            reads=[('r', k), 'cdec'], writes=[('a', k)])
                        b.op('pool', lambda E, a=a, a2=a2, n=n: E.tensor_tensor(
                            out=a2[:, :n], in0=a[:, :n], in1=a[:, :n], op=ALU.mult),
                            reads=[('a', k)], writes=[('a2', k)])
                        b.op('act', lambda E, a2=a2, n=n: E.activation(
                            out=a2[:, :n], in_=a2[:, :n], func=AF.Sqrt, scale=-1.0, bias=one[:, 0:1]),
                            reads=[('a2', k), 'one'], writes=[('a2', k)])
                        b.op('pool', lambda E, a2=a2, ig=ig, xin=xin, n=n: E.tensor_tensor(
                            out=xin[:, :n], in0=a2[:, :n], in1=ig[:, :n], op=ALU.mult),
                            reads=[('a2', k), ('i', k)], writes=[('xin', k)])
                        b.op('dve', lambda E, xin=xin, n=n, t0=t0, oc=oc: E.tensor_tensor(
                            out=xin[:, :n], in0=xin[:, :n], in1=uc32[:, oc, t0:t0 + n], op=ALU.mult),
                            reads=[('xin', k), ('uc32', oc)], writes=[('xin', k)])
                        if d == 0:
                            init = 0.0 if prev is None else Hf[:, t0 - 1:t0]
                            b.op('dve', lambda E, a=a, xin=xin, n=n, t0=t0, init=init: E.tensor_tensor_scan(
                                out=Hf[:, t0:t0 + n], data0=a[:, :n], data1=xin[:, :n], initial=init,
                                op0=ALU.mult, op1=ALU.add), reads=[('a', k), ('xin', k), 'Hf'], writes=['Hf'])
                            prev = bi
                        else:
                            hb_ = hbr[k]
                            if prev is None:
                                init = 0.0
                                rkeys = []
                            else:
                                pk_, pn_ = prev
                                init = hbr[pk_][:, pn_ - 1:pn_]
                                rkeys = [('hbr', pk_)]
                            b.op('dve', lambda E, a=a, xin=xin, n=n, hb_=hb_, init=init: E.tensor_tensor_scan(
                                out=hb_[:, :n], data0=a[:, :n][:, ::-1], data1=xin[:, :n][:, ::-1], initial=init,
                                op0=ALU.mult, op1=ALU.add), reads=[('a', k), ('xin', k)] + rkeys, writes=[('hbr', k)])
                            prev = (k, n)
                            yb, sb_, mo_ = W['y'][k], W['s'][k], W['mo'][k]
                            for half in range(2):
                                ps_ = slice(half * CH, (half + 1) * CH)
                                if t0 < SC:
                                    for i in range(4):
                                        b.dma('sp', yb[ps_, i * TCc:(i + 1) * TCc],
                                              mine_rows(PAs[0], i, u * 256 + oc * 128, half)[:, TLc:TLc + TCc],
                                              reads=[('Pmine', 0)], writes=[('y', k)])
                                else:
                                    a_ = t0 - SC
                                    i, loc = a_ // TLc, a_ % TLc
                                    b.dma('sp', yb[ps_, :n], mine_rows(PAs[0], i, u * 256 + oc * 128, half)[:, loc:loc + n],
                                          reads=[('Pmine', 0)], writes=[('y', k)])
                            b.op('act', lambda E, yb=yb, n=n: E.activation(
                                out=yb[:, :n], in_=yb[:, :n], func=AF.Gelu_apprx_tanh),
                                reads=[('y', k)], writes=[('y', k)])
                            b.op('dve', lambda E, sb_=sb_, hb_=hb_, n=n, t0=t0: E.tensor_tensor(
                                out=sb_[:, :n], in0=Hf[:, t0:t0 + n], in1=hb_[:, :n][:, ::-1], op=ALU.add),
                                reads=['Hf', ('hbr', k)], writes=[('s', k)])
                            b.op('pool', lambda E, sb_=sb_, yb=yb, mo_=mo_, n=n: E.tensor_tensor(
                                out=mo_[:, :n], in0=sb_[:, :n], in1=yb[:, :n], op=ALU.mult),
                                reads=[('s', k), ('y', k)], writes=[('mo', k)])
                            write_nat(b, MS, u * 256 + oc * 128, t0, n, mo_, [('mo', k)], TLc, TCc)
                    if d == 1:
                        f0_ = u * 256 + oc * 128
                        for dest in range(4):
                            c_ = (dest * 512 + f0_) // CH
                            gather_rows(b, C.Msend, C.Mall, c_, c_ + 2, ['Msend'], 'Mall')
        b.flush()


def phase_scan_a(b, C, j, lbmode):
    SC, SL = CTX_, SEQ_
    TS = SC + SL
    NP_, NS = 4, 8
    NT = TS // 128
    TLc, TCc = C.TLc, C.TCc
    PAs = [C.Pmine[s_].ap() for s_ in range(5)]


    def load_nat(q, dst, n, t0, sec, pi_, wkey):
        for half in range(2):
            ps_ = slice(half * CH, (half + 1) * CH)
            if t0 < SC:
                for i in range(4):
                    b.dma(q, dst[ps_, i * TCc:(i + 1) * TCc], mine_rows(PAs[sec], i, pi_ * 128, half)[:, TLc:TLc + TCc],
                          reads=[('Pmine', sec)], writes=[wkey])
            else:
                a_ = t0 - SC
                i, loc = a_ // TLc, a_ % TLc
                b.dma(q, dst[ps_, :n], mine_rows(PAs[sec], i, pi_ * 128, half)[:, loc:loc + n],
                      reads=[('Pmine', sec)], writes=[wkey])

    nat_blks = [(0, SC)] + [(SC + t, 512) for t in range(0, SL, 512)]
    sblk = {0: nat_blks, 1: [(0, SC)] + [(SC + SL - 512 - t, 512) for t in range(0, SL, 512)]}

    with ExitStack() as es:
        b.es = es
        ones32, epsb = C.ones32, C.epsb
        pv_sb = b.sb('pv', [128, 20], F32)
        b.dma('sp', pv_sb[:], C.pvA[j], writes=['pv'])
        cst_sb = b.sb('cst', [128, 897], F32)
        b.dma('sp', cst_sb[:], C.cst, writes=['cst'])
        ident = b.sb('ident', [128, 128], BF16)
        b.op('dve', lambda E: E.tensor_copy(out=ident[:], in_=cst_sb[:, 0:128]), reads=['cst'], writes=['ident'])
        Jm = b.sb('Jm', [128, 128], BF16)
        b.op('dve', lambda E: E.tensor_copy(out=Jm[:], in_=cst_sb[:, 769:897]), reads=['cst'], writes=['Jm'])
        amask = cst_sb[:, 128:256]
        rmask = cst_sb[:, 256:768]
        lb = b.sb('lb', [128, NS], F32)
        oml = b.sb('oml', [128, NS], F32)
        if lbmode:
            pv3 = pv_sb[:, 0:2 * NS].rearrange("p (s two) -> p s two", two=2)
            b.op('dve', lambda E: E.tensor_tensor(out=lb[:], in0=pv3[:, :, 1], in1=pv3[:, :, 0], op=ALU.subtract),
                 reads=['pv'], writes=['lb'])
            b.op('act', lambda E: E.activation(out=lb[:], in_=lb[:], func=AF.Sigmoid), reads=['lb'], writes=['lb'])
            b.op('dve', lambda E: E.tensor_scalar(out=oml[:], in0=lb[:], scalar1=-1.0, scalar2=1.0,
                                                  op0=ALU.mult, op1=ALU.add), reads=['lb'], writes=['oml'])
        O = [b.sb('O%d' % i, [128, TS], F32) for i in range(2)]
        V = [b.sb('V%d' % i, [128, NT, 128], BF16) for i in range(2)]
        vT = [b.sb('vT%d' % i, [128, 512], BF16) for i in range(2)]
        Wk = {}
        for nm_ in ['q', 'z', 'f', 'lf', 'k', 'bc', 'eb', 'enb', 'k32']:
            Wk[nm_] = [b.sb('a_%s%d' % (nm_, i), [128, 512], F32) for i in range(2)]
        Qt = [b.sb('Qt%d' % i, [128, 512], BF16) for i in range(2)]
        Kt = [b.sb('Kt%d' % i, [128, 512], BF16) for i in range(2)]
        KbT = [b.sb('KbT%d' % i, [128, 512], BF16) for i in range(2)]
        Kb = [b.sb('Kb%d' % i, [128, 128], BF16) for i in range(2)]
        Am = [b.sb('Am%d' % i, [128, 128], BF16) for i in range(2)]
        Kbz = [b.sb('Kbz%d' % i, [128, 128], BF16) for i in range(2)]
        S32 = [b.sb('S32_%d' % i, [128, 128], F32) for i in range(2)]
        S16 = [b.sb('S16_%d' % i, [128, 128], BF16) for i in range(2)]
        pT = [b.ps('pT%d' % i, [128, 128], BF16) for i in range(2)]
        pA = [b.ps('pA%d' % i, [128, 512]) for i in range(2)]
        pO = [b.ps('pO%d' % i, [128, 128]) for i in range(2)]
        pS = [b.ps('pS%d' % i, [128, 128]) for i in range(2)]
        fb = {}
        for nm_ in ['s', 'sq', 'rs', 'g', 'mo']:
            fb[nm_] = b.sb('f_%s' % nm_, [128, 512], F32)
        pNb = pA[0]
        MS = C.Msend.ap()

        for pr in range(NP_):
            for ci, (t0, n) in enumerate(nat_blks):
                vt = vT[ci % 2]
                load_nat('pool', vt, n, t0, 1, pr, ('vT', ci % 2))
                for tt in range(n // 128):
                    ti = (t0 + tt * 128) // 128
                    x_ = ti % 2
                    b.op('pe', lambda E, vt=vt, tt=tt, x_=x_: E.transpose(
                        out=pT[x_][:, :], in_=vt[:, tt * 128:(tt + 1) * 128], identity=ident[:, :]),
                        reads=[('vT', ci % 2), 'ident'], writes=[('pT', x_)])
                    b.op('act', lambda E, ti=ti, x_=x_: E.activation(out=V[0][:, ti, :], in_=pT[x_][:, :], func=AF.Copy),
                         reads=[('pT', x_)], writes=[('V', 0)])
                    b.op('pe', lambda E, ti=ti, x_=x_: E.matmul(pS[x_][:, :], lhsT=Jm[:, :], rhs=V[0][:, ti, :],
                                                               start=True, stop=True),
                         reads=[('V', 0), 'Jm'], writes=[('pS', x_)])
                    b.op('dve', lambda E, ti=ti, x_=x_: E.tensor_copy(out=V[1][:, ti, :], in_=pS[x_][:, :]),
                         reads=[('pS', x_)], writes=[('V', 1)])
            for dr in range(2):
                b.op('dve', lambda E, dr=dr: E.memset(S32[dr][:], 0.0), writes=[('S32', dr)])
                b.op('dve', lambda E, dr=dr: E.memset(S16[dr][:], 0.0), writes=[('S16', dr)])
            for bi in range(len(nat_blks)):
                n = nat_blks[bi][1]
                for dr in range(2):
                    s = 2 * pr + dr
                    nt0 = sblk[dr][bi][0]
                    q, z, f, lf, k, bc, eb, enb, k32 = [Wk[x][dr] for x in
                                                        ['q', 'z', 'f', 'lf', 'k', 'bc', 'eb', 'enb', 'k32']]
                    K_ = lambda x, dr=dr: (x, dr)
                    load_nat('sp', q, n, nt0, 0, pr, K_('q'))
                    load_nat('sp', z, n, nt0, 3 + dr, pr, K_('z'))
                    if dr == 0:
                        zin, qin = z[:, :n], q[:, :n]
                    else:
                        zin, qin = z[:, :n][:, ::-1], q[:, :n][:, ::-1]
                    b.op('act', lambda E, zin=zin, f=f, n=n: E.activation(out=f[:, :n], in_=zin, func=AF.Exp, scale=-1.0),
                         reads=[K_('z')], writes=[K_('f')])
                    b.op('pool', lambda E, f=f, n=n: E.tensor_scalar(out=f[:, :n], in0=f[:, :n], scalar1=1.0, scalar2=None,
                                                                    op0=ALU.add), reads=[K_('f')], writes=[K_('f')])
                    b.op('dve', lambda E, f=f, n=n: E.reciprocal(out=f[:, :n], in_=f[:, :n]),
                         reads=[K_('f')], writes=[K_('f')])
                    if lbmode:
                        b.op('dve', lambda E, f=f, n=n, s=s: E.tensor_scalar(
                            out=f[:, :n], in0=f[:, :n], scalar1=oml[:, s:s + 1], scalar2=lb[:, s:s + 1],
                            op0=ALU.mult, op1=ALU.add), reads=[K_('f'), 'lb', 'oml'], writes=[K_('f')])
                    b.op('act', lambda E, f=f, lf=lf, n=n: E.activation(out=lf[:, :n], in_=f[:, :n], func=AF.Ln),
                         reads=[K_('f')], writes=[K_('lf')])
                    b.op('pool', lambda E, f=f, k=k, n=n: E.tensor_scalar(
                        out=k[:, :n], in0=f[:, :n], scalar1=-1.0, scalar2=1.0, op0=ALU.mult, op1=ALU.add),
                        reads=[K_('f')], writes=[K_('k')])
                    b.op('dve', lambda E, lf=lf, bc=bc, n=n: E.tensor_tensor_scan(
                        out=bc[:, :n], data0=rmask[:, :n], data1=lf[:, :n], initial=0.0, op0=ALU.mult, op1=ALU.add),
                        reads=[K_('lf'), 'cst'], writes=[K_('bc')])
                    b.op('act', lambda E, bc=bc, eb=eb, n=n: E.activation(out=eb[:, :n], in_=bc[:, :n], func=AF.Exp),
                         reads=[K_('bc')], writes=[K_('eb')])
                    b.op('act', lambda E, bc=bc, enb=enb, n=n: E.activation(out=enb[:, :n], in_=bc[:, :n], func=AF.Exp,
                                                                          scale=-1.0),
                         reads=[K_('bc')], writes=[K_('enb')])
                    b.op('pool', lambda E, qin=qin, eb=eb, dr=dr, n=n: E.tensor_tensor(
                        out=Qt[dr][:, :n], in0=qin, in1=eb[:, :n], op=ALU.mult),
                        reads=[K_('q'), K_('eb')], writes=[K_('Qt')])
                    b.op('dve', lambda E, k=k, enb=enb, k32=k32, n=n: E.tensor_tensor(
                        out=k32[:, :n], in0=k[:, :n], in1=enb[:, :n], op=ALU.mult),
                        reads=[K_('k'), K_('enb')], writes=[K_('k32')])
                    b.op('act', lambda E, k32=k32, dr=dr, n=n: E.activation(out=Kt[dr][:, :n], in_=k32[:, :n], func=AF.Copy),
                         reads=[K_('k32')], writes=[K_('Kt')])
                    for c in range(n // 32):
                        b.op('pool', lambda E, k32=k32, eb=eb, dr=dr, c=c: E.tensor_scalar(
                            out=KbT[dr][:, c * 32:(c + 1) * 32], in0=k32[:, c * 32:(c + 1) * 32],
                            scalar1=eb[:, c * 32 + 31:c * 32 + 32], scalar2=None, op0=ALU.mult),
                            reads=[K_('k32'), K_('eb')], writes=[K_('KbT')])
                for tt in range(n // 128):
                    for dr in range(2):
                        K_ = lambda x, dr=dr: (x, dr)
                        eb = Wk['eb'][dr]
                        c0 = tt * 128
                        nt0 = sblk[dr][bi][0]
                        if dr == 0:
                            ti = (nt0 + c0) // 128
                            st0 = nt0 + c0
                        else:
                            ti = (nt0 + n - c0 - 128) // 128
                            st0 = (0 if bi == 0 else SC + (bi - 1) * 512) + c0
                        b.op('pe', lambda E, dr=dr, c0=c0: E.transpose(out=pT[dr][:, :], in_=KbT[dr][:, c0:c0 + 128],
                                                                       identity=ident[:, :]),
                             reads=[K_('KbT'), 'ident'], writes=[('pT', dr)])
                        b.op('act', lambda E, dr=dr: E.activation(out=Kb[dr][:, :], in_=pT[dr][:, :], func=AF.Copy),
                             reads=[('pT', dr)], writes=[K_('Kb')])
                        b.op('pool', lambda E, dr=dr: E.tensor_scalar(
                            out=Kbz[dr][64:128, :], in0=Kb[dr][64:128, :], scalar1=cst_sb[64:128, 768:769], scalar2=None,
                            op0=ALU.mult), reads=[K_('Kb'), 'cst'], writes=[K_('Kbz')])
                        b.op('pe', lambda E, dr=dr, c0=c0: E.matmul(pA[dr][:, 0:128], lhsT=Kt[dr][:, c0:c0 + 128],
                                                                    rhs=Qt[dr][:, c0:c0 + 128], start=True, stop=True),
                             reads=[K_('Kt'), K_('Qt')], writes=[K_('pA')])
                        b.op('dve', lambda E, dr=dr: E.tensor_tensor(out=Am[dr][:, :], in0=pA[dr][:, 0:128], in1=amask,
                                                                     op=ALU.mult),
                             reads=[K_('pA'), 'cst'], writes=[K_('Am')])
                        b.op('pe', lambda E, dr=dr, ti=ti: E.matmul(pO[dr][:, :], lhsT=V[dr][:, ti, :], rhs=Am[dr][:, :],
                                                                    start=True, stop=False),
                             reads=[('V', dr), K_('Am')], writes=[K_('pO')])
                        for c in range(4):
                            cs = slice(c * 32, (c + 1) * 32)
                            b.op('pe', lambda E, dr=dr, c0=c0, c=c, cs=cs: E.matmul(
                                pO[dr][:, cs], lhsT=S16[dr][:, :], rhs=Qt[dr][:, c0 + c * 32:c0 + (c + 1) * 32],
                                start=False, stop=(c == 3)),
                                reads=[('S16', dr), K_('Qt')], writes=[K_('pO')])
                            if c < 3:
                                b.op('pe', lambda E, dr=dr, ti=ti, cs=cs: E.matmul(
                                    pS[dr][:, :], lhsT=Kb[dr][cs, :], rhs=V[dr][cs, ti, :], start=True, stop=True),
                                    reads=[K_('Kb'), ('V', dr)], writes=[('pS', dr)])
                            else:
                                b.op('pe', lambda E, dr=dr, ti=ti: E.matmul(
                                    pS[dr][:, :], lhsT=Kbz[dr][64:128, :], rhs=V[dr][64:128, ti, :], start=True, stop=True),
                                    reads=[K_('Kbz'), ('V', dr)], writes=[('pS', dr)])
                            b.op('dve', lambda E, dr=dr, eb=eb, c0=c0, c=c: E.scalar_tensor_tensor(
                                out=S32[dr][:, :], in0=S32[dr][:, :], scalar=eb[:, c0 + c * 32 + 31:c0 + c * 32 + 32],
                                in1=pS[dr][:, :], op0=ALU.mult, op1=ALU.add),
                                reads=[('S32', dr), K_('eb'), ('pS', dr)], writes=[('S32', dr)])
                            b.op('act', lambda E, dr=dr: E.activation(out=S16[dr][:, :], in_=S32[dr][:, :], func=AF.Copy),
                                 reads=[('S32', dr)], writes=[('S16', dr)])
                        b.op('act', lambda E, dr=dr, st0=st0: E.activation(
                            out=O[dr][:, st0:st0 + 128], in_=pO[dr][:, :], func=AF.Copy),
                            reads=[K_('pO')], writes=[('O', dr)])
            for (t0, n) in nat_blks:
                lo = 0 if t0 < SC else SC + SL - (t0 - SC) - n
                sb_, sq, rs, g, mo = fb['s'], fb['sq'], fb['rs'], fb['g'], fb['mo']
                load_nat('sp', g, n, t0, 2, pr, 'fg')
                b.op('dve', lambda E, t0=t0, n=n, lo=lo: E.tensor_tensor(
                    out=sb_[:, :n], in0=O[0][:, t0:t0 + n], in1=O[1][:, lo:lo + n][:, ::-1], op=ALU.add),
                    reads=[('O', 0), ('O', 1)], writes=['fs'])
                b.op('act', lambda E, n=n: E.activation(out=sq[:, :n], in_=sb_[:, :n], func=AF.Square),
                     reads=['fs'], writes=['fsq'])
                b.op('pe', lambda E, n=n: E.matmul(pNb[:, :n], lhsT=ones32[:, :], rhs=sq[:, :n], start=True, stop=True),
                     reads=['fsq', 'ones32'], writes=[('pA', 0)])
                b.op('act', lambda E, n=n: E.activation(out=rs[:, :n], in_=pNb[:, :n], func=AF.Sqrt,
                                                        scale=1.0 / 128, bias=epsb[:, 0:1]),
                     reads=[('pA', 0), 'epsb'], writes=['frs'])
                b.op('dve', lambda E, n=n: E.reciprocal(out=rs[:, :n], in_=rs[:, :n]), reads=['frs'], writes=['frs'])
                b.op('act', lambda E, n=n: E.activation(out=g[:, :n], in_=g[:, :n], func=AF.Silu),
                     reads=['fg'], writes=['fg'])
                b.op('dve', lambda E, n=n, pr=pr: E.scalar_tensor_tensor(
                    out=sb_[:, :n], in0=sb_[:, :n], scalar=pv_sb[:, 16 + pr:17 + pr], in1=rs[:, :n],
                    op0=ALU.mult, op1=ALU.mult), reads=['fs', 'frs', 'pv'], writes=['fs'])
                b.op('pool', lambda E, n=n: E.tensor_tensor(out=mo[:, :n], in0=sb_[:, :n], in1=g[:, :n], op=ALU.mult),
                     reads=['fs', 'fg'], writes=['fmo'])
                write_nat(b, MS, pr * 128, t0, n, mo, ['fmo'], TLc, TCc)
            for dest in range(4):
                c_ = (dest * 512 + pr * 128) // CH
                gather_rows(b, C.Msend, C.Mall, c_, c_ + 2, ['Msend'], 'Mall')
        b.flush()


def _dbg_dump(b, C, oT, src_ap, rkey):
    b.dma('sp', oT, src_ap, reads=[rkey], writes=['out'])
    b.flush()


def build_fused():
    nc = bass.Bass("TRN2", target_bir_lowering=False)
    C = Ctx()
    C.TLc = SEQ_ * B_ // NCORES
    C.TCc = CTX_ * B_ // NCORES
    C.Tc = C.TLc + C.TCc
    TS = CTX_ + SEQ_

    def inp(name, shape):
        return nc.dram_tensor(name, shape, F32, kind="ExternalInput").ap()
    xT = inp("xT", [D, C.Tc])
    C.cT = inp("cT", [D, 4])
    C.wm = inp("wm", [DEPTH * D, 3072])
    C.bm = inp("bm", [128, DEPTH * 24])
    C.gains = inp("gains", [128, 9, KC])
    NA, NB_ = (NLAYERS + 1) // 2, NLAYERS // 2
    if DBG_STOP == 'mod':
        NA, NB_ = 0, 0
    awin = [inp("awin%d" % j, [D, 5 * D]) for j in range(NA)]
    awout = [inp("awout%d" % j, [D, D]) for j in range(0 if DBG_STOP else NA)]
    bwin = [inp("bwin%d" % j, [D, 2 * D]) for j in range(NB_)]
    bwout = [inp("bwout%d" % j, [D, D]) for j in range(NB_)]
    NW = 0 if DBG_STOP else NLAYERS
    w1 = [inp("w1_%d" % l, [D, DFF]) for l in range(NW)]
    w2 = [inp("w2_%d" % l, [DFF, D]) for l in range(NW)]
    C.pvA = inp("pvA", [2, 128, 20])
    C.cst = inp("cst", [128, 897])
    C.pvB = inp("pvB", [2, 128, 44])
    C.gwB = inp("gwB", [2, 8, 256, 256])
    oT = nc.dram_tensor("oT", [D, C.TLc], F32, kind="ExternalOutput").ap()
    C.modsend = nc.dram_tensor("modsend", [512, DEPTH * 24], F32)
    C.modall = nc.dram_tensor("modall", [4 * 512, DEPTH * 24], F32)
    C.modmine = nc.dram_tensor("modmine", [4 * 128, DEPTH * 24], F32)
    C.xbuf = nc.dram_tensor("xbuf", [D, C.Tc], F32)
    C.Psend = [nc.dram_tensor("Psend%d" % s_, [D, C.Tc], F32) for s_ in range(5)]
    C.Pall = [nc.dram_tensor("Pall%d" % s_, [4 * D, C.Tc], F32) for s_ in range(5)]
    C.Pmine = [nc.dram_tensor("Pmine%d" % s_, [D, C.Tc], F32) for s_ in range(5)]
    C.Msend = nc.dram_tensor("Msend", [4 * 512, C.Tc], F32)
    C.Mall = nc.dram_tensor("Mall", [4 * D, C.Tc], F32)
    C.Mmine = nc.dram_tensor("Mmine", [D, C.Tc], F32)
    C.Upad = nc.dram_tensor("Upad", [2, 2, 128, TS + 7], F32)
    C.wcache = nc.dram_tensor("wcache", [36, 128, 8192], BF16).ap()
    with ExitStack() as es:
        b = Bld(nc, es)
        C.ones32 = b.sb('ones32', [128, 128], F32)
        b.op('dve', lambda E: E.memset(C.ones32[:], 1.0), writes=['ones32'])
        C.epsb = b.sb('epsb', [128, 1], F32)
        b.op('dve', lambda E: E.memset(C.epsb[:], EPS), writes=['epsb'])
        C.modtab = b.sb('modtab', [128, 2, DEPTH, 96], F32)
        C.gains_sb = b.sb('gains', [128, 9, KC], F32)
        phase_mod(b, C)
        if DBG_STOP == 'mod':
            with ExitStack() as es2:
                b.es = es2
                t_ = b.sb('dbg', [128, 2 * DEPTH * 96], F32)
                b.op('dve', lambda E: E.tensor_copy(out=t_[:], in_=C.modtab[:].rearrange("p a l g -> p (a l g)")),
                     reads=['modtab'], writes=['dbg'])
                _dbg_dump(b, C, oT[0:256, 0:384].rearrange("(a p) n -> p a n", p=128), t_[:].rearrange("p (a n) -> p a n", a=2), 'dbg')
            return nc
        for layer in range(NLAYERS):
            j = layer // 2
            final = (layer == NLAYERS - 1)
            x_in = xT if layer == 0 else C.xbuf.ap()
            if layer % 2 == 0:
                phase_proj(b, C, layer, x_in, awin[j], 5 * D)
                if DBG_STOP == 'proj':
                    _dbg_dump(b, C, oT[:, 0:C.TLc], C.Pall[0].ap()[4096:6144, 0:C.TLc], ('Pall', 0))
                    return nc
                phase_extract(b, [(C.Pall[s_], C.Pmine[s_].ap(), 1, 2048, 2048, C.Tc, 'k', ('Pall', s_), ('Pmine', s_))
                                  for s_ in range(5)])
                if DBG_STOP == 'extract':
                    _dbg_dump(b, C, oT[:, 0:C.TLc], C.Pmine[3].ap()[:, 0:C.TLc], ('Pmine', 3))
                    return nc
                phase_scan_a(b, C, j, 1 if j > 0 else 0)
                if DBG_STOP == 'scan':
                    _dbg_dump(b, C, oT[:, 0:C.TLc], C.Mall.ap()[2048:4096, 0:C.TLc], 'Mall')
                    return nc
                wo = awout[j]
            else:
                phase_proj(b, C, layer, x_in, bwin[j], 2 * D)
                phase_extract(b, [(C.Pall[s_], C.Pmine[s_].ap(), 1, 2048, 2048, C.Tc, 'k', ('Pall', s_), ('Pmine', s_))
                                  for s_ in (0, 1)])
                phase_scan_b(b, C, j)
                wo = bwout[j]
            phase_extract(b, [(C.Mall, C.Mmine.ap(), 1, 2048, 2048, C.Tc, 'k', 'Mall', 'Mmine')])
            phase_l3(b, C, layer, x_in, wo, w1[layer], w2[layer], oT if final else C.xbuf.ap(), final)
    return nc


_NC = {}


def _f32(a):
    return np.ascontiguousarray(a, dtype=np.float32)


def _pl(v):
    return np.asarray(v, dtype=np.float32).reshape(KC, 128).T


def _consts_a():
    ident = np.eye(128, dtype=np.float32)
    s_ = np.arange(128)[:, None]
    t_ = np.arange(128)[None, :]
    am = ((s_ <= t_) & (s_ // 32 == t_ // 32)).astype(np.float32)
    rm = np.ones((128, 512), np.float32)
    rm[:, ::32] = 0
    m96 = (np.arange(128) >= 96).astype(np.float32)[:, None]
    Jm = np.ascontiguousarray(ident[::-1])
    return np.ascontiguousarray(np.concatenate([ident, am, rm, m96, Jm], axis=1))


def make_in_maps(x, c, ctx, c_ctx, w_mod, b_mod, norm1, norm2, a_w_in, a_lb_logits, a_onorm, a_w_out,
                 b_w_in, b_conv_w, b_conv_b, b_gate_w, b_gate_b, b_lambda, b_w_out, mlp_w1, mlp_w2, final_norm):
    TLc = SEQ_ * B_ // NCORES
    TCc = CTX_ * B_ // NCORES
    x = np.asarray(x, np.float32)
    ctx = np.asarray(ctx, np.float32)
    latf = x.reshape(B_ * SEQ_, D)
    ctxf = ctx.reshape(B_ * CTX_, D)
    cT = np.zeros((D, 4), np.float32)
    cT[:, 0] = c[0]
    cT[:, 1] = c[1]
    cT[:, 2] = c_ctx
    gains = np.zeros((128, 9, KC), np.float32)
    for l in range(NLAYERS):
        gains[:, l] = _pl(norm1[l])
        gains[:, 4 + l] = _pl(norm2[l])
    gains[:, 8] = _pl(final_norm)
    cst = _consts_a()
    shared = {"cT": cT, "gains": _f32(gains), "cst": cst}
    for j in range((NLAYERS + 1) // 2):
        shared["awin%d" % j] = _f32(a_w_in[j])
        shared["awout%d" % j] = _f32(a_w_out[j])
    for j in range(NLAYERS // 2):
        shared["bwin%d" % j] = _f32(b_w_in[j])
        shared["bwout%d" % j] = _f32(b_w_out[j])
    for l in range(NLAYERS):
        shared["w1_%d" % l] = _f32(mlp_w1[l])
        shared["w2_%d" % l] = _f32(mlp_w2[l])
    ims = []
    for r in range(NCORES):
        k = r % 4
        m = dict(shared)
        m["xT"] = _f32(np.concatenate([latf[r * TLc:(r + 1) * TLc], ctxf[r * TCc:(r + 1) * TCc]], axis=0).T)
        m["wm"] = _f32(np.concatenate([w_mod[l][:, k * 3072:(k + 1) * 3072] for l in range(DEPTH)], axis=0))
        m["bm"] = _f32(np.concatenate([np.asarray(b_mod[l][k * 3072:(k + 1) * 3072]).reshape(24, 128).T
                                       for l in range(DEPTH)], axis=1))
        pvA = []
        for j in range(2):
            cols = []
            for pi_ in range(4):
                hd = 4 * k + pi_
                sl = slice(hd * 128, (hd + 1) * 128)
                for d_ in range(2):
                    cols += [a_lb_logits[0, d_, sl], a_lb_logits[j, d_, sl]]
            for pi_ in range(4):
                hd = 4 * k + pi_
                cols.append(a_onorm[j][hd * 128:(hd + 1) * 128])
            pvA.append(np.stack(cols, axis=1))
        m["pvA"] = _f32(np.stack(pvA))
        pvB, gwB = [], []
        for j in range(2):
            cols, gws = [], []
            for u in range(2):
                blk = 2 * k + u
                sl = slice(blk * 256, (blk + 1) * 256)

                def h2(v_):
                    return np.asarray(v_[sl], np.float32).reshape(2, 128)
                cols += [h2(b_conv_w[j][jj])[cc] for cc in range(2) for jj in range(4)] + \
                        [h2(b_conv_b[j])[cc] for cc in range(2)] + \
                        [h2(b_gate_b[j][d_, gi_])[cc] for d_ in range(2) for gi_ in range(2) for cc in range(2)] + \
                        [h2(b_lambda[j][d_])[cc] for d_ in range(2) for cc in range(2)]
                gws += [b_gate_w[j][d_, gi_, blk] for d_ in range(2) for gi_ in range(2)]
            pvB.append(np.stack(cols, axis=1))
            gwB.append(np.stack(gws))
        m["pvB"] = _f32(np.stack(pvB))
        m["gwB"] = _f32(np.stack(gwB))
        ims.append(m)
    return ims


def kernel(**inputs):
    if 'nc' not in _NC:
        _NC['nc'] = build_fused()
    nc = _NC['nc']
    ims = make_in_maps(**inputs)
    res = run_bass_kernel_spmd(nc, ims, core_ids=list(range(NCORES))).results
    out = np.concatenate([res[r]["oT"].T for r in range(NCORES)], axis=0).reshape(B_, SEQ_, D)
    return np.ascontiguousarray(out, dtype=np.float32)
```

```python
import os
from contextlib import ExitStack
import numpy as np
import concourse.bass as bass
import concourse.mybir as mybir
from concourse.bass import ds
from concourse.bass_utils import run_bass_kernel_spmd

F32 = mybir.dt.float32
BF16 = mybir.dt.bfloat16
AF = mybir.ActivationFunctionType
ALU = mybir.AluOpType

D = 2048
KC = 16
DFF = 8192
EPS = 1e-6
NCORES = 8
B_, SEQ_, CTX_ = 2, 8192, 256
DEPTH = 4
NLAYERS = int(os.environ.get('KDBG_NLAYERS', '4'))
DBG_MOD = ''
DBG_STOP = os.environ.get('KDBG_STOP') or None
GROUPS4 = [[0, 1, 2, 3], [4, 5, 6, 7]]


class Bld:
    ENG = ['pe', 'act', 'dve', 'pool', 'sp']
    NDS = 8

    def __init__(self, nc, es):
        self.nc = nc
        self.es = es
        self.q = {e: [] for e in self.ENG}
        self.cnt = {e: 0 for e in self.ENG + ['cc']}
        self.sem = {e: es.enter_context(nc.semaphore('s_' + e)) for e in ['pe', 'act', 'dve', 'pool', 'cc']}
        self.dsem = {e: [es.enter_context(nc.semaphore('d_%s%d' % (e, i))) for i in range(self.NDS)]
                     for e in ['sp', 'pool', 'act']}
        self.dcnt = {e: 0 for e in ['sp', 'pool', 'act']}
        self.lastw = {}
        self.readers = {}
        self.waited = {e: {} for e in self.ENG}
        self.uid = 0

    def sb(self, name, shape, dtype):
        self.uid += 1
        return self.es.enter_context(self.nc.sbuf_tensor('sb%d_%s' % (self.uid, name), shape, dtype))

    def ps(self, name, shape, dtype=F32):
        self.uid += 1
        return self.es.enter_context(self.nc.psum_tensor('ps%d_%s' % (self.uid, name), shape, dtype))

    def _wait(self, eng, tok):
        kind, e2, n = tok
        if kind == 'c':
            if e2 == eng and eng == 'pe':
                return
            sem = self.sem[e2]
            val = n
            key = ('c', e2)
        else:
            sem = self.dsem[e2][n % self.NDS]
            val = 16 * (n // self.NDS + 1)
            key = ('d', e2, n % self.NDS)
        if self.waited[eng].get(key, 0) >= val:
            return
        self.waited[eng][key] = val
        self.q[eng].append(lambda E, sem=sem, val=val: E.wait_ge(sem, val))

    def _deps(self, eng, reads, writes):
        toks = set()
        for k in reads:
            if k in self.lastw:
                toks.add(self.lastw[k])
        for k in writes:
            if k in self.lastw:
                toks.add(self.lastw[k])
            for t in self.readers.get(k, {}).values():
                if t[0] == 'c' and t[1] == eng:
                    continue
                toks.add(t)
        for t in toks:
            self._wait(eng, t)

    def _commit(self, tok, reads, writes):
        for k in writes:
            self.lastw[k] = tok
            self.readers[k] = {}
        for k in reads:
            r = self.readers.setdefault(k, {})
            if tok[0] == 'c':
                r[('c', tok[1])] = tok
            else:
                r[tok] = tok

    def op(self, eng, fn, reads=(), writes=()):
        self._deps(eng, reads, writes)
        self.cnt[eng] += 1
        n = self.cnt[eng]
        sem = self.sem[eng]
        self.q[eng].append(lambda E, fn=fn, sem=sem: fn(E).then_inc(sem, 1))
        self._commit(('c', eng, n), reads, writes)

    def dma(self, qe, out, in_, reads=(), writes=()):
        i = self.dcnt[qe]
        self.dcnt[qe] += 1
        if i >= self.NDS:
            self._wait(qe, ('d', qe, i - self.NDS))
        self._deps(qe, reads, writes)
        sem = self.dsem[qe][i % self.NDS]

        def f(E, out=out, in_=in_, sem=sem):
            o = out(E) if callable(out) else out
            s = in_(E) if callable(in_) else in_
            try:
                E.dma_start(out=o, in_=s).then_inc(sem, 16)
            except Exception:
                print("DMA build failed: out=", o, " in=", s)
                raise
        self.q[qe].append(f)
        self._commit(('d', qe, i), reads, writes)

    def cc(self, kind, groups, in_ap, out_ap, reads=(), writes=()):
        self._deps('pool', reads, writes)
        self.cnt['cc'] += 1
        n = self.cnt['cc']
        sem = self.sem['cc']
        self.q['pool'].append(lambda E: E.collective_compute(
            kind, ALU.bypass, replica_groups=groups, ins=[in_ap], outs=[out_ap]).then_inc(sem, 1))
        self._commit(('c', 'cc', n), reads, writes)

    def barrier(self):
        for e in self.ENG:
            for e2 in ['pe', 'act', 'dve', 'pool', 'cc']:
                if self.cnt[e2] > 0:
                    self._wait(e, ('c', e2, self.cnt[e2]))
            for qe in ['sp', 'pool', 'act']:
                for i in range(max(0, self.dcnt[qe] - self.NDS), self.dcnt[qe]):
                    self._wait(e, ('d', qe, i))

    def flush(self):
        self.barrier()
        _PID.clear()
        q = self.q
        with self.nc.Block() as block:
            @block.tensor
            def _(E):
                for f in q['pe']:
                    f(E)

            @block.scalar
            def _(E):
                for f in q['act']:
                    f(E)

            @block.vector
            def _(E):
                for f in q['dve']:
                    f(E)

            @block.gpsimd
            def _(E):
                for f in q['pool']:
                    f(E)

            @block.sync
            def _(E):
                for f in q['sp']:
                    f(E)
        self.q = {e: [] for e in self.ENG}
        for a in ('wbuf',):
            if hasattr(self, a):
                delattr(self, a)


_PID = {}
_XQ = {'i': 0}


def _pid(E):
    key = id(E)
    if key not in _PID:
        _PID[key] = {'p': E.partition_id()}
    return _PID[key]


def phase_extract(b, items):
    qe = ['sp', 'act'][_XQ['i'] % 2]
    _XQ['i'] += 1
    for (gat, mine, nsrc, rps, rm, T, which, kin, kout) in items:
        def src(E, gat=gat, nsrc=nsrc, rps=rps, rm=rm, T=T, which=which):
            d = _pid(E)
            bk = (which, rm * T)
            if bk not in d:
                idv = (d['p'] % 4) if which == 'k' else (d['p'] // 4)
                d[bk] = E.compute_val(idv * (rm * T))
            return bass.AP(tensor=gat, offset=d[bk], ap=[[rps * T, nsrc], [T, rm], [1, T]])
        b.dma(qe, mine.rearrange("(i r) t -> i r t", i=nsrc), src, reads=[kin], writes=[kout])
    b.flush()


CH = 64


def gather_rows(b, send, gall, c0, c1, rkeys, wkey):
    for c in range(c0, c1):
        b.cc("AllGather", GROUPS4, send.ap()[c * CH:(c + 1) * CH, :], gall.ap()[c * 4 * CH:(c + 1) * 4 * CH, :],
             reads=rkeys, writes=[wkey])


def mine_rows(ap2d, i, lo, half):
    r0 = ((lo // CH + half) * 4 + i) * CH
    return ap2d[r0:r0 + CH, :]


def write_nat(b, MS, f0, t0, n, src, rkeys, TLc, TCc):
    if t0 < CTX_:
        for i in range(4):
            b.dma('sp', MS[i * 512 + f0:i * 512 + f0 + 128, TLc:TLc + TCc], src[:, i * TCc:(i + 1) * TCc],
                  reads=rkeys, writes=['Msend'])
    else:
        a_ = t0 - CTX_
        i, loc = a_ // TLc, a_ % TLc
        b.dma('sp', MS[i * 512 + f0:i * 512 + f0 + 128, loc:loc + n], src[:, :n], reads=rkeys, writes=['Msend'])


def blocks_for(T, TL):
    bl = []
    t = 0
    while t < TL:
        n = min(512, TL - t)
        bl.append((t, n, 0))
        t += n
    while t < T:
        n = min(512, T - t)
        bl.append((t, n, 1))
        t += n
    return bl


class NormMod:
    def __init__(self, b, tag, ones32, epsb, gain, sc, sh, pkeys):
        self.b = b
        self.tag = tag
        self.ones32 = ones32
        self.epsb = epsb
        self.gain = gain
        self.sh = sh
        self.pkeys = list(pkeys)
        self.sq = b.sb('sq_' + tag, [128, KC, 512], F32)
        self.rs = b.sb('rs_' + tag, [128, 512], F32)
        self.pss = b.ps('pss_' + tag, [128, 512])
        self.gm = b.sb('gm_' + tag, [128, 2, KC], F32)
        for g in range(2):
            b.op('dve', lambda E, g=g: E.scalar_tensor_tensor(
                out=self.gm[:, g, :], in0=sc[g], scalar=1.0, in1=gain, op0=ALU.add, op1=ALU.mult),
                reads=self.pkeys, writes=['gm_' + tag])

    def emit(self, xs, xkey, n, g, hout, hkey, plain=False):
        b = self.b
        tag = self.tag
        sq, rs, pss = self.sq, self.rs, self.pss
        tmp = sq
        b.op('act', lambda E: E.activation(out=sq[:, :, :n], in_=xs, func=AF.Square),
             reads=[xkey], writes=[('sq_' + tag, kc) for kc in range(KC)])
        for kc in range(KC):
            b.op('pe', lambda E, kc=kc: E.matmul(pss[:, :n], lhsT=self.ones32[:, :], rhs=sq[:, kc, :n],
                                                 start=(kc == 0), stop=(kc == KC - 1)),
                 reads=[('sq_' + tag, kc), 'ones32'], writes=['pss_' + tag])
        b.op('act', lambda E: E.activation(out=rs[:, :n], in_=pss[:, :n], func=AF.Sqrt,
                                           scale=1.0 / D, bias=self.epsb[:, 0:1]),
             reads=['pss_' + tag, 'epsb'], writes=['rs_' + tag])
        b.op('dve', lambda E: E.reciprocal(out=rs[:, :n], in_=rs[:, :n]),
             reads=['rs_' + tag], writes=['rs_' + tag])
        for kc in range(KC):
            if plain:
                gcol = self.gain[:, kc:kc + 1]
                b.op('dve', lambda E, kc=kc, gcol=gcol: E.scalar_tensor_tensor(
                    out=hout[:, kc, :], in0=xs[:, kc, :], scalar=gcol, in1=rs[:, :n],
                    op0=ALU.mult, op1=ALU.mult),
                    reads=[xkey, 'rs_' + tag] + self.pkeys, writes=[hkey])
                continue
            gcol = self.gm[:, g, kc:kc + 1]
            shcol = self.sh[g][:, kc:kc + 1]
            b.op('dve', lambda E, kc=kc, gcol=gcol: E.scalar_tensor_tensor(
                out=tmp[:, kc, :n], in0=xs[:, kc, :], scalar=gcol, in1=rs[:, :n],
                op0=ALU.mult, op1=ALU.mult),
                reads=[xkey, 'rs_' + tag, 'gm_' + tag], writes=[('sq_' + tag, kc)])
            b.op('act', lambda E, kc=kc, shcol=shcol: E.activation(
                out=hout[:, kc, :], in_=tmp[:, kc, :n], func=AF.Identity, bias=shcol, scale=1.0),
                reads=[('sq_' + tag, kc)] + self.pkeys, writes=[hkey])


def gemm_stream(b, w_dram, K, N, ngrp, rhs_fn, rhs_keys, tblocks, epi, psums, after_group=None, cache=None):
    kcs = K // 128
    if not hasattr(b, 'wbuf'):
        b.wbuf = [b.sb('wbuf%d' % i, [128, 8192], BF16) for i in range(2)]
        b.wi = 0
        b.pi = 0
    wv = w_dram.rearrange("(kc p) n -> p kc n", p=128)
    for gi in range(N // ngrp):
        wflat = b.wbuf[b.wi % 2]
        wt = wflat[:, :kcs * ngrp].rearrange("p (k n) -> p k n", k=kcs)
        wk = ('wbuf', b.wi % 2)
        b.wi += 1
        if cache is not None and cache[2] == 'use':
            ck = ('wcache', cache[1] + gi)
            b.dma('sp', wflat[:, :], cache[0][cache[1] + gi], reads=[ck], writes=[wk])
        else:
            b.dma('pool', wt[:], wv[:, :, gi * ngrp:(gi + 1) * ngrp], writes=[wk])
            if cache is not None:
                ck = ('wcache', cache[1] + gi)
                b.dma('sp', cache[0][cache[1] + gi], wflat[:, :], reads=[wk], writes=[ck])
        for (t0, n, g) in tblocks:
            for c in range(ngrp // 128):
                ps, pk = psums[b.pi % len(psums)]
                b.pi += 1
                for kc in range(kcs):
                    b.op('pe', lambda E, kc=kc, c=c, ps=ps, wt=wt, t0=t0, n=n: E.matmul(
                        ps[:, :n], lhsT=wt[:, kc, c * 128:(c + 1) * 128], rhs=rhs_fn(kc, t0, n),
                        start=(kc == 0), stop=(kc == kcs - 1)),
                        reads=[wk] + rhs_keys, writes=[pk])
                epi(gi * (ngrp // 128) + c, t0, n, g, ps, pk)
        if after_group is not None:
            after_group(gi)


class Ctx:
    pass


def phase_mod(b, C):
    NCH = 24
    with ExitStack() as es:
        b.es = es
        c_sb = b.sb('c', [128, KC, 4], F32)
        s_sb = b.sb('s', [128, KC, 4], F32)
        bm_sb = b.sb('bm', [128, DEPTH * NCH], F32)
        o_sb = b.sb('o', [128, 4, DEPTH * NCH], F32)
        b.dma('sp', c_sb[:], C.cT.rearrange("(kc p) n -> p kc n", p=128), writes=['c'])
        b.dma('sp', bm_sb[:], C.bm, writes=['bm'])
        b.dma('sp', C.gains_sb[:], C.gains, writes=['gains'])
        b.op('act', lambda E: E.activation(out=s_sb[:], in_=c_sb[:], func=AF.Silu), reads=['c'], writes=['s'])
        wb = [b.sb('wm%d' % i, [128, KC, 512], F32) for i in range(2)]
        pss = [b.ps('pm%d' % i, [128, 4]) for i in range(4)]
        wv = C.wm.rearrange("(l kc p) n -> p l kc n", p=128, kc=KC)
        gi = 0
        for l in range(DEPTH):
            for gq in range(NCH // 4):
                wt = wb[gi % 2]
                wk = ('wm', gi % 2)
                gi += 1
                b.dma('sp', wt[:], wv[:, l, :, gq * 512:(gq + 1) * 512], writes=[wk])
                for c in range(4):
                    j = l * NCH + gq * 4 + c
                    ps = pss[j % 4]
                    for kc in range(KC):
                        b.op('pe', lambda E, kc=kc, c=c, wt=wt, ps=ps: E.matmul(
                            ps[:, :], lhsT=wt[:, kc, c * 128:(c + 1) * 128], rhs=s_sb[:, kc, :],
                            start=(kc == 0), stop=(kc == KC - 1)), reads=[wk, 's'], writes=[('pm', j % 4)])
                    b.op('dve', lambda E, j=j, ps=ps: E.tensor_scalar(
                        out=o_sb[:, :, j], in0=ps[:, :], scalar1=bm_sb[:, j:j + 1], scalar2=None, op0=ALU.add),
                        reads=[('pm', j % 4), 'bm'], writes=['o'])
        b.dma('sp', C.modsend.ap().rearrange("(q p) j -> p q j", p=128), o_sb[:], reads=['o'], writes=['modsend'])
        b.cc("AllGather", GROUPS4, C.modsend.ap().opt(), C.modall.ap().opt(), reads=['modsend'], writes=['modall'])
        b.flush()
    phase_extract(b, [(C.modall, C.modmine.ap(), 4, 512, 128, DEPTH * NCH, 'b', 'modall', 'modmine')])
    MA = C.modall.ap()
    MM = C.modmine.ap()
    for l in range(DEPTH):
        for i in range(4):
            b.dma('sp', C.modtab[:, 0, l, i * NCH:(i + 1) * NCH], MM[i * 128:(i + 1) * 128, l * NCH:(l + 1) * NCH],
                  reads=['modmine'], writes=['modtab'])
            b.dma('sp', C.modtab[:, 1, l, i * NCH:(i + 1) * NCH], MA[i * 512 + 256:i * 512 + 384, l * NCH:(l + 1) * NCH],
                  reads=['modall'], writes=['modtab'])
    b.flush()


def mod_ap(C, st, layer, m):
    return C.modtab[:, st, layer, m * 16:(m + 1) * 16]


def phase_proj(b, C, layer, x_dram, w, NP):
    T, TL = C.Tc, C.TLc
    tbl = blocks_for(T, TL)
    with ExitStack() as es:
        b.es = es
        nm = NormMod(b, 'n1', C.ones32, C.epsb, C.gains_sb[:, layer, :],
                     [mod_ap(C, 0, layer, 1), mod_ap(C, 1, layer, 1)],
                     [mod_ap(C, 0, layer, 0), mod_ap(C, 1, layer, 0)], ['modtab', 'gains'])
        h = b.sb('h', [128, KC, T], BF16)
        xb = [b.sb('xb%d' % i, [128, KC, 512], F32) for i in range(1)]
        xv = x_dram.rearrange("(kc p) t -> p kc t", p=128)
        for bi, (t0, n, g) in enumerate(tbl):
            xs = xb[0]
            b.dma('sp', xs[:, :, :n], xv[:, :, t0:t0 + n], reads=['xdram'], writes=[('xb', 0)])
            nm.emit(xs[:, :, :n], ('xb', 0), n, g, h[:, :, t0:t0 + n], 'h')
        psums = [(b.ps('pp%d' % i, [128, 512]), ('pp', i)) for i in range(4)]
        ob = [b.sb('ob%d' % i, [128, 512], F32) for i in range(4)]
        pvs = [C.Psend[s_].ap().rearrange("(c p) t -> p c t", p=128) for s_ in range(5)]
        st = {'i': 0}

        def epi(nch, t0, n, g, ps, pk):
            i = st['i'] % 4
            st['i'] += 1
            o = ob[i]
            if i % 2:
                b.op('act', lambda E: E.activation(out=o[:, :n], in_=ps[:, :n], func=AF.Copy),
                     reads=[pk], writes=[('ob', i)])
            else:
                b.op('dve', lambda E: E.tensor_copy(out=o[:, :n], in_=ps[:, :n]),
                     reads=[pk], writes=[('ob', i)])
            b.dma('sp', pvs[nch // 16][:, nch % 16, t0:t0 + n], o[:, :n], reads=[('ob', i)],
                  writes=[('Psend', nch // 16)])

        def after_group(gi):
            if gi % 4 == 3:
                s_ = gi // 4
                gather_rows(b, C.Psend[s_], C.Pall[s_], 0, 2048 // CH, [('Psend', s_)], ('Pall', s_))

        gemm_stream(b, w, D, NP, 512, lambda kc, t0, n: h[:, kc, t0:t0 + n], ['h'], tbl, epi, psums, after_group)
        b.flush()


def phase_l3(b, C, layer, x_dram, wo, w1, w2, out_dram, final):
    T, TL = (C.TLc, C.TLc) if final else (C.Tc, C.TLc)
    tbl = blocks_for(T, TL)
    with ExitStack() as es:
        b.es = es
        nm = NormMod(b, 'n2', C.ones32, C.epsb, C.gains_sb[:, 4 + layer, :],
                     [mod_ap(C, 0, layer, 4), mod_ap(C, 1, layer, 4)],
                     [mod_ap(C, 0, layer, 3), mod_ap(C, 1, layer, 3)], ['modtab', 'gains'])
        if final:
            nf = NormMod.__new__(NormMod)
            nf.__dict__.update(nm.__dict__)
            nf.gain = C.gains_sb[:, 8, :]
        gate = {2: [mod_ap(C, 0, layer, 2), mod_ap(C, 1, layer, 2)],
                5: [mod_ap(C, 0, layer, 5), mod_ap(C, 1, layer, 5)]}
        xs = b.sb('xs', [128, KC, 512], F32)
        hb = b.sb('hb', [128, KC, 512], BF16)
        hid = b.sb('hid', [128, 32, 512], BF16)
        r32 = [b.sb('r32_%d' % i, [128, 512], F32) for i in range(2)]
        psums = [(b.ps('pp%d' % i, [128, 512]), ('pp', i)) for i in range(4)]
        xv = x_dram.rearrange("(kc p) t -> p kc t", p=128)
        mv = C.Mmine.ap().rearrange("(fc hf kk r) t -> hf r kk fc t", fc=4, hf=2, kk=4, r=CH)
        ov = out_dram.rearrange("(kc p) t -> p kc t", p=128)
        for bix, (t0, n, g) in enumerate(tbl):
            cm_ = 'fill' if bix == 0 else 'use'
            b.dma('sp', xs[:, :, :n], xv[:, :, t0:t0 + n], reads=['xdram'], writes=['xs'])
            for half in range(2):
                for kk in range(4):
                    b.dma('pool', hb[half * CH:(half + 1) * CH, kk * 4:(kk + 1) * 4, :n],
                          mv[half][:, kk, :, t0:t0 + n], reads=['Mmine'], writes=['hb'])

            def epi_res(m):
                def epi(nch, t0_, n_, g_, ps, pk):
                    gcol = gate[m][g_][:, nch:nch + 1]
                    b.op('dve', lambda E: E.scalar_tensor_tensor(
                        out=xs[:, nch, :n_], in0=ps[:, :n_], scalar=gcol, in1=xs[:, nch, :n_],
                        op0=ALU.mult, op1=ALU.add), reads=[pk, 'modtab', 'xs'], writes=['xs'])
                return epi
            blk = [(0, n, g)]
            gemm_stream(b, wo, D, D, 512, lambda kc, t0_, n_: hb[:, kc, :n_], ['hb'], blk, epi_res(2), psums,
                        cache=(C.wcache, 0, cm_))
            nm.emit(xs[:, :, :n], 'xs', n, g, hb[:, :, :n], 'hb')
            for half in range(2):
                def epi_h(nch, t0_, n_, g_, ps, pk):
                    ri = b.pi % 2
                    r = r32[ri]
                    b.op('act', lambda E: E.activation(out=r[:, :n_], in_=ps[:, :n_], func=AF.Relu),
                         reads=[pk], writes=[('r32', ri)])
                    b.op('pool', lambda E: E.tensor_tensor(out=hid[:, nch, :n_], in0=r[:, :n_], in1=r[:, :n_],
                                                           op=ALU.mult),
                         reads=[('r32', ri)], writes=[('hid', nch)])
                gemm_stream(b, w1[:, half * 4096:(half + 1) * 4096], D, 4096, 512,
                            lambda kc, t0_, n_: hb[:, kc, :n_], ['hb'], blk, epi_h, psums,
                            cache=(C.wcache, 4 + half * 8, cm_))
                gemm_stream(b, w2[half * 4096:(half + 1) * 4096, :], 4096, D, 256,
                            lambda kc, t0_, n_: hid[:, kc, :n_], [('hid', i) for i in range(32)], blk,
                            epi_res(5), psums, cache=(C.wcache, 20 + half * 8, cm_))
            if final:
                nf.emit(xs[:, :, :n], 'xs', n, g, xs[:, :, :n], 'xs', plain=True)
            b.dma('sp', ov[:, :, t0:t0 + n], xs[:, :, :n], reads=['xs'], writes=['xdram'])
        b.flush()


def phase_scan_b(b, C, j):
    SC, SL = CTX_, SEQ_
    TS = SC + SL
    LP = TS + 6
    NU = 2
    blks = [(0, SC, 0)]
    t = 0
    while t < SL:
        blks.append((SC + 3 + t, 512, SC + t))
        t += 512
    nctx = 1
    PAs = [C.Pmine[s_].ap() for s_ in range(2)]
    TLc, TCc = C.TLc, C.TCc


    with ExitStack() as es:
        b.es = es
        pv_sb = b.sb('pv', [128, NU * 22], F32)
        b.dma('sp', pv_sb[:], C.pvB[j], writes=['pv'])
        one = b.sb('one', [128, 1], F32)
        b.op('dve', lambda E: E.memset(one[:], 1.0), writes=['one'])
        zt = b.sb('zt', [128, 4], F32)
        b.op('dve', lambda E: E.memset(zt[:], 0.0), writes=['zt'])
        cdec = b.sb('cdec', [128, NU * 4], F32)
        for u in range(NU):
            cs = slice(u * 4, u * 4 + 4)
            b.op('act', lambda E, u=u, cs=cs: E.activation(out=cdec[:, cs], in_=pv_sb[:, u * 22 + 18:u * 22 + 22],
                                                          func=AF.Exp, scale=-1.0), reads=['pv'], writes=['cdec'])
        b.op('act', lambda E: E.activation(out=cdec[:], in_=cdec[:], func=AF.Ln, bias=one[:, 0:1], scale=1.0),
             reads=['cdec', 'one'], writes=['cdec'])
        b.op('dve', lambda E: E.tensor_scalar(out=cdec[:], in0=cdec[:], scalar1=-8.0, scalar2=None, op0=ALU.mult),
             reads=['cdec'], writes=['cdec'])
        gwb = b.sb('gwb', [128, NU * 4, 2, 256], BF16)
        b.dma('pool', gwb[:], C.gwB[j].rearrange("g (kc p) n -> p g kc n", p=128), writes=['gwb'])
        UP = C.Upad.ap()
        for u in range(NU):
            for cc in range(2):
                for (c0, w_) in ((0, 2), (SC + 2, 3), (LP - 1, 2)):
                    b.dma('sp', UP[u, cc, :, c0:c0 + w_], zt[:, :w_], reads=['zt'], writes=['Upad'])
                for i in range(4):
                    for half in range(2):
                        ps_ = slice(half * CH, (half + 1) * CH)
                        src_ = mine_rows(PAs[1], i, u * 256 + cc * 128, half)
                        b.dma('sp', UP[u, cc, ps_, 2 + i * TCc:2 + (i + 1) * TCc], src_[:, TLc:TLc + TCc],
                              reads=[('Pmine', 1)], writes=['Upad'])
                        b.dma('sp', UP[u, cc, ps_, SC + 5 + i * TLc:SC + 5 + (i + 1) * TLc], src_[:, 0:TLc],
                              reads=[('Pmine', 1)], writes=['Upad'])
        uc32 = b.sb('uc32', [128, 2, TS], F32)
        uc16 = b.sb('uc16', [128, 2, TS], BF16)
        Hf = b.sb('Hf', [128, TS], F32)
        ub = [b.sb('ub%d' % i, [128, 2, 515], F32) for i in range(2)]
        W = {}
        for nm_ in ['r', 'i', 'a', 'a2', 'xin', 'y', 's', 'mo']:
            W[nm_] = [b.sb('w_%s%d' % (nm_, i), [128, 512], F32) for i in range(2)]
        hbr = [b.sb('hbr%d' % i, [128, 512], F32) for i in range(2)]
        pz = [b.ps('pz%d' % i, [128, 512]) for i in range(4)]
        cnt = {'k': 0}
        MS = C.Msend.ap()
        for u in range(NU):
            pb = u * 22
            for bi, (p0, n, t0) in enumerate(blks):
                ub_ = ub[bi % 2]
                uk = ('ub', bi % 2)
                for cc in range(2):
                    b.dma('sp', ub_[:, cc, :n + 3], UP[u, cc, :, p0:p0 + n + 3], reads=['Upad'], writes=[uk])
                for cc in range(2):
                    dst = uc32[:, cc, t0:t0 + n]
                    b.op('dve', lambda E, ub_=ub_, cc=cc, n=n, dst=dst, pb=pb: E.tensor_scalar(
                        out=dst, in0=ub_[:, cc, 0:n], scalar1=pv_sb[:, pb + cc * 4:pb + cc * 4 + 1],
                        scalar2=pv_sb[:, pb + 8 + cc:pb + 9 + cc], op0=ALU.mult, op1=ALU.add),
                        reads=[uk, 'pv'], writes=[('uc32', cc)])
                    for jj in range(1, 4):
                        b.op('dve', lambda E, ub_=ub_, cc=cc, n=n, jj=jj, dst=dst, pb=pb: E.scalar_tensor_tensor(
                            out=dst, in0=ub_[:, cc, jj:jj + n], scalar=pv_sb[:, pb + cc * 4 + jj:pb + cc * 4 + jj + 1],
                            in1=dst, op0=ALU.mult, op1=ALU.add), reads=[uk, 'pv', ('uc32', cc)], writes=[('uc32', cc)])
                    b.op('act', lambda E, cc=cc, t0=t0, n=n, dst=dst: E.activation(
                        out=uc16[:, cc, t0:t0 + n], in_=dst, func=AF.Copy),
                        reads=[('uc32', cc)], writes=[('uc16', cc)])
            for oc in range(2):
                for d in range(2):
                    if d == 0:
                        order = list(range(len(blks)))
                    else:
                        order = list(range(nctx - 1, -1, -1)) + list(range(len(blks) - 1, nctx - 1, -1))
                    prev = None
                    for bi in order:
                        p0, n, t0 = blks[bi]
                        k = cnt['k'] % 2
                        cnt['k'] += 1
                        zr, zi = pz[2 * k], pz[2 * k + 1]
                        for gi_, zp in ((0, zr), (1, zi)):
                            for kc in range(2):
                                b.op('pe', lambda E, zp=zp, gi_=gi_, kc=kc, t0=t0, n=n, d=d, oc=oc, u=u: E.matmul(
                                    zp[:, :n], lhsT=gwb[:, u * 4 + d * 2 + gi_, kc, oc * 128:(oc + 1) * 128],
                                    rhs=uc16[:, kc, t0:t0 + n], start=(kc == 0), stop=(kc == 1)),
                                    reads=['gwb', ('uc16', 0), ('uc16', 1)], writes=[('pz', 2 * k + gi_)])
                        r, ig, a, a2, xin = W['r'][k], W['i'][k], W['a'][k], W['a2'][k], W['xin'][k]
                        c_r = pb + 10 + (d * 2 + 0) * 2 + oc
                        c_i = pb + 10 + (d * 2 + 1) * 2 + oc
                        br = pv_sb[:, c_r:c_r + 1]
                        bi_ = pv_sb[:, c_i:c_i + 1]
                        cd = cdec[:, u * 4 + d * 2 + oc: u * 4 + d * 2 + oc + 1]
                        b.op('act', lambda E, r=r, zr=zr, n=n, br=br: E.activation(
                            out=r[:, :n], in_=zr[:, :n], func=AF.Sigmoid, bias=br, scale=1.0),
                            reads=[('pz', 2 * k), 'pv'], writes=[('r', k)])
                        b.op('act', lambda E, ig=ig, zi=zi, n=n, bi_=bi_: E.activation(
                            out=ig[:, :n], in_=zi[:, :n], func=AF.Sigmoid, bias=bi_, scale=1.0),
                            reads=[('pz', 2 * k + 1), 'pv'], writes=[('i', k)])
                        b.op('act', lambda E, a=a, r=r, n=n, cd=cd: E.activation(
                            out=a[:, :n], in_=r[:, :n], func=AF.Exp, scale=cd),
                            reads=[('r', k), 'cdec'], writes=[('a', k)])
                        b.op('pool', lambda E, a=a, a2=a2, n=n: E.tensor_tensor(
                            out=a2[:, :n], in0=a[:, :n], in1=a[:, :n], op=ALU.mult),
                            reads=[('a', k)], writes=[('a2', k)])
                        b.op('act', lambda E, a2=a2, n=n: E.activation(
                            out=a2[:, :n], in_=a2[:, :n], func=AF.Sqrt, scale=-1.0, bias=one[:, 0:1]),
                            reads=[('a2', k), 'one'], writes=[('a2', k)])
                        b.op('pool', lambda E, a2=a2, ig=ig, xin=xin, n=n: E.tensor_tensor(
                            out=xin[:, :n], in0=a2[:, :n], in1=ig[:, :n], op=ALU.mult),
                            reads=[('a2', k), ('i', k)], writes=[('xin', k)])
                        b.op('dve', lambda E, xin=xin, n=n, t0=t0, oc=oc: E.tensor_tensor(
                            out=xin[:, :n], in0=xin[:, :n], in1=uc32[:, oc, t0:t0 + n], op=ALU.mult),
                            reads=[('xin', k), ('uc32', oc)], writes=[('xin', k)])
                        if d == 0:
                            init = 0.0 if prev is None else Hf[:, t0 - 1:t0]
                            b.op('dve', lambda E, a=a, xin=xin, n=n, t0=t0, init=init: E.tensor_tensor_scan(
                                out=Hf[:, t0:t0 + n], data0=a[:, :n], data1=xin[:, :n], initial=init,
                                op0=ALU.mult, op1=ALU.add), reads=[('a', k), ('xin', k), 'Hf'], writes=['Hf'])
                            prev = bi
                        else:
                            hb_ = hbr[k]
                            if prev is None:
                                init = 0.0
                                rkeys = []
                            else:
                                pk_, pn_ = prev
                                init = hbr[pk_][:, pn_ - 1:pn_]
                                rkeys = [('hbr', pk_)]
                            b.op('dve', lambda E, a=a, xin=xin, n=n, hb_=hb_, init=init: E.tensor_tensor_scan(
                                out=hb_[:, :n], data0=a[:, :n][:, ::-1], data1=xin[:, :n][:, ::-1], initial=init,
                                op0=ALU.mult, op1=ALU.add), reads=[('a', k), ('xin', k)] + rkeys, writes=[('hbr', k)])
                            prev = (k, n)
                            yb, sb_, mo_ = W['y'][k], W['s'][k], W['mo'][k]
                            for half in range(2):
                                ps_ = slice(half * CH, (half + 1) * CH)
                                if t0 < SC:
                                    for i in range(4):
                                        b.dma('sp', yb[ps_, i * TCc:(i + 1) * TCc],
                                              mine_rows(PAs[0], i, u * 256 + oc * 128, half)[:, TLc:TLc + TCc],
                                              reads=[('Pmine', 0)], writes=[('y', k)])
                                else:
                                    a_ = t0 - SC
                                    i, loc = a_ // TLc, a_ % TLc
                                    b.dma('sp', yb[ps_, :n], mine_rows(PAs[0], i, u * 256 + oc * 128, half)[:, loc:loc + n],
                                          reads=[('Pmine', 0)], writes=[('y', k)])
                            b.op('act', lambda E, yb=yb, n=n: E.activation(
                                out=yb[:, :n], in_=yb[:, :n], func=AF.Gelu_apprx_tanh),
                                reads=[('y', k)], writes=[('y', k)])
                            b.op('dve', lambda E, sb_=sb_, hb_=hb_, n=n, t0=t0: E.tensor_tensor(
                                out=sb_[:, :n], in0=Hf[:, t0:t0 + n], in1=hb_[:, :n][:, ::-1], op=ALU.add),
                                reads=['Hf', ('hbr', k)], writes=[('s', k)])
                            b.op('pool', lambda E, sb_=sb_, yb=yb, mo_=mo_, n=n: E.tensor_tensor(
                                out=mo_[:, :n], in0=sb_[:, :n], in1=yb[:, :n], op=ALU.mult),
                                reads=[('s', k), ('y', k)], writes=[('mo', k)])
                            write_nat(b, MS, u * 256 + oc * 128, t0, n, mo_, [('mo', k)], TLc, TCc)
                    if d == 1:
                        f0_ = u * 256 + oc * 128
                        for dest in range(4):
                            c_ = (dest * 512 + f0_) // CH
                            gather_rows(b, C.Msend, C.Mall, c_, c_ + 2, ['Msend'], 'Mall')
        b.flush()


def phase_scan_a(b, C, j, lbmode):
    SC, SL = CTX_, SEQ_
    TS = SC + SL
    NP_, NS = 4, 8
    NT = TS // 128
    TLc, TCc = C.TLc, C.TCc
    PAs = [C.Pmine[s_].ap() for s_ in range(5)]


    def load_nat(q, dst, n, t0, sec, pi_, wkey):
        for half in range(2):
            ps_ = slice(half * CH, (half + 1) * CH)
            if t0 < SC:
                for i in range(4):
                    b.dma(q, dst[ps_, i * TCc:(i + 1) * TCc], mine_rows(PAs[sec], i, pi_ * 128, half)[:, TLc:TLc + TCc],
                          reads=[('Pmine', sec)], writes=[wkey])
            else:
                a_ = t0 - SC
                i, loc = a_ // TLc, a_ % TLc
                b.dma(q, dst[ps_, :n], mine_rows(PAs[sec], i, pi_ * 128, half)[:, loc:loc + n],
                      reads=[('Pmine', sec)], writes=[wkey])

    nat_blks = [(0, SC)] + [(SC + t, 512) for t in range(0, SL, 512)]
    sblk = {0: nat_blks, 1: [(0, SC)] + [(SC + SL - 512 - t, 512) for t in range(0, SL, 512)]}

    with ExitStack() as es:
        b.es = es
        ones32, epsb = C.ones32, C.epsb
        pv_sb = b.sb('pv', [128, 20], F32)
        b.dma('sp', pv_sb[:], C.pvA[j], writes=['pv'])
        cst_sb = b.sb('cst', [128, 897], F32)
        b.dma('sp', cst_sb[:], C.cst, writes=['cst'])
        ident = b.sb('ident', [128, 128], BF16)
        b.op('dve', lambda E: E.tensor_copy(out=ident[:], in_=cst_sb[:, 0:128]), reads=['cst'], writes=['ident'])
        Jm = b.sb('Jm', [128, 128], BF16)
        b.op('dve', lambda E: E.tensor_copy(out=Jm[:], in_=cst_sb[:, 769:897]), reads=['cst'], writes=['Jm'])
        amask = cst_sb[:, 128:256]
        rmask = cst_sb[:, 256:768]
        lb = b.sb('lb', [128, NS], F32)
        oml = b.sb('oml', [128, NS], F32)
        if lbmode:
            pv3 = pv_sb[:, 0:2 * NS].rearrange("p (s two) -> p s two", two=2)
            b.op('dve', lambda E: E.tensor_tensor(out=lb[:], in0=pv3[:, :, 1], in1=pv3[:, :, 0], op=ALU.subtract),
                 reads=['pv'], writes=['lb'])
            b.op('act', lambda E: E.activation(out=lb[:], in_=lb[:], func=AF.Sigmoid), reads=['lb'], writes=['lb'])
            b.op('dve', lambda E: E.tensor_scalar(out=oml[:], in0=lb[:], scalar1=-1.0, scalar2=1.0,
                                                  op0=ALU.mult, op1=ALU.add), reads=['lb'], writes=['oml'])
        O = [b.sb('O%d' % i, [128, TS], F32) for i in range(2)]
        V = [b.sb('V%d' % i, [128, NT, 128], BF16) for i in range(2)]
        vT = [b.sb('vT%d' % i, [128, 512], BF16) for i in range(2)]
        Wk = {}
        for nm_ in ['q', 'z', 'f', 'lf', 'k', 'bc', 'eb', 'enb', 'k32']:
            Wk[nm_] = [b.sb('a_%s%d' % (nm_, i), [128, 512], F32) for i in range(2)]
        Qt = [b.sb('Qt%d' % i, [128, 512], BF16) for i in range(2)]
        Kt = [b.sb('Kt%d' % i, [128, 512], BF16) for i in range(2)]
        KbT = [b.sb('KbT%d' % i, [128, 512], BF16) for i in range(2)]
        Kb = [b.sb('Kb%d' % i, [128, 128], BF16) for i in range(2)]
        Am = [b.sb('Am%d' % i, [128, 128], BF16) for i in range(2)]
        Kbz = [b.sb('Kbz%d' % i, [128, 128], BF16) for i in range(2)]
        S32 = [b.sb('S32_%d' % i, [128, 128], F32) for i in range(2)]
        S16 = [[b.sb('S16_%d_%d' % (i, c), [128, 128], BF16) for c in range(4)] for i in range(2)]
        pT1 = b.ps('pT', [128, 128], BF16)
        pA1 = b.ps('pA', [128, 512])
        pO = [b.ps('pO%d' % i, [128, 128]) for i in range(2)]
        pS = [[b.ps('pS%d_%d' % (i, c), [128, 128]) for c in range(2)] for i in range(2)]
        fb = {}
        for nm_ in ['s', 'sq', 'rs', 'g', 'mo']:
            fb[nm_] = b.sb('f_%s' % nm_, [128, 512], F32)
        pNb = pA1
        MS = C.Msend.ap()

        for pr in range(NP_):
            for ci, (t0, n) in enumerate(nat_blks):
                vt = vT[ci % 2]
                load_nat('pool', vt, n, t0, 1, pr, ('vT', ci % 2))
                for tt in range(n // 128):
                    ti = (t0 + tt * 128) // 128
                    x_ = ti % 2
                    b.op('pe', lambda E, vt=vt, tt=tt, x_=x_: E.transpose(
                        out=pT1[:, :], in_=vt[:, tt * 128:(tt + 1) * 128], identity=ident[:, :]),
                        reads=[('vT', ci % 2), 'ident'], writes=[('pT', 0)])
                    b.op('act', lambda E, ti=ti, x_=x_: E.activation(out=V[0][:, ti, :], in_=pT1[:, :], func=AF.Copy),
                         reads=[('pT', 0)], writes=[('V', 0)])
                    b.op('pe', lambda E, ti=ti, x_=x_: E.matmul(pS[0][x_][:, :], lhsT=Jm[:, :], rhs=V[0][:, ti, :],
                                                               start=True, stop=True),
                         reads=[('V', 0), 'Jm'], writes=[('pS', 0, x_)])
                    b.op('dve', lambda E, ti=ti, x_=x_: E.tensor_copy(out=V[1][:, ti, :], in_=pS[0][x_][:, :]),
                         reads=[('pS', 0, x_)], writes=[('V', 1)])
            for dr in range(2):
                b.op('dve', lambda E, dr=dr: E.memset(S32[dr][:], 0.0), writes=[('S32', dr)])
                b.op('dve', lambda E, dr=dr: E.memset(S16[dr][3][:], 0.0), writes=[('S16', dr, 3)])
            def emit_ps(dr, ti, c):
                cs = slice(c * 32, (c + 1) * 32)
                psl = pS[dr][c % 2][:, :]
                if c < 3:
                    b.op('pe', lambda E: E.matmul(psl, lhsT=Kb[dr][cs, :], rhs=V[dr][cs, ti, :], start=True, stop=True),
                         reads=[('Kb', dr), ('V', dr)], writes=[('pS', dr, c % 2)])
                else:
                    b.op('pe', lambda E: E.matmul(psl, lhsT=Kbz[dr][64:128, :], rhs=V[dr][64:128, ti, :],
                                                  start=True, stop=True),
                         reads=[('Kbz', dr), ('V', dr)], writes=[('pS', dr, c % 2)])

            for bi in range(len(nat_blks)):
                n = nat_blks[bi][1]
                for dr in range(2):
                    s = 2 * pr + dr
                    nt0 = sblk[dr][bi][0]
                    q, z, f, lf, k, bc, eb, enb, k32 = [Wk[x][dr] for x in
                                                        ['q', 'z', 'f', 'lf', 'k', 'bc', 'eb', 'enb', 'k32']]
                    K_ = lambda x, dr=dr: (x, dr)
                    load_nat('sp', q, n, nt0, 0, pr, K_('q'))
                    load_nat('sp', z, n, nt0, 3 + dr, pr, K_('z'))
                    if dr == 0:
                        zin, qin = z[:, :n], q[:, :n]
                    else:
                        zin, qin = z[:, :n][:, ::-1], q[:, :n][:, ::-1]
                    b.op('act', lambda E, zin=zin, f=f, n=n: E.activation(out=f[:, :n], in_=zin, func=AF.Exp, scale=-1.0),
                         reads=[K_('z')], writes=[K_('f')])
                    b.op('pool', lambda E, f=f, n=n: E.tensor_scalar(out=f[:, :n], in0=f[:, :n], scalar1=1.0, scalar2=None,
                                                                    op0=ALU.add), reads=[K_('f')], writes=[K_('f')])
                    b.op('dve', lambda E, f=f, n=n: E.reciprocal(out=f[:, :n], in_=f[:, :n]),
                         reads=[K_('f')], writes=[K_('f')])
                    if lbmode:
                        b.op('dve', lambda E, f=f, n=n, s=s: E.tensor_scalar(
                            out=f[:, :n], in0=f[:, :n], scalar1=oml[:, s:s + 1], scalar2=lb[:, s:s + 1],
                            op0=ALU.mult, op1=ALU.add), reads=[K_('f'), 'lb', 'oml'], writes=[K_('f')])
                    b.op('act', lambda E, f=f, lf=lf, n=n: E.activation(out=lf[:, :n], in_=f[:, :n], func=AF.Ln),
                         reads=[K_('f')], writes=[K_('lf')])
                    b.op('pool', lambda E, f=f, k=k, n=n: E.tensor_scalar(
                        out=k[:, :n], in0=f[:, :n], scalar1=-1.0, scalar2=1.0, op0=ALU.mult, op1=ALU.add),
                        reads=[K_('f')], writes=[K_('k')])
                    b.op('dve', lambda E, lf=lf, bc=bc, n=n: E.tensor_tensor_scan(
                        out=bc[:, :n], data0=rmask[:, :n], data1=lf[:, :n], initial=0.0, op0=ALU.mult, op1=ALU.add),
                        reads=[K_('lf'), 'cst'], writes=[K_('bc')])
                    b.op('act', lambda E, bc=bc, eb=eb, n=n: E.activation(out=eb[:, :n], in_=bc[:, :n], func=AF.Exp),
                         reads=[K_('bc')], writes=[K_('eb')])
                    b.op('act', lambda E, bc=bc, enb=enb, n=n: E.activation(out=enb[:, :n], in_=bc[:, :n], func=AF.Exp,
                                                                          scale=-1.0),
                         reads=[K_('bc')], writes=[K_('enb')])
                    b.op('pool', lambda E, qin=qin, eb=eb, dr=dr, n=n: E.tensor_tensor(
                        out=Qt[dr][:, :n], in0=qin, in1=eb[:, :n], op=ALU.mult),
                        reads=[K_('q'), K_('eb')], writes=[K_('Qt')])
                    b.op('dve', lambda E, k=k, enb=enb, k32=k32, n=n: E.tensor_tensor(
                        out=k32[:, :n], in0=k[:, :n], in1=enb[:, :n], op=ALU.mult),
                        reads=[K_('k'), K_('enb')], writes=[K_('k32')])
                    b.op('act', lambda E, k32=k32, dr=dr, n=n: E.activation(out=Kt[dr][:, :n], in_=k32[:, :n], func=AF.Copy),
                         reads=[K_('k32')], writes=[K_('Kt')])
                    for c in range(n // 32):
                        b.op('pool', lambda E, k32=k32, eb=eb, dr=dr, c=c: E.tensor_scalar(
                            out=KbT[dr][:, c * 32:(c + 1) * 32], in0=k32[:, c * 32:(c + 1) * 32],
                            scalar1=eb[:, c * 32 + 31:c * 32 + 32], scalar2=None, op0=ALU.mult),
                            reads=[K_('k32'), K_('eb')], writes=[K_('KbT')])
                for tt in range(n // 128):
                    info = {}
                    for dr in range(2):
                        K_ = lambda x, dr=dr: (x, dr)
                        c0 = tt * 128
                        nt0 = sblk[dr][bi][0]
                        if dr == 0:
                            ti = (nt0 + c0) // 128
                            st0 = nt0 + c0
                        else:
                            ti = (nt0 + n - c0 - 128) // 128
                            st0 = (0 if bi == 0 else SC + (bi - 1) * 512) + c0
                        info[dr] = (ti, st0)
                        b.op('pe', lambda E, dr=dr, c0=c0: E.transpose(out=pT1[:, :], in_=KbT[dr][:, c0:c0 + 128],
                                                                       identity=ident[:, :]),
                             reads=[K_('KbT'), 'ident'], writes=[('pT', 0)])
                        b.op('act', lambda E, dr=dr: E.activation(out=Kb[dr][:, :], in_=pT1[:, :], func=AF.Copy),
                             reads=[('pT', 0)], writes=[K_('Kb')])
                        b.op('pool', lambda E, dr=dr: E.tensor_scalar(
                            out=Kbz[dr][64:128, :], in0=Kb[dr][64:128, :], scalar1=cst_sb[64:128, 768:769], scalar2=None,
                            op0=ALU.mult), reads=[K_('Kb'), 'cst'], writes=[K_('Kbz')])
                        b.op('pe', lambda E, dr=dr, c0=c0: E.matmul(pA1[:, 0:128], lhsT=Kt[dr][:, c0:c0 + 128],
                                                                    rhs=Qt[dr][:, c0:c0 + 128], start=True, stop=True),
                             reads=[K_('Kt'), K_('Qt')], writes=[('pA', 0)])
                        b.op('dve', lambda E, dr=dr: E.tensor_tensor(out=Am[dr][:, :], in0=pA1[:, 0:128], in1=amask,
                                                                     op=ALU.mult),
                             reads=[('pA', 0), 'cst'], writes=[K_('Am')])
                        b.op('pe', lambda E, dr=dr, ti=ti: E.matmul(pO[dr][:, :], lhsT=V[dr][:, ti, :], rhs=Am[dr][:, :],
                                                                    start=True, stop=False),
                             reads=[('V', dr), K_('Am')], writes=[K_('pO')])
                        for c in range(2):
                            emit_ps(dr, ti, c)
                    for dr in range(2):
                        K_ = lambda x, dr=dr: (x, dr)
                        eb = Wk['eb'][dr]
                        c0 = tt * 128
                        ti, st0 = info[dr]
                        for c in range(4):
                            cs = slice(c * 32, (c + 1) * 32)
                            psl = pS[dr][c % 2][:, :]
                            sprev = S16[dr][(c - 1) % 4]
                            b.op('pe', lambda E, dr=dr, c0=c0, c=c, cs=cs, sprev=sprev: E.matmul(
                                pO[dr][:, cs], lhsT=sprev[:, :], rhs=Qt[dr][:, c0 + c * 32:c0 + (c + 1) * 32],
                                start=False, stop=(c == 3)),
                                reads=[('S16', dr, (c - 1) % 4), K_('Qt')], writes=[K_('pO')])
                            b.op('dve', lambda E, dr=dr, eb=eb, c0=c0, c=c, psl=psl: E.scalar_tensor_tensor(
                                out=S32[dr][:, :], in0=S32[dr][:, :], scalar=eb[:, c0 + c * 32 + 31:c0 + c * 32 + 32],
                                in1=psl, op0=ALU.mult, op1=ALU.add),
                                reads=[('S32', dr), K_('eb'), ('pS', dr, c % 2)], writes=[('S32', dr)])
                            b.op('dve', lambda E, dr=dr, c=c: E.tensor_copy(out=S16[dr][c][:, :], in_=S32[dr][:, :]),
                                 reads=[('S32', dr)], writes=[('S16', dr, c)])
                            if c + 2 < 4:
                                emit_ps(dr, ti, c + 2)
                        b.op('act', lambda E, dr=dr, st0=st0: E.activation(
                            out=O[dr][:, st0:st0 + 128], in_=pO[dr][:, :], func=AF.Copy),
                            reads=[K_('pO')], writes=[('O', dr)])
            for (t0, n) in nat_blks:
                lo = 0 if t0 < SC else SC + SL - (t0 - SC) - n
                sb_, sq, rs, g, mo = fb['s'], fb['sq'], fb['rs'], fb['g'], fb['mo']
                load_nat('sp', g, n, t0, 2, pr, 'fg')
                b.op('dve', lambda E, t0=t0, n=n, lo=lo: E.tensor_tensor(
                    out=sb_[:, :n], in0=O[0][:, t0:t0 + n], in1=O[1][:, lo:lo + n][:, ::-1], op=ALU.add),
                    reads=[('O', 0), ('O', 1)], writes=['fs'])
                b.op('act', lambda E, n=n: E.activation(out=sq[:, :n], in_=sb_[:, :n], func=AF.Square),
                     reads=['fs'], writes=['fsq'])
                b.op('pe', lambda E, n=n: E.matmul(pNb[:, :n], lhsT=ones32[:, :], rhs=sq[:, :n], start=True, stop=True),
                     reads=['fsq', 'ones32'], writes=[('pA', 0)])
                b.op('act', lambda E, n=n: E.activation(out=rs[:, :n], in_=pNb[:, :n], func=AF.Sqrt,
                                                        scale=1.0 / 128, bias=epsb[:, 0:1]),
                     reads=[('pA', 0), 'epsb'], writes=['frs'])
                b.op('dve', lambda E, n=n: E.reciprocal(out=rs[:, :n], in_=rs[:, :n]), reads=['frs'], writes=['frs'])
                b.op('act', lambda E, n=n: E.activation(out=g[:, :n], in_=g[:, :n], func=AF.Silu),
                     reads=['fg'], writes=['fg'])
                b.op('dve', lambda E, n=n, pr=pr: E.scalar_tensor_tensor(
                    out=sb_[:, :n], in0=sb_[:, :n], scalar=pv_sb[:, 16 + pr:17 + pr], in1=rs[:, :n],
                    op0=ALU.mult, op1=ALU.mult), reads=['fs', 'frs', 'pv'], writes=['fs'])
                b.op('pool', lambda E, n=n: E.tensor_tensor(out=mo[:, :n], in0=sb_[:, :n], in1=g[:, :n], op=ALU.mult),
                     reads=['fs', 'fg'], writes=['fmo'])
                write_nat(b, MS, pr * 128, t0, n, mo, ['fmo'], TLc, TCc)
            for dest in range(4):
                c_ = (dest * 512 + pr * 128) // CH
                gather_rows(b, C.Msend, C.Mall, c_, c_ + 2, ['Msend'], 'Mall')
        b.flush()


def _dbg_dump(b, C, oT, src_ap, rkey):
    b.dma('sp', oT, src_ap, reads=[rkey], writes=['out'])
    b.flush()


def build_fused():
    nc = bass.Bass("TRN2", target_bir_lowering=False)
    C = Ctx()
    C.TLc = SEQ_ * B_ // NCORES
    C.TCc = CTX_ * B_ // NCORES
    C.Tc = C.TLc + C.TCc
    TS = CTX_ + SEQ_

    def inp(name, shape):
        return nc.dram_tensor(name, shape, F32, kind="ExternalInput").ap()
    xT = inp("xT", [D, C.Tc])
    C.cT = inp("cT", [D, 4])
    C.wm = inp("wm", [DEPTH * D, 3072])
    C.bm = inp("bm", [128, DEPTH * 24])
    C.gains = inp("gains", [128, 9, KC])
    NA, NB_ = (NLAYERS + 1) // 2, NLAYERS // 2
    if DBG_STOP == 'mod':
        NA, NB_ = 0, 0
    awin = [inp("awin%d" % j, [D, 5 * D]) for j in range(NA)]
    awout = [inp("awout%d" % j, [D, D]) for j in range(0 if DBG_STOP else NA)]
    bwin = [inp("bwin%d" % j, [D, 2 * D]) for j in range(NB_)]
    bwout = [inp("bwout%d" % j, [D, D]) for j in range(NB_)]
    NW = 0 if DBG_STOP else NLAYERS
    w1 = [inp("w1_%d" % l, [D, DFF]) for l in range(NW)]
    w2 = [inp("w2_%d" % l, [DFF, D]) for l in range(NW)]
    C.pvA = inp("pvA", [2, 128, 20])
    C.cst = inp("cst", [128, 897])
    C.pvB = inp("pvB", [2, 128, 44])
    C.gwB = inp("gwB", [2, 8, 256, 256])
    oT = nc.dram_tensor("oT", [D, C.TLc], F32, kind="ExternalOutput").ap()
    C.modsend = nc.dram_tensor("modsend", [512, DEPTH * 24], F32)
    C.modall = nc.dram_tensor("modall", [4 * 512, DEPTH * 24], F32)
    C.modmine = nc.dram_tensor("modmine", [4 * 128, DEPTH * 24], F32)
    C.xbuf = nc.dram_tensor("xbuf", [D, C.Tc], F32)
    C.Psend = [nc.dram_tensor("Psend%d" % s_, [D, C.Tc], F32) for s_ in range(5)]
    C.Pall = [nc.dram_tensor("Pall%d" % s_, [4 * D, C.Tc], F32) for s_ in range(5)]
    C.Pmine = [nc.dram_tensor("Pmine%d" % s_, [D, C.Tc], F32) for s_ in range(5)]
    C.Msend = nc.dram_tensor("Msend", [4 * 512, C.Tc], F32)
    C.Mall = nc.dram_tensor("Mall", [4 * D, C.Tc], F32)
    C.Mmine = nc.dram_tensor("Mmine", [D, C.Tc], F32)
    C.Upad = nc.dram_tensor("Upad", [2, 2, 128, TS + 7], F32)
    C.wcache = nc.dram_tensor("wcache", [36, 128, 8192], BF16).ap()
    with ExitStack() as es:
        b = Bld(nc, es)
        C.ones32 = b.sb('ones32', [128, 128], F32)
        b.op('dve', lambda E: E.memset(C.ones32[:], 1.0), writes=['ones32'])
        C.epsb = b.sb('epsb', [128, 1], F32)
        b.op('dve', lambda E: E.memset(C.epsb[:], EPS), writes=['epsb'])
        C.modtab = b.sb('modtab', [128, 2, DEPTH, 96], F32)
        C.gains_sb = b.sb('gains', [128, 9, KC], F32)
        phase_mod(b, C)
        if DBG_STOP == 'mod':
            with ExitStack() as es2:
                b.es = es2
                t_ = b.sb('dbg', [128, 2 * DEPTH * 96], F32)
                b.op('dve', lambda E: E.tensor_copy(out=t_[:], in_=C.modtab[:].rearrange("p a l g -> p (a l g)")),
                     reads=['modtab'], writes=['dbg'])
                _dbg_dump(b, C, oT[0:256, 0:384].rearrange("(a p) n -> p a n", p=128), t_[:].rearrange("p (a n) -> p a n", a=2), 'dbg')
            return nc
        for layer in range(NLAYERS):
            j = layer // 2
            final = (layer == NLAYERS - 1)
            x_in = xT if layer == 0 else C.xbuf.ap()
            if layer % 2 == 0:
                phase_proj(b, C, layer, x_in, awin[j], 5 * D)
                if DBG_STOP == 'proj':
                    _dbg_dump(b, C, oT[:, 0:C.TLc], C.Pall[0].ap()[4096:6144, 0:C.TLc], ('Pall', 0))
                    return nc
                phase_extract(b, [(C.Pall[s_], C.Pmine[s_].ap(), 1, 2048, 2048, C.Tc, 'k', ('Pall', s_), ('Pmine', s_))
                                  for s_ in range(5)])
                if DBG_STOP == 'extract':
                    _dbg_dump(b, C, oT[:, 0:C.TLc], C.Pmine[3].ap()[:, 0:C.TLc], ('Pmine', 3))
                    return nc
                phase_scan_a(b, C, j, 1 if j > 0 else 0)
                if DBG_STOP == 'scan':
                    _dbg_dump(b, C, oT[:, 0:C.TLc], C.Mall.ap()[2048:4096, 0:C.TLc], 'Mall')
                    return nc
                wo = awout[j]
            else:
                phase_proj(b, C, layer, x_in, bwin[j], 2 * D)
                phase_extract(b, [(C.Pall[s_], C.Pmine[s_].ap(), 1, 2048, 2048, C.Tc, 'k', ('Pall', s_), ('Pmine', s_))
                                  for s_ in (0, 1)])
                phase_scan_b(b, C, j)
                wo = bwout[j]
            phase_extract(b, [(C.Mall, C.Mmine.ap(), 1, 2048, 2048, C.Tc, 'k', 'Mall', 'Mmine')])
            phase_l3(b, C, layer, x_in, wo, w1[layer], w2[layer], oT if final else C.xbuf.ap(), final)
    return nc


_NC = {}


def _f32(a):
    return np.ascontiguousarray(a, dtype=np.float32)


def _pl(v):
    return np.asarray(v, dtype=np.float32).reshape(KC, 128).T


def _consts_a():
    ident = np.eye(128, dtype=np.float32)
    s_ = np.arange(128)[:, None]
    t_ = np.arange(128)[None, :]
    am = ((s_ <= t_) & (s_ // 32 == t_ // 32)).astype(np.float32)
    rm = np.ones((128, 512), np.float32)
    rm[:, ::32] = 0
    m96 = (np.arange(128) >= 96).astype(np.float32)[:, None]
    Jm = np.ascontiguousarray(ident[::-1])
    return np.ascontiguousarray(np.concatenate([ident, am, rm, m96, Jm], axis=1))


def make_in_maps(x, c, ctx, c_ctx, w_mod, b_mod, norm1, norm2, a_w_in, a_lb_logits, a_onorm, a_w_out,
                 b_w_in, b_conv_w, b_conv_b, b_gate_w, b_gate_b, b_lambda, b_w_out, mlp_w1, mlp_w2, final_norm):
    TLc = SEQ_ * B_ // NCORES
    TCc = CTX_ * B_ // NCORES
    x = np.asarray(x, np.float32)
    ctx = np.asarray(ctx, np.float32)
    latf = x.reshape(B_ * SEQ_, D)
    ctxf = ctx.reshape(B_ * CTX_, D)
    cT = np.zeros((D, 4), np.float32)
    cT[:, 0] = c[0]
    cT[:, 1] = c[1]
    cT[:, 2] = c_ctx
    gains = np.zeros((128, 9, KC), np.float32)
    for l in range(NLAYERS):
        gains[:, l] = _pl(norm1[l])
        gains[:, 4 + l] = _pl(norm2[l])
    gains[:, 8] = _pl(final_norm)
    cst = _consts_a()
    shared = {"cT": cT, "gains": _f32(gains), "cst": cst}
    for j in range((NLAYERS + 1) // 2):
        shared["awin%d" % j] = _f32(a_w_in[j])
        shared["awout%d" % j] = _f32(a_w_out[j])
    for j in range(NLAYERS // 2):
        shared["bwin%d" % j] = _f32(b_w_in[j])
        shared["bwout%d" % j] = _f32(b_w_out[j])
    for l in range(NLAYERS):
        shared["w1_%d" % l] = _f32(mlp_w1[l])
        shared["w2_%d" % l] = _f32(mlp_w2[l])
    ims = []
    for r in range(NCORES):
        k = r % 4
        m = dict(shared)
        m["xT"] = _f32(np.concatenate([latf[r * TLc:(r + 1) * TLc], ctxf[r * TCc:(r + 1) * TCc]], axis=0).T)
        m["wm"] = _f32(np.concatenate([w_mod[l][:, k * 3072:(k + 1) * 3072] for l in range(DEPTH)], axis=0))
        m["bm"] = _f32(np.concatenate([np.asarray(b_mod[l][k * 3072:(k + 1) * 3072]).reshape(24, 128).T
                                       for l in range(DEPTH)], axis=1))
        pvA = []
        for j in range(2):
            cols = []
            for pi_ in range(4):
                hd = 4 * k + pi_
                sl = slice(hd * 128, (hd + 1) * 128)
                for d_ in range(2):
                    cols += [a_lb_logits[0, d_, sl], a_lb_logits[j, d_, sl]]
            for pi_ in range(4):
                hd = 4 * k + pi_
                cols.append(a_onorm[j][hd * 128:(hd + 1) * 128])
            pvA.append(np.stack(cols, axis=1))
        m["pvA"] = _f32(np.stack(pvA))
        pvB, gwB = [], []
        for j in range(2):
            cols, gws = [], []
            for u in range(2):
                blk = 2 * k + u
                sl = slice(blk * 256, (blk + 1) * 256)

                def h2(v_):
                    return np.asarray(v_[sl], np.float32).reshape(2, 128)
                cols += [h2(b_conv_w[j][jj])[cc] for cc in range(2) for jj in range(4)] + \
                        [h2(b_conv_b[j])[cc] for cc in range(2)] + \
                        [h2(b_gate_b[j][d_, gi_])[cc] for d_ in range(2) for gi_ in range(2) for cc in range(2)] + \
                        [h2(b_lambda[j][d_])[cc] for d_ in range(2) for cc in range(2)]
                gws += [b_gate_w[j][d_, gi_, blk] for d_ in range(2) for gi_ in range(2)]
            pvB.append(np.stack(cols, axis=1))
            gwB.append(np.stack(gws))
        m["pvB"] = _f32(np.stack(pvB))
        m["gwB"] = _f32(np.stack(gwB))
        ims.append(m)
    return ims


def kernel(**inputs):
    if 'nc' not in _NC:
        _NC['nc'] = build_fused()
    nc = _NC['nc']
    ims = make_in_maps(**inputs)
    res = run_bass_kernel_spmd(nc, ims, core_ids=list(range(NCORES))).results
    out = np.concatenate([res[r]["oT"].T for r in range(NCORES)], axis=0).reshape(B_, SEQ_, D)
    return np.ascontiguousarray(out, dtype=np.float32)
```

```python
import os
from contextlib import ExitStack
import numpy as np
import concourse.bass as bass
import concourse.mybir as mybir
from concourse.bass import ds
from concourse.bass_utils import run_bass_kernel_spmd

F32 = mybir.dt.float32
BF16 = mybir.dt.bfloat16
AF = mybir.ActivationFunctionType
ALU = mybir.AluOpType

D = 2048
KC = 16
DFF = 8192
EPS = 1e-6
NCORES = 8
B_, SEQ_, CTX_ = 2, 8192, 256
DEPTH = 4
NLAYERS = int(os.environ.get('KDBG_NLAYERS', '4'))
DBG_MOD = ''
DBG_STOP = os.environ.get('KDBG_STOP') or None
GROUPS4 = [[0, 1, 2, 3], [4, 5, 6, 7]]


class Bld:
    ENG = ['pe', 'act', 'dve', 'pool', 'sp']
    NDS = 8

    def __init__(self, nc, es):
        self.nc = nc
        self.es = es
        self.q = {e: [] for e in self.ENG}
        self.cnt = {e: 0 for e in self.ENG + ['cc']}
        self.sem = {e: es.enter_context(nc.semaphore('s_' + e)) for e in ['pe', 'act', 'dve', 'pool', 'cc']}
        self.dsem = {e: [es.enter_context(nc.semaphore('d_%s%d' % (e, i))) for i in range(self.NDS)]
                     for e in ['sp', 'pool', 'act']}
        self.dcnt = {e: 0 for e in ['sp', 'pool', 'act']}
        self.lastw = {}
        self.readers = {}
        self.waited = {e: {} for e in self.ENG}
        self.uid = 0

    def sb(self, name, shape, dtype):
        self.uid += 1
        return self.es.enter_context(self.nc.sbuf_tensor('sb%d_%s' % (self.uid, name), shape, dtype))

    def ps(self, name, shape, dtype=F32):
        self.uid += 1
        return self.es.enter_context(self.nc.psum_tensor('ps%d_%s' % (self.uid, name), shape, dtype))

    def _wait(self, eng, tok):
        kind, e2, n = tok
        if kind == 'c':
            if e2 == eng and eng == 'pe':
                return
            sem = self.sem[e2]
            val = n
            key = ('c', e2)
        else:
            sem = self.dsem[e2][n % self.NDS]
            val = 16 * (n // self.NDS + 1)
            key = ('d', e2, n % self.NDS)
        if self.waited[eng].get(key, 0) >= val:
            return
        self.waited[eng][key] = val
        self.q[eng].append(lambda E, sem=sem, val=val: E.wait_ge(sem, val))

    def _deps(self, eng, reads, writes):
        toks = set()
        for k in reads:
            if k in self.lastw:
                toks.add(self.lastw[k])
        for k in writes:
            if k in self.lastw:
                toks.add(self.lastw[k])
            for t in self.readers.get(k, {}).values():
                if t[0] == 'c' and t[1] == eng:
                    continue
                toks.add(t)
        for t in toks:
            self._wait(eng, t)

    def _commit(self, tok, reads, writes):
        for k in writes:
            self.lastw[k] = tok
            self.readers[k] = {}
        for k in reads:
            r = self.readers.setdefault(k, {})
            if tok[0] == 'c':
                r[('c', tok[1])] = tok
            else:
                r[tok] = tok

    def op(self, eng, fn, reads=(), writes=()):
        self._deps(eng, reads, writes)
        self.cnt[eng] += 1
        n = self.cnt[eng]
        sem = self.sem[eng]
        self.q[eng].append(lambda E, fn=fn, sem=sem: fn(E).then_inc(sem, 1))
        self._commit(('c', eng, n), reads, writes)

    def dma(self, qe, out, in_, reads=(), writes=()):
        i = self.dcnt[qe]
        self.dcnt[qe] += 1
        if i >= self.NDS:
            self._wait(qe, ('d', qe, i - self.NDS))
        self._deps(qe, reads, writes)
        sem = self.dsem[qe][i % self.NDS]

        def f(E, out=out, in_=in_, sem=sem):
            o = out(E) if callable(out) else out
            s = in_(E) if callable(in_) else in_
            try:
                E.dma_start(out=o, in_=s).then_inc(sem, 16)
            except Exception:
                print("DMA build failed: out=", o, " in=", s)
                raise
        self.q[qe].append(f)
        self._commit(('d', qe, i), reads, writes)

    def cc(self, kind, groups, in_ap, out_ap, reads=(), writes=()):
        self._deps('pool', reads, writes)
        self.cnt['cc'] += 1
        n = self.cnt['cc']
        sem = self.sem['cc']
        self.q['pool'].append(lambda E: E.collective_compute(
            kind, ALU.bypass, replica_groups=groups, ins=[in_ap], outs=[out_ap]).then_inc(sem, 1))
        self._commit(('c', 'cc', n), reads, writes)

    def barrier(self):
        for e in self.ENG:
            for e2 in ['pe', 'act', 'dve', 'pool', 'cc']:
                if self.cnt[e2] > 0:
                    self._wait(e, ('c', e2, self.cnt[e2]))
            for qe in ['sp', 'pool', 'act']:
                for i in range(max(0, self.dcnt[qe] - self.NDS), self.dcnt[qe]):
                    self._wait(e, ('d', qe, i))

    def flush(self):
        self.barrier()
        _PID.clear()
        q = self.q
        with self.nc.Block() as block:
            @block.tensor
            def _(E):
                for f in q['pe']:
                    f(E)

            @block.scalar
            def _(E):
                for f in q['act']:
                    f(E)

            @block.vector
            def _(E):
                for f in q['dve']:
                    f(E)

            @block.gpsimd
            def _(E):
                for f in q['pool']:
                    f(E)

            @block.sync
            def _(E):
                for f in q['sp']:
                    f(E)
        self.q = {e: [] for e in self.ENG}
        for a in ('wbuf',):
            if hasattr(self, a):
                delattr(self, a)


_PID = {}
_XQ = {'i': 0}


def _pid(E):
    key = id(E)
    if key not in _PID:
        _PID[key] = {'p': E.partition_id()}
    return _PID[key]


def phase_extract(b, items):
    qe = ['sp', 'act'][_XQ['i'] % 2]
    _XQ['i'] += 1
    for (gat, mine, nsrc, rps, rm, T, which, kin, kout) in items:
        def src(E, gat=gat, nsrc=nsrc, rps=rps, rm=rm, T=T, which=which):
            d = _pid(E)
            bk = (which, rm * T)
            if bk not in d:
                idv = (d['p'] % 4) if which == 'k' else (d['p'] // 4)
                d[bk] = E.compute_val(idv * (rm * T))
            return bass.AP(tensor=gat, offset=d[bk], ap=[[rps * T, nsrc], [T, rm], [1, T]])
        b.dma(qe, mine.rearrange("(i r) t -> i r t", i=nsrc), src, reads=[kin], writes=[kout])
    b.flush()


CH = 64


def gather_rows(b, send, gall, c0, c1, rkeys, wkey):
    for c in range(c0, c1):
        b.cc("AllGather", GROUPS4, send.ap()[c * CH:(c + 1) * CH, :], gall.ap()[c * 4 * CH:(c + 1) * 4 * CH, :],
             reads=rkeys, writes=[wkey])


def mine_rows(ap2d, i, lo, half):
    r0 = ((lo // CH + half) * 4 + i) * CH
    return ap2d[r0:r0 + CH, :]


def write_nat(b, MS, f0, t0, n, src, rkeys, TLc, TCc):
    if t0 < CTX_:
        for i in range(4):
            b.dma('sp', MS[i * 512 + f0:i * 512 + f0 + 128, TLc:TLc + TCc], src[:, i * TCc:(i + 1) * TCc],
                  reads=rkeys, writes=['Msend'])
    else:
        a_ = t0 - CTX_
        i, loc = a_ // TLc, a_ % TLc
        b.dma('sp', MS[i * 512 + f0:i * 512 + f0 + 128, loc:loc + n], src[:, :n], reads=rkeys, writes=['Msend'])


def blocks_for(T, TL):
    bl = []
    t = 0
    while t < TL:
        n = min(512, TL - t)
        bl.append((t, n, 0))
        t += n
    while t < T:
        n = min(512, T - t)
        bl.append((t, n, 1))
        t += n
    return bl


class NormMod:
    def __init__(self, b, tag, ones32, epsb, gain, sc, sh, pkeys):
        self.b = b
        self.tag = tag
        self.ones32 = ones32
        self.epsb = epsb
        self.gain = gain
        self.sh = sh
        self.pkeys = list(pkeys)
        self.sq = b.sb('sq_' + tag, [128, KC, 512], F32)
        self.rs = b.sb('rs_' + tag, [128, 512], F32)
        self.pss = b.ps('pss_' + tag, [128, 512])
        self.gm = b.sb('gm_' + tag, [128, 2, KC], F32)
        for g in range(2):
            b.op('dve', lambda E, g=g: E.scalar_tensor_tensor(
                out=self.gm[:, g, :], in0=sc[g], scalar=1.0, in1=gain, op0=ALU.add, op1=ALU.mult),
                reads=self.pkeys, writes=['gm_' + tag])

    def emit(self, xs, xkey, n, g, hout, hkey, plain=False):
        b = self.b
        tag = self.tag
        sq, rs, pss = self.sq, self.rs, self.pss
        tmp = sq
        b.op('act', lambda E: E.activation(out=sq[:, :, :n], in_=xs, func=AF.Square),
             reads=[xkey], writes=[('sq_' + tag, kc) for kc in range(KC)])
        for kc in range(KC):
            b.op('pe', lambda E, kc=kc: E.matmul(pss[:, :n], lhsT=self.ones32[:, :], rhs=sq[:, kc, :n],
                                                 start=(kc == 0), stop=(kc == KC - 1)),
                 reads=[('sq_' + tag, kc), 'ones32'], writes=['pss_' + tag])
        b.op('act', lambda E: E.activation(out=rs[:, :n], in_=pss[:, :n], func=AF.Sqrt,
                                           scale=1.0 / D, bias=self.epsb[:, 0:1]),
             reads=['pss_' + tag, 'epsb'], writes=['rs_' + tag])
        b.op('dve', lambda E: E.reciprocal(out=rs[:, :n], in_=rs[:, :n]),
             reads=['rs_' + tag], writes=['rs_' + tag])
        for kc in range(KC):
            if plain:
                gcol = self.gain[:, kc:kc + 1]
                b.op('dve', lambda E, kc=kc, gcol=gcol: E.scalar_tensor_tensor(
                    out=hout[:, kc, :], in0=xs[:, kc, :], scalar=gcol, in1=rs[:, :n],
                    op0=ALU.mult, op1=ALU.mult),
                    reads=[xkey, 'rs_' + tag] + self.pkeys, writes=[hkey])
                continue
            gcol = self.gm[:, g, kc:kc + 1]
            shcol = self.sh[g][:, kc:kc + 1]
            b.op('dve', lambda E, kc=kc, gcol=gcol: E.scalar_tensor_tensor(
                out=tmp[:, kc, :n], in0=xs[:, kc, :], scalar=gcol, in1=rs[:, :n],
                op0=ALU.mult, op1=ALU.mult),
                reads=[xkey, 'rs_' + tag, 'gm_' + tag], writes=[('sq_' + tag, kc)])
            b.op('act', lambda E, kc=kc, shcol=shcol: E.activation(
                out=hout[:, kc, :], in_=tmp[:, kc, :n], func=AF.Identity, bias=shcol, scale=1.0),
                reads=[('sq_' + tag, kc)] + self.pkeys, writes=[hkey])


def gemm_stream(b, w_dram, K, N, ngrp, rhs_fn, rhs_keys, tblocks, epi, psums, after_group=None, cache=None):
    kcs = K // 128
    if not hasattr(b, 'wbuf'):
        b.wbuf = [b.sb('wbuf%d' % i, [128, 8192], BF16) for i in range(2)]
        b.wi = 0
        b.pi = 0
    wv = w_dram.rearrange("(kc p) n -> p kc n", p=128)
    for gi in range(N // ngrp):
        wflat = b.wbuf[b.wi % 2]
        wt = wflat[:, :kcs * ngrp].rearrange("p (k n) -> p k n", k=kcs)
        wk = ('wbuf', b.wi % 2)
        b.wi += 1
        if cache is not None and cache[2] == 'use':
            ck = ('wcache', cache[1] + gi)
            b.dma('sp', wflat[:, :], cache[0][cache[1] + gi], reads=[ck], writes=[wk])
        else:
            b.dma('pool', wt[:], wv[:, :, gi * ngrp:(gi + 1) * ngrp], writes=[wk])
            if cache is not None:
                ck = ('wcache', cache[1] + gi)
                b.dma('sp', cache[0][cache[1] + gi], wflat[:, :], reads=[wk], writes=[ck])
        for (t0, n, g) in tblocks:
            for c in range(ngrp // 128):
                ps, pk = psums[b.pi % len(psums)]
                b.pi += 1
                for kc in range(kcs):
                    b.op('pe', lambda E, kc=kc, c=c, ps=ps, wt=wt, t0=t0, n=n: E.matmul(
                        ps[:, :n], lhsT=wt[:, kc, c * 128:(c + 1) * 128], rhs=rhs_fn(kc, t0, n),
                        start=(kc == 0), stop=(kc == kcs - 1)),
                        reads=[wk] + rhs_keys, writes=[pk])
                epi(gi * (ngrp // 128) + c, t0, n, g, ps, pk)
        if after_group is not None:
            after_group(gi)


class Ctx:
    pass


def phase_mod(b, C):
    NCH = 24
    with ExitStack() as es:
        b.es = es
        c_sb = b.sb('c', [128, KC, 4], F32)
        s_sb = b.sb('s', [128, KC, 4], F32)
        bm_sb = b.sb('bm', [128, DEPTH * NCH], F32)
        o_sb = b.sb('o', [128, 4, DEPTH * NCH], F32)
        b.dma('sp', c_sb[:], C.cT.rearrange("(kc p) n -> p kc n", p=128), writes=['c'])
        b.dma('sp', bm_sb[:], C.bm, writes=['bm'])
        b.dma('sp', C.gains_sb[:], C.gains, writes=['gains'])
        b.op('act', lambda E: E.activation(out=s_sb[:], in_=c_sb[:], func=AF.Silu), reads=['c'], writes=['s'])
        wb = [b.sb('wm%d' % i, [128, KC, 512], F32) for i in range(2)]
        pss = [b.ps('pm%d' % i, [128, 4]) for i in range(4)]
        wv = C.wm.rearrange("(l kc p) n -> p l kc n", p=128, kc=KC)
        gi = 0
        for l in range(DEPTH):
            for gq in range(NCH // 4):
                wt = wb[gi % 2]
                wk = ('wm', gi % 2)
                gi += 1
                b.dma('sp', wt[:], wv[:, l, :, gq * 512:(gq + 1) * 512], writes=[wk])
                for c in range(4):
                    j = l * NCH + gq * 4 + c
                    ps = pss[j % 4]
                    for kc in range(KC):
                        b.op('pe', lambda E, kc=kc, c=c, wt=wt, ps=ps: E.matmul(
                            ps[:, :], lhsT=wt[:, kc, c * 128:(c + 1) * 128], rhs=s_sb[:, kc, :],
                            start=(kc == 0), stop=(kc == KC - 1)), reads=[wk, 's'], writes=[('pm', j % 4)])
                    b.op('dve', lambda E, j=j, ps=ps: E.tensor_scalar(
                        out=o_sb[:, :, j], in0=ps[:, :], scalar1=bm_sb[:, j:j + 1], scalar2=None, op0=ALU.add),
                        reads=[('pm', j % 4), 'bm'], writes=['o'])
        b.dma('sp', C.modsend.ap().rearrange("(q p) j -> p q j", p=128), o_sb[:], reads=['o'], writes=['modsend'])
        b.cc("AllGather", GROUPS4, C.modsend.ap().opt(), C.modall.ap().opt(), reads=['modsend'], writes=['modall'])
        b.flush()
    phase_extract(b, [(C.modall, C.modmine.ap(), 4, 512, 128, DEPTH * NCH, 'b', 'modall', 'modmine')])
    MA = C.modall.ap()
    MM = C.modmine.ap()
    for l in range(DEPTH):
        for i in range(4):
            b.dma('sp', C.modtab[:, 0, l, i * NCH:(i + 1) * NCH], MM[i * 128:(i + 1) * 128, l * NCH:(l + 1) * NCH],
                  reads=['modmine'], writes=['modtab'])
            b.dma('sp', C.modtab[:, 1, l, i * NCH:(i + 1) * NCH], MA[i * 512 + 256:i * 512 + 384, l * NCH:(l + 1) * NCH],
                  reads=['modall'], writes=['modtab'])
    b.flush()


def mod_ap(C, st, layer, m):
    return C.modtab[:, st, layer, m * 16:(m + 1) * 16]


def phase_proj(b, C, layer, x_dram, w, NP):
    T, TL = C.Tc, C.TLc
    tbl = blocks_for(T, TL)
    with ExitStack() as es:
        b.es = es
        nm = NormMod(b, 'n1', C.ones32, C.epsb, C.gains_sb[:, layer, :],
                     [mod_ap(C, 0, layer, 1), mod_ap(C, 1, layer, 1)],
                     [mod_ap(C, 0, layer, 0), mod_ap(C, 1, layer, 0)], ['modtab', 'gains'])
        h = b.sb('h', [128, KC, T], BF16)
        xb = [b.sb('xb%d' % i, [128, KC, 512], F32) for i in range(1)]
        xv = x_dram.rearrange("(kc p) t -> p kc t", p=128)
        for bi, (t0, n, g) in enumerate(tbl):
            xs = xb[0]
            b.dma('sp', xs[:, :, :n], xv[:, :, t0:t0 + n], reads=['xdram'], writes=[('xb', 0)])
            nm.emit(xs[:, :, :n], ('xb', 0), n, g, h[:, :, t0:t0 + n], 'h')
        psums = [(b.ps('pp%d' % i, [128, 512]), ('pp', i)) for i in range(4)]
        ob = [b.sb('ob%d' % i, [128, 512], F32) for i in range(4)]
        pvs = [C.Psend[s_].ap().rearrange("(c p) t -> p c t", p=128) for s_ in range(5)]
        st = {'i': 0}

        def epi(nch, t0, n, g, ps, pk):
            i = st['i'] % 4
            st['i'] += 1
            o = ob[i]
            if i % 2:
                b.op('act', lambda E: E.activation(out=o[:, :n], in_=ps[:, :n], func=AF.Copy),
                     reads=[pk], writes=[('ob', i)])
            else:
                b.op('dve', lambda E: E.tensor_copy(out=o[:, :n], in_=ps[:, :n]),
                     reads=[pk], writes=[('ob', i)])
            b.dma('sp', pvs[nch // 16][:, nch % 16, t0:t0 + n], o[:, :n], reads=[('ob', i)],
                  writes=[('Psend', nch // 16)])

        def after_group(gi):
            if gi % 4 == 3:
                s_ = gi // 4
                gather_rows(b, C.Psend[s_], C.Pall[s_], 0, 2048 // CH, [('Psend', s_)], ('Pall', s_))

        gemm_stream(b, w, D, NP, 512, lambda kc, t0, n: h[:, kc, t0:t0 + n], ['h'], tbl, epi, psums, after_group)
        b.flush()


def phase_l3(b, C, layer, x_dram, wo, w1, w2, out_dram, final):
    T, TL = (C.TLc, C.TLc) if final else (C.Tc, C.TLc)
    tbl = blocks_for(T, TL)
    with ExitStack() as es:
        b.es = es
        nm = NormMod(b, 'n2', C.ones32, C.epsb, C.gains_sb[:, 4 + layer, :],
                     [mod_ap(C, 0, layer, 4), mod_ap(C, 1, layer, 4)],
                     [mod_ap(C, 0, layer, 3), mod_ap(C, 1, layer, 3)], ['modtab', 'gains'])
        if final:
            nf = NormMod.__new__(NormMod)
            nf.__dict__.update(nm.__dict__)
            nf.gain = C.gains_sb[:, 8, :]
        gate = {2: [mod_ap(C, 0, layer, 2), mod_ap(C, 1, layer, 2)],
                5: [mod_ap(C, 0, layer, 5), mod_ap(C, 1, layer, 5)]}
        xs = b.sb('xs', [128, KC, 512], F32)
        hb = b.sb('hb', [128, KC, 512], BF16)
        hid = b.sb('hid', [128, 32, 512], BF16)
        r32 = [b.sb('r32_%d' % i, [128, 512], F32) for i in range(2)]
        psums = [(b.ps('pp%d' % i, [128, 512]), ('pp', i)) for i in range(4)]
        xv = x_dram.rearrange("(kc p) t -> p kc t", p=128)
        mv = C.Mmine.ap().rearrange("(fc hf kk r) t -> hf r kk fc t", fc=4, hf=2, kk=4, r=CH)
        ov = out_dram.rearrange("(kc p) t -> p kc t", p=128)
        for bix, (t0, n, g) in enumerate(tbl):
            cm_ = 'fill' if bix == 0 else 'use'
            b.dma('sp', xs[:, :, :n], xv[:, :, t0:t0 + n], reads=['xdram'], writes=['xs'])
            for half in range(2):
                for kk in range(4):
                    b.dma('pool', hb[half * CH:(half + 1) * CH, kk * 4:(kk + 1) * 4, :n],
                          mv[half][:, kk, :, t0:t0 + n], reads=['Mmine'], writes=['hb'])

            def epi_res(m):
                def epi(nch, t0_, n_, g_, ps, pk):
                    gcol = gate[m][g_][:, nch:nch + 1]
                    b.op('dve', lambda E: E.scalar_tensor_tensor(
                        out=xs[:, nch, :n_], in0=ps[:, :n_], scalar=gcol, in1=xs[:, nch, :n_],
                        op0=ALU.mult, op1=ALU.add), reads=[pk, 'modtab', 'xs'], writes=['xs'])
                return epi
            blk = [(0, n, g)]
            gemm_stream(b, wo, D, D, 512, lambda kc, t0_, n_: hb[:, kc, :n_], ['hb'], blk, epi_res(2), psums,
                        cache=(C.wcache, 0, cm_))
            nm.emit(xs[:, :, :n], 'xs', n, g, hb[:, :, :n], 'hb')
            for half in range(2):
                def epi_h(nch, t0_, n_, g_, ps, pk):
                    ri = b.pi % 2
                    r = r32[ri]
                    b.op('act', lambda E: E.activation(out=r[:, :n_], in_=ps[:, :n_], func=AF.Relu),
                         reads=[pk], writes=[('r32', ri)])
                    b.op('pool', lambda E: E.tensor_tensor(out=hid[:, nch, :n_], in0=r[:, :n_], in1=r[:, :n_],
                                                           op=ALU.mult),
                         reads=[('r32', ri)], writes=[('hid', nch)])
                gemm_stream(b, w1[:, half * 4096:(half + 1) * 4096], D, 4096, 512,
                            lambda kc, t0_, n_: hb[:, kc, :n_], ['hb'], blk, epi_h, psums,
                            cache=(C.wcache, 4 + half * 8, cm_))
                gemm_stream(b, w2[half * 4096:(half + 1) * 4096, :], 4096, D, 256,
                            lambda kc, t0_, n_: hid[:, kc, :n_], [('hid', i) for i in range(32)], blk,
                            epi_res(5), psums, cache=(C.wcache, 20 + half * 8, cm_))
            if final:
                nf.emit(xs[:, :, :n], 'xs', n, g, xs[:, :, :n], 'xs', plain=True)
            b.dma('sp', ov[:, :, t0:t0 + n], xs[:, :, :n], reads=['xs'], writes=['xdram'])
        b.flush()


def phase_scan_b(b, C, j):
    SC, SL = CTX_, SEQ_
    TS = SC + SL
    LP = TS + 6
    NU = 2
    blks = [(0, SC, 0)]
    t = 0
    while t < SL:
        blks.append((SC + 3 + t, 512, SC + t))
        t += 512
    nctx = 1
    PAs = [C.Pmine[s_].ap() for s_ in range(2)]
    TLc, TCc = C.TLc, C.TCc


    with ExitStack() as es:
        b.es = es
        pv_sb = b.sb('pv', [128, NU * 22], F32)
        b.dma('sp', pv_sb[:], C.pvB[j], writes=['pv'])
        one = b.sb('one', [128, 1], F32)
        b.op('dve', lambda E: E.memset(one[:], 1.0), writes=['one'])
        zt = b.sb('zt', [128, 4], F32)
        b.op('dve', lambda E: E.memset(zt[:], 0.0), writes=['zt'])
        cdec = b.sb('cdec', [128, NU * 4], F32)
        for u in range(NU):
            cs = slice(u * 4, u * 4 + 4)
            b.op('act', lambda E, u=u, cs=cs: E.activation(out=cdec[:, cs], in_=pv_sb[:, u * 22 + 18:u * 22 + 22],
                                                          func=AF.Exp, scale=-1.0), reads=['pv'], writes=['cdec'])
        b.op('act', lambda E: E.activation(out=cdec[:], in_=cdec[:], func=AF.Ln, bias=one[:, 0:1], scale=1.0),
             reads=['cdec', 'one'], writes=['cdec'])
        b.op('dve', lambda E: E.tensor_scalar(out=cdec[:], in0=cdec[:], scalar1=-8.0, scalar2=None, op0=ALU.mult),
             reads=['cdec'], writes=['cdec'])
        gwb = b.sb('gwb', [128, NU * 4, 2, 256], BF16)
        b.dma('pool', gwb[:], C.gwB[j].rearrange("g (kc p) n -> p g kc n", p=128), writes=['gwb'])
        UP = C.Upad.ap()
        for u in range(NU):
            for cc in range(2):
                for (c0, w_) in ((0, 2), (SC + 2, 3), (LP - 1, 2)):
                    b.dma('sp', UP[u, cc, :, c0:c0 + w_], zt[:, :w_], reads=['zt'], writes=['Upad'])
                for i in range(4):
                    for half in range(2):
                        ps_ = slice(half * CH, (half + 1) * CH)
                        src_ = mine_rows(PAs[1], i, u * 256 + cc * 128, half)
                        b.dma('sp', UP[u, cc, ps_, 2 + i * TCc:2 + (i + 1) * TCc], src_[:, TLc:TLc + TCc],
                              reads=[('Pmine', 1)], writes=['Upad'])
                        b.dma('sp', UP[u, cc, ps_, SC + 5 + i * TLc:SC + 5 + (i + 1) * TLc], src_[:, 0:TLc],
                              reads=[('Pmine', 1)], writes=['Upad'])
        uc32 = b.sb('uc32', [128, 2, TS], F32)
        uc16 = b.sb('uc16', [128, 2, TS], BF16)
        Hf = b.sb('Hf', [128, TS], F32)
        ub = [b.sb('ub%d' % i, [128, 2, 515], F32) for i in range(2)]
        W = {}
        for nm_ in ['r', 'i', 'a', 'a2', 'xin', 'y', 's', 'mo']:
            W[nm_] = [b.sb('w_%s%d' % (nm_, i), [128, 512], F32) for i in range(2)]
        hbr = [b.sb('hbr%d' % i, [128, 512], F32) for i in range(2)]
        pz = [b.ps('pz%d' % i, [128, 512]) for i in range(4)]
        cnt = {'k': 0}
        MS = C.Msend.ap()
        for u in range(NU):
            pb = u * 22
            for bi, (p0, n, t0) in enumerate(blks):
                ub_ = ub[bi % 2]
                uk = ('ub', bi % 2)
                for cc in range(2):
                    b.dma('sp', ub_[:, cc, :n + 3], UP[u, cc, :, p0:p0 + n + 3], reads=['Upad'], writes=[uk])
                for cc in range(2):
                    dst = uc32[:, cc, t0:t0 + n]
                    b.op('dve', lambda E, ub_=ub_, cc=cc, n=n, dst=dst, pb=pb: E.tensor_scalar(
                        out=dst, in0=ub_[:, cc, 0:n], scalar1=pv_sb[:, pb + cc * 4:pb + cc * 4 + 1],
                        scalar2=pv_sb[:, pb + 8 + cc:pb + 9 + cc], op0=ALU.mult, op1=ALU.add),
                        reads=[uk, 'pv'], writes=[('uc32', cc)])
                    for jj in range(1, 4):
                        b.op('dve', lambda E, ub_=ub_, cc=cc, n=n, jj=jj, dst=dst, pb=pb: E.scalar_tensor_tensor(
                            out=dst, in0=ub_[:, cc, jj:jj + n], scalar=pv_sb[:, pb + cc * 4 + jj:pb + cc * 4 + jj + 1],
                            in1=dst, op0=ALU.mult, op1=ALU.add), reads=[uk, 'pv', ('uc32', cc)], writes=[('uc32', cc)])
                    b.op('act', lambda E, cc=cc, t0=t0, n=n, dst=dst: E.activation(
                        out=uc16[:, cc, t0:t0 + n], in_=dst, func=AF.Copy),
                        reads=[('uc32', cc)], writes=[('uc16', cc)])
            for oc in range(2):
                for d in range(2):
                    if d == 0:
                        order = list(range(len(blks)))
                    else:
                        order = list(range(nctx - 1, -1, -1)) + list(range(len(blks) - 1, nctx - 1, -1))
                    prev = None
                    for bi in order:
                        p0, n, t0 = blks[bi]
                        k = cnt['k'] % 2
                        cnt['k'] += 1
                        zr, zi = pz[2 * k], pz[2 * k + 1]
                        for gi_, zp in ((0, zr), (1, zi)):
                            for kc in range(2):
                                b.op('pe', lambda E, zp=zp, gi_=gi_, kc=kc, t0=t0, n=n, d=d, oc=oc, u=u: E.matmul(
                                    zp[:, :n], lhsT=gwb[:, u * 4 + d * 2 + gi_, kc, oc * 128:(oc + 1) * 128],
                                    rhs=uc16[:, kc, t0:t0 + n], start=(kc == 0), stop=(kc == 1)),
                                    reads=['gwb', ('uc16', 0), ('uc16', 1)], writes=[('pz', 2 * k + gi_)])
                        r, ig, a, a2, xin = W['r'][k], W['i'][k], W['a'][k], W['a2'][k], W['xin'][k]
                        c_r = pb + 10 + (d * 2 + 0) * 2 + oc
                        c_i = pb + 10 + (d * 2 + 1) * 2 + oc
                        br = pv_sb[:, c_r:c_r + 1]
                        bi_ = pv_sb[:, c_i:c_i + 1]
                        cd = cdec[:, u * 4 + d * 2 + oc: u * 4 + d * 2 + oc + 1]
                        b.op('act', lambda E, r=r, zr=zr, n=n, br=br: E.activation(
                            out=r[:, :n], in_=zr[:, :n], func=AF.Sigmoid, bias=br, scale=1.0),
                            reads=[('pz', 2 * k), 'pv'], writes=[('r', k)])
                        b.op('act', lambda E, ig=ig, zi=zi, n=n, bi_=bi_: E.activation(
                            out=ig[:, :n], in_=zi[:, :n], func=AF.Sigmoid, bias=bi_, scale=1.0),
                            reads=[('pz', 2 * k + 1), 'pv'], writes=[('i', k)])
                        b.op('act', lambda E, a=a, r=r, n=n, cd=cd: E.activation(
                            out=a[:, :n], in_=r[:, :n], func=AF.Exp, scale=cd),
                            reads=[('r', k), 'cdec'], writes=[('a', k)])
                        b.op('pool', lambda E, a=a, a2=a2, n=n: E.tensor_tensor(
                            out=a2[:, :n], in0=a[:, :n], in1=a[:, :n], op=ALU.mult),
                            reads=[('a', k)], writes=[('a2', k)])
                        b.op('act', lambda E, a2=a2, n=n: E.activation(
                            out=a2[:, :n], in_=a2[:, :n], func=AF.Sqrt, scale=-1.0, bias=one[:, 0:1]),
                            reads=[('a2', k), 'one'], writes=[('a2', k)])
                        b.op('pool', lambda E, a2=a2, ig=ig, xin=xin, n=n: E.tensor_tensor(
                            out=xin[:, :n], in0=a2[:, :n], in1=ig[:, :n], op=ALU.mult),
                            reads=[('a2', k), ('i', k)], writes=[('xin', k)])
                        b.op('dve', lambda E, xin=xin, n=n, t0=t0, oc=oc: E.tensor_tensor(
                            out=xin[:, :n], in0=xin[:, :n], in1=uc32[:, oc, t0:t0 + n], op=ALU.mult),
                            reads=[('xin', k), ('uc32', oc)], writes=[('xin', k)])
                        if d == 0:
                            init = 0.0 if prev is None else Hf[:, t0 - 1:t0]
                            b.op('dve', lambda E, a=a, xin=xin, n=n, t0=t0, init=init: E.tensor_tensor_scan(
                                out=Hf[:, t0:t0 + n], data0=a[:, :n], data1=xin[:, :n], initial=init,
                                op0=ALU.mult, op1=ALU.add), reads=[('a', k), ('xin', k), 'Hf'], writes=['Hf'])
                            prev = bi
                        else:
                            hb_ = hbr[k]
                            if prev is None:
                                init = 0.0
                                rkeys = []
                            else:
                                pk_, pn_ = prev
                                init = hbr[pk_][:, pn_ - 1:pn_]
                                rkeys = [('hbr', pk_)]
                            b.op('dve', lambda E, a=a, xin=xin, n=n, hb_=hb_, init=init: E.tensor_tensor_scan(
                                out=hb_[:, :n], data0=a[:, :n][:, ::-1], data1=xin[:, :n][:, ::-1], initial=init,
                                op0=ALU.mult, op1=ALU.add), reads=[('a', k), ('xin', k)] + rkeys, writes=[('hbr', k)])
                            prev = (k, n)
                            yb, sb_, mo_ = W['y'][k], W['s'][k], W['mo'][k]
                            for half in range(2):
                                ps_ = slice(half * CH, (half + 1) * CH)
                                if t0 < SC:
                                    for i in range(4):
                                        b.dma('sp', yb[ps_, i * TCc:(i + 1) * TCc],
                                              mine_rows(PAs[0], i, u * 256 + oc * 128, half)[:, TLc:TLc + TCc],
                                              reads=[('Pmine', 0)], writes=[('y', k)])
                                else:
                                    a_ = t0 - SC
                                    i, loc = a_ // TLc, a_ % TLc
                                    b.dma('sp', yb[ps_, :n], mine_rows(PAs[0], i, u * 256 + oc * 128, half)[:, loc:loc + n],
                                          reads=[('Pmine', 0)], writes=[('y', k)])
                            b.op('act', lambda E, yb=yb, n=n: E.activation(
                                out=yb[:, :n], in_=yb[:, :n], func=AF.Gelu_apprx_tanh),
                                reads=[('y', k)], writes=[('y', k)])
                            b.op('dve', lambda E, sb_=sb_, hb_=hb_, n=n, t0=t0: E.tensor_tensor(
                                out=sb_[:, :n], in0=Hf[:, t0:t0 + n], in1=hb_[:, :n][:, ::-1], op=ALU.add),
                                reads=['Hf', ('hbr', k)], writes=[('s', k)])
                            b.op('pool', lambda E, sb_=sb_, yb=yb, mo_=mo_, n=n: E.tensor_tensor(
                                out=mo_[:, :n], in0=sb_[:, :n], in1=yb[:, :n], op=ALU.mult),
                                reads=[('s', k), ('y', k)], writes=[('mo', k)])
                            write_nat(b, MS, u * 256 + oc * 128, t0, n, mo_, [('mo', k)], TLc, TCc)
                    if d == 1:
                        f0_ = u * 256 + oc * 128
                        for dest in range(4):
                            c_ = (dest * 512 + f0_) // CH
                            gather_rows(b, C.Msend, C.Mall, c_, c_ + 2, ['Msend'], 'Mall')
        b.flush()


def phase_scan_a(b, C, j, lbmode):
    SC, SL = CTX_, SEQ_
    TS = SC + SL
    NP_, NS = 4, 8
    NT = TS // 128
    TLc, TCc = C.TLc, C.TCc
    PAs = [C.Pmine[s_].ap() for s_ in range(5)]


    def load_nat(q, dst, n, t0, sec, pi_, wkey):
        for half in range(2):
            ps_ = slice(half * CH, (half + 1) * CH)
            if t0 < SC:
                for i in range(4):
                    b.dma(q, dst[ps_, i * TCc:(i + 1) * TCc], mine_rows(PAs[sec], i, pi_ * 128, half)[:, TLc:TLc + TCc],
                          reads=[('Pmine', sec)], writes=[wkey])
            else:
                a_ = t0 - SC
                i, loc = a_ // TLc, a_ % TLc
                b.dma(q, dst[ps_, :n], mine_rows(PAs[sec], i, pi_ * 128, half)[:, loc:loc + n],
                      reads=[('Pmine', sec)], writes=[wkey])

    nat_blks = [(0, SC)] + [(SC + t, 512) for t in range(0, SL, 512)]
    sblk = {0: nat_blks, 1: [(0, SC)] + [(SC + SL - 512 - t, 512) for t in range(0, SL, 512)]}

    with ExitStack() as es:
        b.es = es
        ones32, epsb = C.ones32, C.epsb
        pv_sb = b.sb('pv', [128, 20], F32)
        b.dma('sp', pv_sb[:], C.pvA[j], writes=['pv'])
        cst_sb = b.sb('cst', [128, 897], F32)
        b.dma('sp', cst_sb[:], C.cst, writes=['cst'])
        ident = b.sb('ident', [128, 128], BF16)
        b.op('dve', lambda E: E.tensor_copy(out=ident[:], in_=cst_sb[:, 0:128]), reads=['cst'], writes=['ident'])
        Jm = b.sb('Jm', [128, 128], BF16)
        b.op('dve', lambda E: E.tensor_copy(out=Jm[:], in_=cst_sb[:, 769:897]), reads=['cst'], writes=['Jm'])
        amask = cst_sb[:, 128:256]
        rmask = cst_sb[:, 256:768]
        lb = b.sb('lb', [128, NS], F32)
        oml = b.sb('oml', [128, NS], F32)
        if lbmode:
            pv3 = pv_sb[:, 0:2 * NS].rearrange("p (s two) -> p s two", two=2)
            b.op('dve', lambda E: E.tensor_tensor(out=lb[:], in0=pv3[:, :, 1], in1=pv3[:, :, 0], op=ALU.subtract),
                 reads=['pv'], writes=['lb'])
            b.op('act', lambda E: E.activation(out=lb[:], in_=lb[:], func=AF.Sigmoid), reads=['lb'], writes=['lb'])
            b.op('dve', lambda E: E.tensor_scalar(out=oml[:], in0=lb[:], scalar1=-1.0, scalar2=1.0,
                                                  op0=ALU.mult, op1=ALU.add), reads=['lb'], writes=['oml'])
        O = [b.sb('O%d' % i, [128, TS], F32) for i in range(2)]
        V = [b.sb('V%d' % i, [128, NT, 128], BF16) for i in range(2)]
        vT = [b.sb('vT%d' % i, [128, 512], BF16) for i in range(2)]
        Wk = {}
        for nm_ in ['q', 'z', 'f', 'lf', 'k', 'bc', 'eb', 'enb', 'k32']:
            Wk[nm_] = [b.sb('a_%s%d' % (nm_, i), [128, 512], F32) for i in range(2)]
        Qt = [b.sb('Qt%d' % i, [128, 512], BF16) for i in range(2)]
        Kt = [b.sb('Kt%d' % i, [128, 512], BF16) for i in range(2)]
        KbT = [b.sb('KbT%d' % i, [128, 512], BF16) for i in range(2)]
        Kb = [b.sb('Kb%d' % i, [128, 128], BF16) for i in range(2)]
        Am = [b.sb('Am%d' % i, [128, 128], BF16) for i in range(2)]
        Kbz = [b.sb('Kbz%d' % i, [128, 128], BF16) for i in range(2)]
        S32 = [b.sb('S32_%d' % i, [128, 128], F32) for i in range(2)]
        S16 = [[b.sb('S16_%d_%d' % (i, c), [128, 128], BF16) for c in range(4)] for i in range(2)]
        pT1 = b.ps('pT', [128, 128], BF16)
        pA1 = b.ps('pA', [128, 512])
        pO = [b.ps('pO%d' % i, [128, 128]) for i in range(2)]
        pS = [[b.ps('pS%d_%d' % (i, c), [128, 128]) for c in range(2)] for i in range(2)]
        fb = {}
        for nm_ in ['s', 'sq', 'rs', 'g', 'mo']:
            fb[nm_] = b.sb('f_%s' % nm_, [128, 512], F32)
        pNb = pA1
        MS = C.Msend.ap()

        for pr in range(NP_):
            for ci, (t0, n) in enumerate(nat_blks):
                vt = vT[ci % 2]
                load_nat('pool', vt, n, t0, 1, pr, ('vT', ci % 2))
                for tt in range(n // 128):
                    ti = (t0 + tt * 128) // 128
                    x_ = ti % 2
                    b.op('pe', lambda E, vt=vt, tt=tt, x_=x_: E.transpose(
                        out=pT1[:, :], in_=vt[:, tt * 128:(tt + 1) * 128], identity=ident[:, :]),
                        reads=[('vT', ci % 2), 'ident'], writes=[('pT', 0)])
                    b.op('act', lambda E, ti=ti, x_=x_: E.activation(out=V[0][:, ti, :], in_=pT1[:, :], func=AF.Copy),
                         reads=[('pT', 0)], writes=[('V', 0)])
                    b.op('pe', lambda E, ti=ti, x_=x_: E.matmul(pS[0][x_][:, :], lhsT=Jm[:, :], rhs=V[0][:, ti, :],
                                                               start=True, stop=True),
                         reads=[('V', 0), 'Jm'], writes=[('pS', 0, x_)])
                    b.op('dve', lambda E, ti=ti, x_=x_: E.tensor_copy(out=V[1][:, ti, :], in_=pS[0][x_][:, :]),
                         reads=[('pS', 0, x_)], writes=[('V', 1)])
            for dr in range(2):
                b.op('dve', lambda E, dr=dr: E.memset(S32[dr][:], 0.0), writes=[('S32', dr)])
                b.op('dve', lambda E, dr=dr: E.memset(S16[dr][3][:], 0.0), writes=[('S16', dr, 3)])
            def emit_ps(dr, ti, c):
                cs = slice(c * 32, (c + 1) * 32)
                psl = pS[dr][c % 2][:, :]
                if c < 3:
                    b.op('pe', lambda E: E.matmul(psl, lhsT=Kb[dr][cs, :], rhs=V[dr][cs, ti, :], start=True, stop=True),
                         reads=[('Kb', dr), ('V', dr)], writes=[('pS', dr, c % 2)])
                else:
                    b.op('pe', lambda E: E.matmul(psl, lhsT=Kbz[dr][64:128, :], rhs=V[dr][64:128, ti, :],
                                                  start=True, stop=True),
                         reads=[('Kbz', dr), ('V', dr)], writes=[('pS', dr, c % 2)])

            for bi in range(len(nat_blks)):
                n = nat_blks[bi][1]
                for dr in range(2):
                    s = 2 * pr + dr
                    nt0 = sblk[dr][bi][0]
                    q, z, f, lf, k, bc, eb, enb, k32 = [Wk[x][dr] for x in
                                                        ['q', 'z', 'f', 'lf', 'k', 'bc', 'eb', 'enb', 'k32']]
                    K_ = lambda x, dr=dr: (x, dr)
                    load_nat('sp', q, n, nt0, 0, pr, K_('q'))
                    load_nat('sp', z, n, nt0, 3 + dr, pr, K_('z'))
                    if dr == 0:
                        zin, qin = z[:, :n], q[:, :n]
                    else:
                        zin, qin = z[:, :n][:, ::-1], q[:, :n][:, ::-1]
                    b.op('act', lambda E, zin=zin, f=f, n=n: E.activation(out=f[:, :n], in_=zin, func=AF.Exp, scale=-1.0),
                         reads=[K_('z')], writes=[K_('f')])
                    b.op('pool', lambda E, f=f, n=n: E.tensor_scalar(out=f[:, :n], in0=f[:, :n], scalar1=1.0, scalar2=None,
                                                                    op0=ALU.add), reads=[K_('f')], writes=[K_('f')])
                    b.op('dve', lambda E, f=f, n=n: E.reciprocal(out=f[:, :n], in_=f[:, :n]),
                         reads=[K_('f')], writes=[K_('f')])
                    if lbmode:
                        b.op('dve', lambda E, f=f, n=n, s=s: E.tensor_scalar(
                            out=f[:, :n], in0=f[:, :n], scalar1=oml[:, s:s + 1], scalar2=lb[:, s:s + 1],
                            op0=ALU.mult, op1=ALU.add), reads=[K_('f'), 'lb', 'oml'], writes=[K_('f')])
                    b.op('act', lambda E, f=f, lf=lf, n=n: E.activation(out=lf[:, :n], in_=f[:, :n], func=AF.Ln),
                         reads=[K_('f')], writes=[K_('lf')])
                    b.op('pool', lambda E, f=f, k=k, n=n: E.tensor_scalar(
                        out=k[:, :n], in0=f[:, :n], scalar1=-1.0, scalar2=1.0, op0=ALU.mult, op1=ALU.add),
                        reads=[K_('f')], writes=[K_('k')])
                    b.op('dve', lambda E, lf=lf, bc=bc, n=n: E.tensor_tensor_scan(
                        out=bc[:, :n], data0=rmask[:, :n], data1=lf[:, :n], initial=0.0, op0=ALU.mult, op1=ALU.add),
                        reads=[K_('lf'), 'cst'], writes=[K_('bc')])
                    b.op('act', lambda E, bc=bc, eb=eb, n=n: E.activation(out=eb[:, :n], in_=bc[:, :n], func=AF.Exp),
                         reads=[K_('bc')], writes=[K_('eb')])
                    b.op('act', lambda E, bc=bc, enb=enb, n=n: E.activation(out=enb[:, :n], in_=bc[:, :n], func=AF.Exp,
                                                                          scale=-1.0),
                         reads=[K_('bc')], writes=[K_('enb')])
                    b.op('pool', lambda E, qin=qin, eb=eb, dr=dr, n=n: E.tensor_tensor(
                        out=Qt[dr][:, :n], in0=qin, in1=eb[:, :n], op=ALU.mult),
                        reads=[K_('q'), K_('eb')], writes=[K_('Qt')])
                    b.op('dve', lambda E, k=k, enb=enb, k32=k32, n=n: E.tensor_tensor(
                        out=k32[:, :n], in0=k[:, :n], in1=enb[:, :n], op=ALU.mult),
                        reads=[K_('k'), K_('enb')], writes=[K_('k32')])
                    b.op('act', lambda E, k32=k32, dr=dr, n=n: E.activation(out=Kt[dr][:, :n], in_=k32[:, :n], func=AF.Copy),
                         reads=[K_('k32')], writes=[K_('Kt')])
                    ebl = eb[:, :n].rearrange("p (c j) -> p c j", j=32)[:, :, 31:32].broadcast_to([128, n // 32, 32])
                    b.op('pool', lambda E, k32=k32, ebl=ebl, dr=dr, n=n: E.tensor_tensor(
                        out=KbT[dr][:, :n].rearrange("p (c j) -> p c j", j=32),
                        in0=k32[:, :n].rearrange("p (c j) -> p c j", j=32), in1=ebl, op=ALU.mult),
                        reads=[K_('k32'), K_('eb')], writes=[K_('KbT')])
                for tt in range(n // 128):
                    info = {}
                    for dr in range(2):
                        K_ = lambda x, dr=dr: (x, dr)
                        c0 = tt * 128
                        nt0 = sblk[dr][bi][0]
                        if dr == 0:
                            ti = (nt0 + c0) // 128
                            st0 = nt0 + c0
                        else:
                            ti = (nt0 + n - c0 - 128) // 128
                            st0 = (0 if bi == 0 else SC + (bi - 1) * 512) + c0
                        info[dr] = (ti, st0)
                        b.op('pe', lambda E, dr=dr, c0=c0: E.transpose(out=pT1[:, :], in_=KbT[dr][:, c0:c0 + 128],
                                                                       identity=ident[:, :]),
                             reads=[K_('KbT'), 'ident'], writes=[('pT', 0)])
                        b.op('act', lambda E, dr=dr: E.activation(out=Kb[dr][:, :], in_=pT1[:, :], func=AF.Copy),
                             reads=[('pT', 0)], writes=[K_('Kb')])
                        b.op('pool', lambda E, dr=dr: E.tensor_scalar(
                            out=Kbz[dr][64:128, :], in0=Kb[dr][64:128, :], scalar1=cst_sb[64:128, 768:769], scalar2=None,
                            op0=ALU.mult), reads=[K_('Kb'), 'cst'], writes=[K_('Kbz')])
                        b.op('pe', lambda E, dr=dr, c0=c0: E.matmul(pA1[:, 0:128], lhsT=Kt[dr][:, c0:c0 + 128],
                                                                    rhs=Qt[dr][:, c0:c0 + 128], start=True, stop=True),
                             reads=[K_('Kt'), K_('Qt')], writes=[('pA', 0)])
                        b.op('dve', lambda E, dr=dr: E.tensor_tensor(out=Am[dr][:, :], in0=pA1[:, 0:128], in1=amask,
                                                                     op=ALU.mult),
                             reads=[('pA', 0), 'cst'], writes=[K_('Am')])
                        b.op('pe', lambda E, dr=dr, ti=ti: E.matmul(pO[dr][:, :], lhsT=V[dr][:, ti, :], rhs=Am[dr][:, :],
                                                                    start=True, stop=False),
                             reads=[('V', dr), K_('Am')], writes=[K_('pO')])
                        for c in range(2):
                            emit_ps(dr, ti, c)
                    for dr in range(2):
                        K_ = lambda x, dr=dr: (x, dr)
                        eb = Wk['eb'][dr]
                        c0 = tt * 128
                        ti, st0 = info[dr]
                        for c in range(4):
                            cs = slice(c * 32, (c + 1) * 32)
                            psl = pS[dr][c % 2][:, :]
                            sprev = S16[dr][(c - 1) % 4]
                            b.op('pe', lambda E, dr=dr, c0=c0, c=c, cs=cs, sprev=sprev: E.matmul(
                                pO[dr][:, cs], lhsT=sprev[:, :], rhs=Qt[dr][:, c0 + c * 32:c0 + (c + 1) * 32],
                                start=False, stop=(c == 3)),
                                reads=[('S16', dr, (c - 1) % 4), K_('Qt')], writes=[K_('pO')])
                            b.op('dve', lambda E, dr=dr, eb=eb, c0=c0, c=c, psl=psl: E.scalar_tensor_tensor(
                                out=S32[dr][:, :], in0=S32[dr][:, :], scalar=eb[:, c0 + c * 32 + 31:c0 + c * 32 + 32],
                                in1=psl, op0=ALU.mult, op1=ALU.add),
                                reads=[('S32', dr), K_('eb'), ('pS', dr, c % 2)], writes=[('S32', dr)])
                            b.op('dve', lambda E, dr=dr, c=c: E.tensor_copy(out=S16[dr][c][:, :], in_=S32[dr][:, :]),
                                 reads=[('S32', dr)], writes=[('S16', dr, c)])
                            if c + 2 < 4:
                                emit_ps(dr, ti, c + 2)
                        b.op('act', lambda E, dr=dr, st0=st0: E.activation(
                            out=O[dr][:, st0:st0 + 128], in_=pO[dr][:, :], func=AF.Copy),
                            reads=[K_('pO')], writes=[('O', dr)])
            for (t0, n) in nat_blks:
                lo = 0 if t0 < SC else SC + SL - (t0 - SC) - n
                sb_, sq, rs, g, mo = fb['s'], fb['sq'], fb['rs'], fb['g'], fb['mo']
                load_nat('sp', g, n, t0, 2, pr, 'fg')
                b.op('dve', lambda E, t0=t0, n=n, lo=lo: E.tensor_tensor(
                    out=sb_[:, :n], in0=O[0][:, t0:t0 + n], in1=O[1][:, lo:lo + n][:, ::-1], op=ALU.add),
                    reads=[('O', 0), ('O', 1)], writes=['fs'])
                b.op('act', lambda E, n=n: E.activation(out=sq[:, :n], in_=sb_[:, :n], func=AF.Square),
                     reads=['fs'], writes=['fsq'])
                b.op('pe', lambda E, n=n: E.matmul(pNb[:, :n], lhsT=ones32[:, :], rhs=sq[:, :n], start=True, stop=True),
                     reads=['fsq', 'ones32'], writes=[('pA', 0)])
                b.op('act', lambda E, n=n: E.activation(out=rs[:, :n], in_=pNb[:, :n], func=AF.Sqrt,
                                                        scale=1.0 / 128, bias=epsb[:, 0:1]),
                     reads=[('pA', 0), 'epsb'], writes=['frs'])
                b.op('dve', lambda E, n=n: E.reciprocal(out=rs[:, :n], in_=rs[:, :n]), reads=['frs'], writes=['frs'])
                b.op('act', lambda E, n=n: E.activation(out=g[:, :n], in_=g[:, :n], func=AF.Silu),
                     reads=['fg'], writes=['fg'])
                b.op('dve', lambda E, n=n, pr=pr: E.scalar_tensor_tensor(
                    out=sb_[:, :n], in0=sb_[:, :n], scalar=pv_sb[:, 16 + pr:17 + pr], in1=rs[:, :n],
                    op0=ALU.mult, op1=ALU.mult), reads=['fs', 'frs', 'pv'], writes=['fs'])
                b.op('pool', lambda E, n=n: E.tensor_tensor(out=mo[:, :n], in0=sb_[:, :n], in1=g[:, :n], op=ALU.mult),
                     reads=['fs', 'fg'], writes=['fmo'])
                write_nat(b, MS, pr * 128, t0, n, mo, ['fmo'], TLc, TCc)
            for dest in range(4):
                c_ = (dest * 512 + pr * 128) // CH
                gather_rows(b, C.Msend, C.Mall, c_, c_ + 2, ['Msend'], 'Mall')
        b.flush()


def _dbg_dump(b, C, oT, src_ap, rkey):
    b.dma('sp', oT, src_ap, reads=[rkey], writes=['out'])
    b.flush()


def build_fused():
    nc = bass.Bass("TRN2", target_bir_lowering=False)
    C = Ctx()
    C.TLc = SEQ_ * B_ // NCORES
    C.TCc = CTX_ * B_ // NCORES
    C.Tc = C.TLc + C.TCc
    TS = CTX_ + SEQ_

    def inp(name, shape):
        return nc.dram_tensor(name, shape, F32, kind="ExternalInput").ap()
    xT = inp("xT", [D, C.Tc])
    C.cT = inp("cT", [D, 4])
    C.wm = inp("wm", [DEPTH * D, 3072])
    C.bm = inp("bm", [128, DEPTH * 24])
    C.gains = inp("gains", [128, 9, KC])
    NA, NB_ = (NLAYERS + 1) // 2, NLAYERS // 2
    if DBG_STOP == 'mod':
        NA, NB_ = 0, 0
    awin = [inp("awin%d" % j, [D, 5 * D]) for j in range(NA)]
    awout = [inp("awout%d" % j, [D, D]) for j in range(0 if DBG_STOP else NA)]
    bwin = [inp("bwin%d" % j, [D, 2 * D]) for j in range(NB_)]
    bwout = [inp("bwout%d" % j, [D, D]) for j in range(NB_)]
    NW = 0 if DBG_STOP else NLAYERS
    w1 = [inp("w1_%d" % l, [D, DFF]) for l in range(NW)]
    w2 = [inp("w2_%d" % l, [DFF, D]) for l in range(NW)]
    C.pvA = inp("pvA", [2, 128, 20])
    C.cst = inp("cst", [128, 897])
    C.pvB = inp("pvB", [2, 128, 44])
    C.gwB = inp("gwB", [2, 8, 256, 256])
    oT = nc.dram_tensor("oT", [D, C.TLc], F32, kind="ExternalOutput").ap()
    C.modsend = nc.dram_tensor("modsend", [512, DEPTH * 24], F32)
    C.modall = nc.dram_tensor("modall", [4 * 512, DEPTH * 24], F32)
    C.modmine = nc.dram_tensor("modmine", [4 * 128, DEPTH * 24], F32)
    C.xbuf = nc.dram_tensor("xbuf", [D, C.Tc], F32)
    C.Psend = [nc.dram_tensor("Psend%d" % s_, [D, C.Tc], F32) for s_ in range(5)]
    C.Pall = [nc.dram_tensor("Pall%d" % s_, [4 * D, C.Tc], F32) for s_ in range(5)]
    C.Pmine = [nc.dram_tensor("Pmine%d" % s_, [D, C.Tc], F32) for s_ in range(5)]
    C.Msend = nc.dram_tensor("Msend", [4 * 512, C.Tc], F32)
    C.Mall = nc.dram_tensor("Mall", [4 * D, C.Tc], F32)
    C.Mmine = nc.dram_tensor("Mmine", [D, C.Tc], F32)
    C.Upad = nc.dram_tensor("Upad", [2, 2, 128, TS + 7], F32)
    C.wcache = nc.dram_tensor("wcache", [36, 128, 8192], BF16).ap()
    with ExitStack() as es:
        b = Bld(nc, es)
        C.ones32 = b.sb('ones32', [128, 128], F32)
        b.op('dve', lambda E: E.memset(C.ones32[:], 1.0), writes=['ones32'])
        C.epsb = b.sb('epsb', [128, 1], F32)
        b.op('dve', lambda E: E.memset(C.epsb[:], EPS), writes=['epsb'])
        C.modtab = b.sb('modtab', [128, 2, DEPTH, 96], F32)
        C.gains_sb = b.sb('gains', [128, 9, KC], F32)
        phase_mod(b, C)
        if DBG_STOP == 'mod':
            with ExitStack() as es2:
                b.es = es2
                t_ = b.sb('dbg', [128, 2 * DEPTH * 96], F32)
                b.op('dve', lambda E: E.tensor_copy(out=t_[:], in_=C.modtab[:].rearrange("p a l g -> p (a l g)")),
                     reads=['modtab'], writes=['dbg'])
                _dbg_dump(b, C, oT[0:256, 0:384].rearrange("(a p) n -> p a n", p=128), t_[:].rearrange("p (a n) -> p a n", a=2), 'dbg')
            return nc
        for layer in range(NLAYERS):
            j = layer // 2
            final = (layer == NLAYERS - 1)
            x_in = xT if layer == 0 else C.xbuf.ap()
            if layer % 2 == 0:
                phase_proj(b, C, layer, x_in, awin[j], 5 * D)
                if DBG_STOP == 'proj':
                    _dbg_dump(b, C, oT[:, 0:C.TLc], C.Pall[0].ap()[4096:6144, 0:C.TLc], ('Pall', 0))
                    return nc
                phase_extract(b, [(C.Pall[s_], C.Pmine[s_].ap(), 1, 2048, 2048, C.Tc, 'k', ('Pall', s_), ('Pmine', s_))
                                  for s_ in range(5)])
                if DBG_STOP == 'extract':
                    _dbg_dump(b, C, oT[:, 0:C.TLc], C.Pmine[3].ap()[:, 0:C.TLc], ('Pmine', 3))
                    return nc
                phase_scan_a(b, C, j, 1 if j > 0 else 0)
                if DBG_STOP == 'scan':
                    _dbg_dump(b, C, oT[:, 0:C.TLc], C.Mall.ap()[2048:4096, 0:C.TLc], 'Mall')
                    return nc
                wo = awout[j]
            else:
                phase_proj(b, C, layer, x_in, bwin[j], 2 * D)
                phase_extract(b, [(C.Pall[s_], C.Pmine[s_].ap(), 1, 2048, 2048, C.Tc, 'k', ('Pall', s_), ('Pmine', s_))
                                  for s_ in (0, 1)])
                phase_scan_b(b, C, j)
                wo = bwout[j]
            phase_extract(b, [(C.Mall, C.Mmine.ap(), 1, 2048, 2048, C.Tc, 'k', 'Mall', 'Mmine')])
            phase_l3(b, C, layer, x_in, wo, w1[layer], w2[layer], oT if final else C.xbuf.ap(), final)
    return nc


_NC = {}


def _f32(a):
    return np.ascontiguousarray(a, dtype=np.float32)


def _pl(v):
    return np.asarray(v, dtype=np.float32).reshape(KC, 128).T


def _consts_a():
    ident = np.eye(128, dtype=np.float32)
    s_ = np.arange(128)[:, None]
    t_ = np.arange(128)[None, :]
    am = ((s_ <= t_) & (s_ // 32 == t_ // 32)).astype(np.float32)
    rm = np.ones((128, 512), np.float32)
    rm[:, ::32] = 0
    m96 = (np.arange(128) >= 96).astype(np.float32)[:, None]
    Jm = np.ascontiguousarray(ident[::-1])
    return np.ascontiguousarray(np.concatenate([ident, am, rm, m96, Jm], axis=1))


def make_in_maps(x, c, ctx, c_ctx, w_mod, b_mod, norm1, norm2, a_w_in, a_lb_logits, a_onorm, a_w_out,
                 b_w_in, b_conv_w, b_conv_b, b_gate_w, b_gate_b, b_lambda, b_w_out, mlp_w1, mlp_w2, final_norm):
    TLc = SEQ_ * B_ // NCORES
    TCc = CTX_ * B_ // NCORES
    x = np.asarray(x, np.float32)
    ctx = np.asarray(ctx, np.float32)
    latf = x.reshape(B_ * SEQ_, D)
    ctxf = ctx.reshape(B_ * CTX_, D)
    cT = np.zeros((D, 4), np.float32)
    cT[:, 0] = c[0]
    cT[:, 1] = c[1]
    cT[:, 2] = c_ctx
    gains = np.zeros((128, 9, KC), np.float32)
    for l in range(NLAYERS):
        gains[:, l] = _pl(norm1[l])
        gains[:, 4 + l] = _pl(norm2[l])
    gains[:, 8] = _pl(final_norm)
    cst = _consts_a()
    shared = {"cT": cT, "gains": _f32(gains), "cst": cst}
    for j in range((NLAYERS + 1) // 2):
        shared["awin%d" % j] = _f32(a_w_in[j])
        shared["awout%d" % j] = _f32(a_w_out[j])
    for j in range(NLAYERS // 2):
        shared["bwin%d" % j] = _f32(b_w_in[j])
        shared["bwout%d" % j] = _f32(b_w_out[j])
    for l in range(NLAYERS):
        shared["w1_%d" % l] = _f32(mlp_w1[l])
        shared["w2_%d" % l] = _f32(mlp_w2[l])
    ims = []
    for r in range(NCORES):
        k = r % 4
        m = dict(shared)
        m["xT"] = _f32(np.concatenate([latf[r * TLc:(r + 1) * TLc], ctxf[r * TCc:(r + 1) * TCc]], axis=0).T)
        m["wm"] = _f32(np.concatenate([w_mod[l][:, k * 3072:(k + 1) * 3072] for l in range(DEPTH)], axis=0))
        m["bm"] = _f32(np.concatenate([np.asarray(b_mod[l][k * 3072:(k + 1) * 3072]).reshape(24, 128).T
                                       for l in range(DEPTH)], axis=1))
        pvA = []
        for j in range(2):
            cols = []
            for pi_ in range(4):
                hd = 4 * k + pi_
                sl = slice(hd * 128, (hd + 1) * 128)
                for d_ in range(2):
                    cols += [a_lb_logits[0, d_, sl], a_lb_logits[j, d_, sl]]
            for pi_ in range(4):
                hd = 4 * k + pi_
                cols.append(a_onorm[j][hd * 128:(hd + 1) * 128])
            pvA.append(np.stack(cols, axis=1))
        m["pvA"] = _f32(np.stack(pvA))
        pvB, gwB = [], []
        for j in range(2):
            cols, gws = [], []
            for u in range(2):
                blk = 2 * k + u
                sl = slice(blk * 256, (blk + 1) * 256)

                def h2(v_):
                    return np.asarray(v_[sl], np.float32).reshape(2, 128)
                cols += [h2(b_conv_w[j][jj])[cc] for cc in range(2) for jj in range(4)] + \
                        [h2(b_conv_b[j])[cc] for cc in range(2)] + \
                        [h2(b_gate_b[j][d_, gi_])[cc] for d_ in range(2) for gi_ in range(2) for cc in range(2)] + \
                        [h2(b_lambda[j][d_])[cc] for d_ in range(2) for cc in range(2)]
                gws += [b_gate_w[j][d_, gi_, blk] for d_ in range(2) for gi_ in range(2)]
            pvB.append(np.stack(cols, axis=1))
            gwB.append(np.stack(gws))
        m["pvB"] = _f32(np.stack(pvB))
        m["gwB"] = _f32(np.stack(gwB))
        ims.append(m)
    return ims


def kernel(**inputs):
    if 'nc' not in _NC:
        _NC['nc'] = build_fused()
    nc = _NC['nc']
    ims = make_in_maps(**inputs)
    res = run_bass_kernel_spmd(nc, ims, core_ids=list(range(NCORES))).results
    out = np.concatenate([res[r]["oT"].T for r in range(NCORES)], axis=0).reshape(B_, SEQ_, D)
    return np.ascontiguousarray(out, dtype=np.float32)
```

```python
import os
from contextlib import ExitStack
import numpy as np
import concourse.bass as bass
import concourse.mybir as mybir
from concourse.bass import ds
from concourse.bass_utils import run_bass_kernel_spmd

F32 = mybir.dt.float32
BF16 = mybir.dt.bfloat16
AF = mybir.ActivationFunctionType
ALU = mybir.AluOpType

D = 2048
KC = 16
DFF = 8192
EPS = 1e-6
NCORES = 8
B_, SEQ_, CTX_ = 2, 8192, 256
DEPTH = 4
NLAYERS = int(os.environ.get('KDBG_NLAYERS', '4'))
DBG_MOD = ''
DBG_STOP = os.environ.get('KDBG_STOP') or None
GROUPS4 = [[0, 1, 2, 3], [4, 5, 6, 7]]


class Bld:
    ENG = ['pe', 'act', 'dve', 'pool', 'sp']
    NDS = 8

    def __init__(self, nc, es):
        self.nc = nc
        self.es = es
        self.q = {e: [] for e in self.ENG}
        self.cnt = {e: 0 for e in self.ENG + ['cc']}
        self.sem = {e: es.enter_context(nc.semaphore('s_' + e)) for e in ['pe', 'act', 'dve', 'pool', 'cc']}
        self.dsem = {e: [es.enter_context(nc.semaphore('d_%s%d' % (e, i))) for i in range(self.NDS)]
                     for e in ['sp', 'pool', 'act']}
        self.dcnt = {e: 0 for e in ['sp', 'pool', 'act']}
        self.lastw = {}
        self.readers = {}
        self.waited = {e: {} for e in self.ENG}
        self.uid = 0

    def sb(self, name, shape, dtype):
        self.uid += 1
        return self.es.enter_context(self.nc.sbuf_tensor('sb%d_%s' % (self.uid, name), shape, dtype))

    def ps(self, name, shape, dtype=F32):
        self.uid += 1
        return self.es.enter_context(self.nc.psum_tensor('ps%d_%s' % (self.uid, name), shape, dtype))

    def _wait(self, eng, tok):
        kind, e2, n = tok
        if kind == 'c':
            if e2 == eng and eng == 'pe':
                return
            sem = self.sem[e2]
            val = n
            key = ('c', e2)
        else:
            sem = self.dsem[e2][n % self.NDS]
            val = 16 * (n // self.NDS + 1)
            key = ('d', e2, n % self.NDS)
        if self.waited[eng].get(key, 0) >= val:
            return
        self.waited[eng][key] = val
        self.q[eng].append(lambda E, sem=sem, val=val: E.wait_ge(sem, val))

    def _deps(self, eng, reads, writes):
        toks = set()
        for k in reads:
            if k in self.lastw:
                toks.add(self.lastw[k])
        for k in writes:
            if k in self.lastw:
                toks.add(self.lastw[k])
            for t in self.readers.get(k, {}).values():
                if t[0] == 'c' and t[1] == eng:
                    continue
                toks.add(t)
        for t in toks:
            self._wait(eng, t)

    def _commit(self, tok, reads, writes):
        for k in writes:
            self.lastw[k] = tok
            self.readers[k] = {}
        for k in reads:
            r = self.readers.setdefault(k, {})
            if tok[0] == 'c':
                r[('c', tok[1])] = tok
            else:
                r[tok] = tok

    def op(self, eng, fn, reads=(), writes=()):
        self._deps(eng, reads, writes)
        self.cnt[eng] += 1
        n = self.cnt[eng]
        sem = self.sem[eng]
        self.q[eng].append(lambda E, fn=fn, sem=sem: fn(E).then_inc(sem, 1))
        self._commit(('c', eng, n), reads, writes)

    def dma(self, qe, out, in_, reads=(), writes=()):
        i = self.dcnt[qe]
        self.dcnt[qe] += 1
        if i >= self.NDS:
            self._wait(qe, ('d', qe, i - self.NDS))
        self._deps(qe, reads, writes)
        sem = self.dsem[qe][i % self.NDS]

        def f(E, out=out, in_=in_, sem=sem):
            o = out(E) if callable(out) else out
            s = in_(E) if callable(in_) else in_
            try:
                E.dma_start(out=o, in_=s).then_inc(sem, 16)
            except Exception:
                print("DMA build failed: out=", o, " in=", s)
                raise
        self.q[qe].append(f)
        self._commit(('d', qe, i), reads, writes)

    def cc(self, kind, groups, in_ap, out_ap, reads=(), writes=()):
        self._deps('pool', reads, writes)
        self.cnt['cc'] += 1
        n = self.cnt['cc']
        sem = self.sem['cc']
        self.q['pool'].append(lambda E: E.collective_compute(
            kind, ALU.bypass, replica_groups=groups, ins=[in_ap], outs=[out_ap]).then_inc(sem, 1))
        self._commit(('c', 'cc', n), reads, writes)

    def barrier(self):
        for e in self.ENG:
            for e2 in ['pe', 'act', 'dve', 'pool', 'cc']:
                if self.cnt[e2] > 0:
                    self._wait(e, ('c', e2, self.cnt[e2]))
            for qe in ['sp', 'pool', 'act']:
                for i in range(max(0, self.dcnt[qe] - self.NDS), self.dcnt[qe]):
                    self._wait(e, ('d', qe, i))

    def flush(self):
        self.barrier()
        _PID.clear()
        q = self.q
        with self.nc.Block() as block:
            @block.tensor
            def _(E):
                for f in q['pe']:
                    f(E)

            @block.scalar
            def _(E):
                for f in q['act']:
                    f(E)

            @block.vector
            def _(E):
                for f in q['dve']:
                    f(E)

            @block.gpsimd
            def _(E):
                for f in q['pool']:
                    f(E)

            @block.sync
            def _(E):
                for f in q['sp']:
                    f(E)
        self.q = {e: [] for e in self.ENG}
        for a in ('wbuf',):
            if hasattr(self, a):
                delattr(self, a)


_PID = {}
_XQ = {'i': 0}


def _pid(E):
    key = id(E)
    if key not in _PID:
        _PID[key] = {'p': E.partition_id()}
    return _PID[key]


def phase_extract(b, items):
    qe = ['sp', 'act'][_XQ['i'] % 2]
    _XQ['i'] += 1
    for (gat, mine, nsrc, rps, rm, T, which, kin, kout) in items:
        def src(E, gat=gat, nsrc=nsrc, rps=rps, rm=rm, T=T, which=which):
            d = _pid(E)
            bk = (which, rm * T)
            if bk not in d:
                idv = (d['p'] % 4) if which == 'k' else (d['p'] // 4)
                d[bk] = E.compute_val(idv * (rm * T))
            return bass.AP(tensor=gat, offset=d[bk], ap=[[rps * T, nsrc], [T, rm], [1, T]])
        b.dma(qe, mine.rearrange("(i r) t -> i r t", i=nsrc), src, reads=[kin], writes=[kout])
    b.flush()


CH = 64


def gather_rows(b, send, gall, c0, c1, rkeys, wkey):
    for c in range(c0, c1):
        b.cc("AllGather", GROUPS4, send.ap()[c * CH:(c + 1) * CH, :], gall.ap()[c * 4 * CH:(c + 1) * 4 * CH, :],
             reads=rkeys, writes=[wkey])


def mine_rows(ap2d, i, lo, half):
    r0 = ((lo // CH + half) * 4 + i) * CH
    return ap2d[r0:r0 + CH, :]


def write_nat(b, MS, f0, t0, n, src, rkeys, TLc, TCc):
    if t0 < CTX_:
        for i in range(4):
            b.dma('sp', MS[i * 512 + f0:i * 512 + f0 + 128, TLc:TLc + TCc], src[:, i * TCc:(i + 1) * TCc],
                  reads=rkeys, writes=['Msend'])
    else:
        a_ = t0 - CTX_
        i, loc = a_ // TLc, a_ % TLc
        b.dma('sp', MS[i * 512 + f0:i * 512 + f0 + 128, loc:loc + n], src[:, :n], reads=rkeys, writes=['Msend'])


def blocks_for(T, TL):
    bl = []
    t = 0
    while t < TL:
        n = min(512, TL - t)
        bl.append((t, n, 0))
        t += n
    while t < T:
        n = min(512, T - t)
        bl.append((t, n, 1))
        t += n
    return bl


class NormMod:
    def __init__(self, b, tag, ones32, epsb, gain, sc, sh, pkeys):
        self.b = b
        self.tag = tag
        self.ones32 = ones32
        self.epsb = epsb
        self.gain = gain
        self.sh = sh
        self.pkeys = list(pkeys)
        self.sq = b.sb('sq_' + tag, [128, KC, 512], F32)
        self.rs = b.sb('rs_' + tag, [128, 512], F32)
        self.pss = b.ps('pss_' + tag, [128, 512])
        self.gm = b.sb('gm_' + tag, [128, 2, KC], F32)
        for g in range(2):
            b.op('dve', lambda E, g=g: E.scalar_tensor_tensor(
                out=self.gm[:, g, :], in0=sc[g], scalar=1.0, in1=gain, op0=ALU.add, op1=ALU.mult),
                reads=self.pkeys, writes=['gm_' + tag])

    def emit(self, xs, xkey, n, g, hout, hkey, plain=False):
        b = self.b
        tag = self.tag
        sq, rs, pss = self.sq, self.rs, self.pss
        tmp = sq
        b.op('act', lambda E: E.activation(out=sq[:, :, :n], in_=xs, func=AF.Square),
             reads=[xkey], writes=[('sq_' + tag, kc) for kc in range(KC)])
        for kc in range(KC):
            b.op('pe', lambda E, kc=kc: E.matmul(pss[:, :n], lhsT=self.ones32[:, :], rhs=sq[:, kc, :n],
                                                 start=(kc == 0), stop=(kc == KC - 1)),
                 reads=[('sq_' + tag, kc), 'ones32'], writes=['pss_' + tag])
        b.op('act', lambda E: E.activation(out=rs[:, :n], in_=pss[:, :n], func=AF.Sqrt,
                                           scale=1.0 / D, bias=self.epsb[:, 0:1]),
             reads=['pss_' + tag, 'epsb'], writes=['rs_' + tag])
        b.op('dve', lambda E: E.reciprocal(out=rs[:, :n], in_=rs[:, :n]),
             reads=['rs_' + tag], writes=['rs_' + tag])
        for kc in range(KC):
            if plain:
                gcol = self.gain[:, kc:kc + 1]
                b.op('dve', lambda E, kc=kc, gcol=gcol: E.scalar_tensor_tensor(
                    out=hout[:, kc, :], in0=xs[:, kc, :], scalar=gcol, in1=rs[:, :n],
                    op0=ALU.mult, op1=ALU.mult),
                    reads=[xkey, 'rs_' + tag] + self.pkeys, writes=[hkey])
                continue
            gcol = self.gm[:, g, kc:kc + 1]
            shcol = self.sh[g][:, kc:kc + 1]
            b.op('dve', lambda E, kc=kc, gcol=gcol: E.scalar_tensor_tensor(
                out=tmp[:, kc, :n], in0=xs[:, kc, :], scalar=gcol, in1=rs[:, :n],
                op0=ALU.mult, op1=ALU.mult),
                reads=[xkey, 'rs_' + tag, 'gm_' + tag], writes=[('sq_' + tag, kc)])
            b.op('act', lambda E, kc=kc, shcol=shcol: E.activation(
                out=hout[:, kc, :], in_=tmp[:, kc, :n], func=AF.Identity, bias=shcol, scale=1.0),
                reads=[('sq_' + tag, kc)] + self.pkeys, writes=[hkey])


def gemm_stream(b, w_dram, K, N, ngrp, rhs_fn, rhs_keys, tblocks, epi, psums, after_group=None, cache=None):
    kcs = K // 128
    if not hasattr(b, 'wbuf'):
        b.wbuf = [b.sb('wbuf%d' % i, [128, 8192], BF16) for i in range(2)]
        b.wi = 0
        b.pi = 0
    wv = w_dram.rearrange("(kc p) n -> p kc n", p=128)
    for gi in range(N // ngrp):
        wflat = b.wbuf[b.wi % 2]
        wt = wflat[:, :kcs * ngrp].rearrange("p (k n) -> p k n", k=kcs)
        wk = ('wbuf', b.wi % 2)
        b.wi += 1
        if cache is not None and cache[2] == 'use':
            ck = ('wcache', cache[1] + gi)
            b.dma('sp', wflat[:, :], cache[0][cache[1] + gi], reads=[ck], writes=[wk])
        else:
            b.dma('pool', wt[:], wv[:, :, gi * ngrp:(gi + 1) * ngrp], writes=[wk])
            if cache is not None:
                ck = ('wcache', cache[1] + gi)
                b.dma('sp', cache[0][cache[1] + gi], wflat[:, :], reads=[wk], writes=[ck])
        for (t0, n, g) in tblocks:
            for c in range(ngrp // 128):
                ps, pk = psums[b.pi % len(psums)]
                b.pi += 1
                for kc in range(kcs):
                    b.op('pe', lambda E, kc=kc, c=c, ps=ps, wt=wt, t0=t0, n=n: E.matmul(
                        ps[:, :n], lhsT=wt[:, kc, c * 128:(c + 1) * 128], rhs=rhs_fn(kc, t0, n),
                        start=(kc == 0), stop=(kc == kcs - 1)),
                        reads=[wk] + rhs_keys, writes=[pk])
                epi(gi * (ngrp // 128) + c, t0, n, g, ps, pk)
        if after_group is not None:
            after_group(gi)


class Ctx:
    pass


def phase_mod(b, C):
    NCH = 24
    with ExitStack() as es:
        b.es = es
        c_sb = b.sb('c', [128, KC, 4], F32)
        s_sb = b.sb('s', [128, KC, 4], F32)
        bm_sb = b.sb('bm', [128, DEPTH * NCH], F32)
        o_sb = b.sb('o', [128, 4, DEPTH * NCH], F32)
        b.dma('sp', c_sb[:], C.cT.rearrange("(kc p) n -> p kc n", p=128), writes=['c'])
        b.dma('sp', bm_sb[:], C.bm, writes=['bm'])
        b.dma('sp', C.gains_sb[:], C.gains, writes=['gains'])
        b.op('act', lambda E: E.activation(out=s_sb[:], in_=c_sb[:], func=AF.Silu), reads=['c'], writes=['s'])
        wb = [b.sb('wm%d' % i, [128, KC, 512], F32) for i in range(2)]
        pss = [b.ps('pm%d' % i, [128, 4]) for i in range(4)]
        wv = C.wm.rearrange("(l kc p) n -> p l kc n", p=128, kc=KC)
        gi = 0
        for l in range(DEPTH):
            for gq in range(NCH // 4):
                wt = wb[gi % 2]
                wk = ('wm', gi % 2)
                gi += 1
                b.dma('sp', wt[:], wv[:, l, :, gq * 512:(gq + 1) * 512], writes=[wk])
                for c in range(4):
                    j = l * NCH + gq * 4 + c
                    ps = pss[j % 4]
                    for kc in range(KC):
                        b.op('pe', lambda E, kc=kc, c=c, wt=wt, ps=ps: E.matmul(
                            ps[:, :], lhsT=wt[:, kc, c * 128:(c + 1) * 128], rhs=s_sb[:, kc, :],
                            start=(kc == 0), stop=(kc == KC - 1)), reads=[wk, 's'], writes=[('pm', j % 4)])
                    b.op('dve', lambda E, j=j, ps=ps: E.tensor_scalar(
                        out=o_sb[:, :, j], in0=ps[:, :], scalar1=bm_sb[:, j:j + 1], scalar2=None, op0=ALU.add),
                        reads=[('pm', j % 4), 'bm'], writes=['o'])
        b.dma('sp', C.modsend.ap().rearrange("(q p) j -> p q j", p=128), o_sb[:], reads=['o'], writes=['modsend'])
        b.cc("AllGather", GROUPS4, C.modsend.ap().opt(), C.modall.ap().opt(), reads=['modsend'], writes=['modall'])
        b.flush()
    phase_extract(b, [(C.modall, C.modmine.ap(), 4, 512, 128, DEPTH * NCH, 'b', 'modall', 'modmine')])
    MA = C.modall.ap()
    MM = C.modmine.ap()
    for l in range(DEPTH):
        for i in range(4):
            b.dma('sp', C.modtab[:, 0, l, i * NCH:(i + 1) * NCH], MM[i * 128:(i + 1) * 128, l * NCH:(l + 1) * NCH],
                  reads=['modmine'], writes=['modtab'])
            b.dma('sp', C.modtab[:, 1, l, i * NCH:(i + 1) * NCH], MA[i * 512 + 256:i * 512 + 384, l * NCH:(l + 1) * NCH],
                  reads=['modall'], writes=['modtab'])
    b.flush()


def mod_ap(C, st, layer, m):
    return C.modtab[:, st, layer, m * 16:(m + 1) * 16]


def phase_proj(b, C, layer, x_dram, w, NP):
    T, TL = C.Tc, C.TLc
    tbl = blocks_for(T, TL)
    with ExitStack() as es:
        b.es = es
        nm = NormMod(b, 'n1', C.ones32, C.epsb, C.gains_sb[:, layer, :],
                     [mod_ap(C, 0, layer, 1), mod_ap(C, 1, layer, 1)],
                     [mod_ap(C, 0, layer, 0), mod_ap(C, 1, layer, 0)], ['modtab', 'gains'])
        h = b.sb('h', [128, KC, T], BF16)
        xb = [b.sb('xb%d' % i, [128, KC, 512], F32) for i in range(1)]
        xv = x_dram.rearrange("(kc p) t -> p kc t", p=128)
        for bi, (t0, n, g) in enumerate(tbl):
            xs = xb[0]
            b.dma('sp', xs[:, :, :n], xv[:, :, t0:t0 + n], reads=['xdram'], writes=[('xb', 0)])
            nm.emit(xs[:, :, :n], ('xb', 0), n, g, h[:, :, t0:t0 + n], 'h')
        psums = [(b.ps('pp%d' % i, [128, 512]), ('pp', i)) for i in range(4)]
        ob = [b.sb('ob%d' % i, [128, 512], F32) for i in range(4)]
        pvs = [C.Psend[s_].ap().rearrange("(c p) t -> p c t", p=128) for s_ in range(5)]
        st = {'i': 0}

        def epi(nch, t0, n, g, ps, pk):
            i = st['i'] % 4
            st['i'] += 1
            o = ob[i]
            if i % 2:
                b.op('act', lambda E: E.activation(out=o[:, :n], in_=ps[:, :n], func=AF.Copy),
                     reads=[pk], writes=[('ob', i)])
            else:
                b.op('dve', lambda E: E.tensor_copy(out=o[:, :n], in_=ps[:, :n]),
                     reads=[pk], writes=[('ob', i)])
            b.dma('sp', pvs[nch // 16][:, nch % 16, t0:t0 + n], o[:, :n], reads=[('ob', i)],
                  writes=[('Psend', nch // 16)])

        def after_group(gi):
            if gi % 4 == 3:
                s_ = gi // 4
                gather_rows(b, C.Psend[s_], C.Pall[s_], 0, 2048 // CH, [('Psend', s_)], ('Pall', s_))

        gemm_stream(b, w, D, NP, 512, lambda kc, t0, n: h[:, kc, t0:t0 + n], ['h'], tbl, epi, psums, after_group)
        b.flush()


def phase_l3(b, C, layer, x_dram, wo, w1, w2, out_dram, final):
    T, TL = (C.TLc, C.TLc) if final else (C.Tc, C.TLc)
    tbl = blocks_for(T, TL)
    with ExitStack() as es:
        b.es = es
        nm = NormMod(b, 'n2', C.ones32, C.epsb, C.gains_sb[:, 4 + layer, :],
                     [mod_ap(C, 0, layer, 4), mod_ap(C, 1, layer, 4)],
                     [mod_ap(C, 0, layer, 3), mod_ap(C, 1, layer, 3)], ['modtab', 'gains'])
        if final:
            nf = NormMod.__new__(NormMod)
            nf.__dict__.update(nm.__dict__)
            nf.gain = C.gains_sb[:, 8, :]
        gate = {2: [mod_ap(C, 0, layer, 2), mod_ap(C, 1, layer, 2)],
                5: [mod_ap(C, 0, layer, 5), mod_ap(C, 1, layer, 5)]}
        xs = b.sb('xs', [128, KC, 512], F32)
        hb = b.sb('hb', [128, KC, 512], BF16)
        hid = b.sb('hid', [128, 32, 512], BF16)
        r32 = [b.sb('r32_%d' % i, [128, 512], F32) for i in range(2)]
        psums = [(b.ps('pp%d' % i, [128, 512]), ('pp', i)) for i in range(4)]
        xv = x_dram.rearrange("(kc p) t -> p kc t", p=128)
        mv = C.Mmine.ap().rearrange("(fc hf kk r) t -> hf r kk fc t", fc=4, hf=2, kk=4, r=CH)
        ov = out_dram.rearrange("(kc p) t -> p kc t", p=128)
        for bix, (t0, n, g) in enumerate(tbl):
            cm_ = 'fill' if bix == 0 else 'use'
            b.dma('sp', xs[:, :, :n], xv[:, :, t0:t0 + n], reads=['xdram'], writes=['xs'])
            for half in range(2):
                for kk in range(4):
                    b.dma('pool', hb[half * CH:(half + 1) * CH, kk * 4:(kk + 1) * 4, :n],
                          mv[half][:, kk, :, t0:t0 + n], reads=['Mmine'], writes=['hb'])

            def epi_res(m):
                def epi(nch, t0_, n_, g_, ps, pk):
                    gcol = gate[m][g_][:, nch:nch + 1]
                    b.op('dve', lambda E: E.scalar_tensor_tensor(
                        out=xs[:, nch, :n_], in0=ps[:, :n_], scalar=gcol, in1=xs[:, nch, :n_],
                        op0=ALU.mult, op1=ALU.add), reads=[pk, 'modtab', 'xs'], writes=['xs'])
                return epi
            blk = [(0, n, g)]
            gemm_stream(b, wo, D, D, 512, lambda kc, t0_, n_: hb[:, kc, :n_], ['hb'], blk, epi_res(2), psums,
                        cache=(C.wcache, 0, cm_))
            nm.emit(xs[:, :, :n], 'xs', n, g, hb[:, :, :n], 'hb')
            for half in range(2):
                def epi_h(nch, t0_, n_, g_, ps, pk):
                    ri = b.pi % 2
                    r = r32[ri]
                    b.op('act', lambda E: E.activation(out=r[:, :n_], in_=ps[:, :n_], func=AF.Relu),
                         reads=[pk], writes=[('r32', ri)])
                    b.op('dve', lambda E: E.tensor_tensor(out=hid[:, nch, :n_], in0=r[:, :n_], in1=r[:, :n_],
                                                          op=ALU.mult),
                         reads=[('r32', ri)], writes=[('hid', nch)])
                gemm_stream(b, w1[:, half * 4096:(half + 1) * 4096], D, 4096, 512,
                            lambda kc, t0_, n_: hb[:, kc, :n_], ['hb'], blk, epi_h, psums,
                            cache=(C.wcache, 4 + half * 8, cm_))
                gemm_stream(b, w2[half * 4096:(half + 1) * 4096, :], 4096, D, 256,
                            lambda kc, t0_, n_: hid[:, kc, :n_], [('hid', i) for i in range(32)], blk,
                            epi_res(5), psums, cache=(C.wcache, 20 + half * 8, cm_))
            if final:
                nf.emit(xs[:, :, :n], 'xs', n, g, xs[:, :, :n], 'xs', plain=True)
            b.dma('sp', ov[:, :, t0:t0 + n], xs[:, :, :n], reads=['xs'], writes=['xdram'])
        b.flush()


def phase_scan_b(b, C, j):
    SC, SL = CTX_, SEQ_
    TS = SC + SL
    LP = TS + 6
    NU = 2
    blks = [(0, SC, 0)]
    t = 0
    while t < SL:
        blks.append((SC + 3 + t, 512, SC + t))
        t += 512
    nctx = 1
    PAs = [C.Pmine[s_].ap() for s_ in range(2)]
    TLc, TCc = C.TLc, C.TCc


    with ExitStack() as es:
        b.es = es
        pv_sb = b.sb('pv', [128, NU * 22], F32)
        b.dma('sp', pv_sb[:], C.pvB[j], writes=['pv'])
        one = b.sb('one', [128, 1], F32)
        b.op('dve', lambda E: E.memset(one[:], 1.0), writes=['one'])
        zt = b.sb('zt', [128, 4], F32)
        b.op('dve', lambda E: E.memset(zt[:], 0.0), writes=['zt'])
        cdec = b.sb('cdec', [128, NU * 4], F32)
        for u in range(NU):
            cs = slice(u * 4, u * 4 + 4)
            b.op('act', lambda E, u=u, cs=cs: E.activation(out=cdec[:, cs], in_=pv_sb[:, u * 22 + 18:u * 22 + 22],
                                                          func=AF.Exp, scale=-1.0), reads=['pv'], writes=['cdec'])
        b.op('act', lambda E: E.activation(out=cdec[:], in_=cdec[:], func=AF.Ln, bias=one[:, 0:1], scale=1.0),
             reads=['cdec', 'one'], writes=['cdec'])
        b.op('dve', lambda E: E.tensor_scalar(out=cdec[:], in0=cdec[:], scalar1=-8.0, scalar2=None, op0=ALU.mult),
             reads=['cdec'], writes=['cdec'])
        gwb = b.sb('gwb', [128, NU * 4, 2, 256], BF16)
        b.dma('pool', gwb[:], C.gwB[j].rearrange("g (kc p) n -> p g kc n", p=128), writes=['gwb'])
        UP = C.Upad.ap()
        for u in range(NU):
            for cc in range(2):
                for (c0, w_) in ((0, 2), (SC + 2, 3), (LP - 1, 2)):
                    b.dma('sp', UP[u, cc, :, c0:c0 + w_], zt[:, :w_], reads=['zt'], writes=['Upad'])
                for i in range(4):
                    for half in range(2):
                        ps_ = slice(half * CH, (half + 1) * CH)
                        src_ = mine_rows(PAs[1], i, u * 256 + cc * 128, half)
                        b.dma('sp', UP[u, cc, ps_, 2 + i * TCc:2 + (i + 1) * TCc], src_[:, TLc:TLc + TCc],
                              reads=[('Pmine', 1)], writes=['Upad'])
                        b.dma('sp', UP[u, cc, ps_, SC + 5 + i * TLc:SC + 5 + (i + 1) * TLc], src_[:, 0:TLc],
                              reads=[('Pmine', 1)], writes=['Upad'])
        uc32 = b.sb('uc32', [128, 2, TS], F32)
        uc16 = b.sb('uc16', [128, 2, TS], BF16)
        Hf = b.sb('Hf', [128, TS], F32)
        ub = [b.sb('ub%d' % i, [128, 2, 515], F32) for i in range(2)]
        W = {}
        for nm_ in ['r', 'i', 'a', 'a2', 'xin', 'y', 's', 'mo']:
            W[nm_] = [b.sb('w_%s%d' % (nm_, i), [128, 512], F32) for i in range(2)]
        hbr = [b.sb('hbr%d' % i, [128, 512], F32) for i in range(2)]
        pz = [b.ps('pz%d' % i, [128, 512]) for i in range(4)]
        cnt = {'k': 0}
        MS = C.Msend.ap()
        for u in range(NU):
            pb = u * 22
            for bi, (p0, n, t0) in enumerate(blks):
                ub_ = ub[bi % 2]
                uk = ('ub', bi % 2)
                for cc in range(2):
                    b.dma('sp', ub_[:, cc, :n + 3], UP[u, cc, :, p0:p0 + n + 3], reads=['Upad'], writes=[uk])
                for cc in range(2):
                    dst = uc32[:, cc, t0:t0 + n]
                    b.op('dve', lambda E, ub_=ub_, cc=cc, n=n, dst=dst, pb=pb: E.tensor_scalar(
                        out=dst, in0=ub_[:, cc, 0:n], scalar1=pv_sb[:, pb + cc * 4:pb + cc * 4 + 1],
                        scalar2=pv_sb[:, pb + 8 + cc:pb + 9 + cc], op0=ALU.mult, op1=ALU.add),
                        reads=[uk, 'pv'], writes=[('uc32', cc)])
                    for jj in range(1, 4):
                        b.op('dve', lambda E, ub_=ub_, cc=cc, n=n, jj=jj, dst=dst, pb=pb: E.scalar_tensor_tensor(
                            out=dst, in0=ub_[:, cc, jj:jj + n], scalar=pv_sb[:, pb + cc * 4 + jj:pb + cc * 4 + jj + 1],
                            in1=dst, op0=ALU.mult, op1=ALU.add), reads=[uk, 'pv', ('uc32', cc)], writes=[('uc32', cc)])
                    b.op('act', lambda E, cc=cc, t0=t0, n=n, dst=dst: E.activation(
                        out=uc16[:, cc, t0:t0 + n], in_=dst, func=AF.Copy),
                        reads=[('uc32', cc)], writes=[('uc16', cc)])
            for oc in range(2):
                for d in range(2):
                    if d == 0:
                        order = list(range(len(blks)))
                    else:
                        order = list(range(nctx - 1, -1, -1)) + list(range(len(blks) - 1, nctx - 1, -1))
                    prev = None
                    for bi in order:
                        p0, n, t0 = blks[bi]
                        k = cnt['k'] % 2
                        cnt['k'] += 1
                        zr, zi = pz[2 * k], pz[2 * k + 1]
                        for gi_, zp in ((0, zr), (1, zi)):
                            for kc in range(2):
                                b.op('pe', lambda E, zp=zp, gi_=gi_, kc=kc, t0=t0, n=n, d=d, oc=oc, u=u: E.matmul(
                                    zp[:, :n], lhsT=gwb[:, u * 4 + d * 2 + gi_, kc, oc * 128:(oc + 1) * 128],
                                    rhs=uc16[:, kc, t0:t0 + n], start=(kc == 0), stop=(kc == 1)),
                                    reads=['gwb', ('uc16', 0), ('uc16', 1)], writes=[('pz', 2 * k + gi_)])
                        r, ig, a, a2, xin = W['r'][k], W['i'][k], W['a'][k], W['a2'][k], W['xin'][k]
                        c_r = pb + 10 + (d * 2 + 0) * 2 + oc
                        c_i = pb + 10 + (d * 2 + 1) * 2 + oc
                        br = pv_sb[:, c_r:c_r + 1]
                        bi_ = pv_sb[:, c_i:c_i + 1]
                        cd = cdec[:, u * 4 + d * 2 + oc: u * 4 + d * 2 + oc + 1]
                        b.op('act', lambda E, r=r, zr=zr, n=n, br=br: E.activation(
                            out=r[:, :n], in_=zr[:, :n], func=AF.Sigmoid, bias=br, scale=1.0),
                            reads=[('pz', 2 * k), 'pv'], writes=[('r', k)])
                        b.op('act', lambda E, ig=ig, zi=zi, n=n, bi_=bi_: E.activation(
                            out=ig[:, :n], in_=zi[:, :n], func=AF.Sigmoid, bias=bi_, scale=1.0),
                            reads=[('pz', 2 * k + 1), 'pv'], writes=[('i', k)])
                        b.op('act', lambda E, a=a, r=r, n=n, cd=cd: E.activation(
                            out=a[:, :n], in_=r[:, :n], func=AF.Exp, scale=cd),
                            reads=[('r', k), 'cdec'], writes=[('a', k)])
                        b.op('pool', lambda E, a=a, a2=a2, n=n: E.tensor_tensor(
                            out=a2[:, :n], in0=a[:, :n], in1=a[:, :n], op=ALU.mult),
                            reads=[('a', k)], writes=[('a2', k)])
                        b.op('act', lambda E, a2=a2, n=n: E.activation(
                            out=a2[:, :n], in_=a2[:, :n], func=AF.Sqrt, scale=-1.0, bias=one[:, 0:1]),
                            reads=[('a2', k), 'one'], writes=[('a2', k)])
                        b.op('pool', lambda E, a2=a2, ig=ig, xin=xin, n=n: E.tensor_tensor(
                            out=xin[:, :n], in0=a2[:, :n], in1=ig[:, :n], op=ALU.mult),
                            reads=[('a2', k), ('i', k)], writes=[('xin', k)])
                        b.op('dve', lambda E, xin=xin, n=n, t0=t0, oc=oc: E.tensor_tensor(
                            out=xin[:, :n], in0=xin[:, :n], in1=uc32[:, oc, t0:t0 + n], op=ALU.mult),
                            reads=[('xin', k), ('uc32', oc)], writes=[('xin', k)])
                        if d == 0:
                            init = 0.0 if prev is None else Hf[:, t0 - 1:t0]
                            b.op('dve', lambda E, a=a, xin=xin, n=n, t0=t0, init=init: E.tensor_tensor_scan(
                                out=Hf[:, t0:t0 + n], data0=a[:, :n], data1=xin[:, :n], initial=init,
                                op0=ALU.mult, op1=ALU.add), reads=[('a', k), ('xin', k), 'Hf'], writes=['Hf'])
                            prev = bi
                        else:
                            hb_ = hbr[k]
                            if prev is None:
                                init = 0.0
                                rkeys = []
                            else:
                                pk_, pn_ = prev
                                init = hbr[pk_][:, pn_ - 1:pn_]
                                rkeys = [('hbr', pk_)]
                            b.op('dve', lambda E, a=a, xin=xin, n=n, hb_=hb_, init=init: E.tensor_tensor_scan(
                                out=hb_[:, :n], data0=a[:, :n][:, ::-1], data1=xin[:, :n][:, ::-1], initial=init,
                                op0=ALU.mult, op1=ALU.add), reads=[('a', k), ('xin', k)] + rkeys, writes=[('hbr', k)])
                            prev = (k, n)
                            yb, sb_, mo_ = W['y'][k], W['s'][k], W['mo'][k]
                            for half in range(2):
                                ps_ = slice(half * CH, (half + 1) * CH)
                                if t0 < SC:
                                    for i in range(4):
                                        b.dma('sp', yb[ps_, i * TCc:(i + 1) * TCc],
                                              mine_rows(PAs[0], i, u * 256 + oc * 128, half)[:, TLc:TLc + TCc],
                                              reads=[('Pmine', 0)], writes=[('y', k)])
                                else:
                                    a_ = t0 - SC
                                    i, loc = a_ // TLc, a_ % TLc
                                    b.dma('sp', yb[ps_, :n], mine_rows(PAs[0], i, u * 256 + oc * 128, half)[:, loc:loc + n],
                                          reads=[('Pmine', 0)], writes=[('y', k)])
                            b.op('act', lambda E, yb=yb, n=n: E.activation(
                                out=yb[:, :n], in_=yb[:, :n], func=AF.Gelu_apprx_tanh),
                                reads=[('y', k)], writes=[('y', k)])
                            b.op('dve', lambda E, sb_=sb_, hb_=hb_, n=n, t0=t0: E.tensor_tensor(
                                out=sb_[:, :n], in0=Hf[:, t0:t0 + n], in1=hb_[:, :n][:, ::-1], op=ALU.add),
                                reads=['Hf', ('hbr', k)], writes=[('s', k)])
                            b.op('pool', lambda E, sb_=sb_, yb=yb, mo_=mo_, n=n: E.tensor_tensor(
                                out=mo_[:, :n], in0=sb_[:, :n], in1=yb[:, :n], op=ALU.mult),
                                reads=[('s', k), ('y', k)], writes=[('mo', k)])
                            write_nat(b, MS, u * 256 + oc * 128, t0, n, mo_, [('mo', k)], TLc, TCc)
                    if d == 1:
                        f0_ = u * 256 + oc * 128
                        for dest in range(4):
                            c_ = (dest * 512 + f0_) // CH
                            gather_rows(b, C.Msend, C.Mall, c_, c_ + 2, ['Msend'], 'Mall')
        b.flush()


def phase_scan_a(b, C, j, lbmode):
    SC, SL = CTX_, SEQ_
    TS = SC + SL
    NP_, NS = 4, 8
    NT = TS // 128
    TLc, TCc = C.TLc, C.TCc
    PAs = [C.Pmine[s_].ap() for s_ in range(5)]


    def load_nat(q, dst, n, t0, sec, pi_, wkey):
        for half in range(2):
            ps_ = slice(half * CH, (half + 1) * CH)
            if t0 < SC:
                for i in range(4):
                    b.dma(q, dst[ps_, i * TCc:(i + 1) * TCc], mine_rows(PAs[sec], i, pi_ * 128, half)[:, TLc:TLc + TCc],
                          reads=[('Pmine', sec)], writes=[wkey])
            else:
                a_ = t0 - SC
                i, loc = a_ // TLc, a_ % TLc
                b.dma(q, dst[ps_, :n], mine_rows(PAs[sec], i, pi_ * 128, half)[:, loc:loc + n],
                      reads=[('Pmine', sec)], writes=[wkey])

    nat_blks = [(0, SC)] + [(SC + t, 512) for t in range(0, SL, 512)]
    sblk = {0: nat_blks, 1: [(0, SC)] + [(SC + SL - 512 - t, 512) for t in range(0, SL, 512)]}

    with ExitStack() as es:
        b.es = es
        ones32, epsb = C.ones32, C.epsb
        pv_sb = b.sb('pv', [128, 20], F32)
        b.dma('sp', pv_sb[:], C.pvA[j], writes=['pv'])
        cst_sb = b.sb('cst', [128, 897], F32)
        b.dma('sp', cst_sb[:], C.cst, writes=['cst'])
        ident = b.sb('ident', [128, 128], BF16)
        b.op('dve', lambda E: E.tensor_copy(out=ident[:], in_=cst_sb[:, 0:128]), reads=['cst'], writes=['ident'])
        Jm = b.sb('Jm', [128, 128], BF16)
        b.op('dve', lambda E: E.tensor_copy(out=Jm[:], in_=cst_sb[:, 769:897]), reads=['cst'], writes=['Jm'])
        amask = cst_sb[:, 128:256]
        rmask = cst_sb[:, 256:768]
        lb = b.sb('lb', [128, NS], F32)
        oml = b.sb('oml', [128, NS], F32)
        if lbmode:
            pv3 = pv_sb[:, 0:2 * NS].rearrange("p (s two) -> p s two", two=2)
            b.op('dve', lambda E: E.tensor_tensor(out=lb[:], in0=pv3[:, :, 1], in1=pv3[:, :, 0], op=ALU.subtract),
                 reads=['pv'], writes=['lb'])
            b.op('act', lambda E: E.activation(out=lb[:], in_=lb[:], func=AF.Sigmoid), reads=['lb'], writes=['lb'])
            b.op('dve', lambda E: E.tensor_scalar(out=oml[:], in0=lb[:], scalar1=-1.0, scalar2=1.0,
                                                  op0=ALU.mult, op1=ALU.add), reads=['lb'], writes=['oml'])
        O = [b.sb('O%d' % i, [128, TS], F32) for i in range(2)]
        V = [b.sb('V%d' % i, [128, NT, 128], BF16) for i in range(2)]
        vT = [b.sb('vT%d' % i, [128, 512], BF16) for i in range(2)]
        Wk = {}
        for nm_ in ['q', 'z', 'f', 'lf', 'k', 'bc', 'eb', 'enb', 'k32']:
            Wk[nm_] = [b.sb('a_%s%d' % (nm_, i), [128, 512], F32) for i in range(2)]
        Qt = [b.sb('Qt%d' % i, [128, 512], BF16) for i in range(2)]
        Kt = [b.sb('Kt%d' % i, [128, 512], BF16) for i in range(2)]
        KbT = [b.sb('KbT%d' % i, [128, 512], BF16) for i in range(2)]
        Kb = [b.sb('Kb%d' % i, [128, 128], BF16) for i in range(2)]
        Am = [b.sb('Am%d' % i, [128, 128], BF16) for i in range(2)]
        Kbz = [b.sb('Kbz%d' % i, [128, 128], BF16) for i in range(2)]
        S32 = [b.sb('S32_%d' % i, [128, 128], F32) for i in range(2)]
        S16 = [[b.sb('S16_%d_%d' % (i, c), [128, 128], BF16) for c in range(4)] for i in range(2)]
        pT1 = b.ps('pT', [128, 128], BF16)
        pA1 = b.ps('pA', [128, 512])
        pO = [b.ps('pO%d' % i, [128, 128]) for i in range(2)]
        pS = [[b.ps('pS%d_%d' % (i, c), [128, 128]) for c in range(2)] for i in range(2)]
        fb = {}
        for nm_ in ['s', 'sq', 'rs', 'g', 'mo']:
            fb[nm_] = b.sb('f_%s' % nm_, [128, 512], F32)
        pNb = pA1
        MS = C.Msend.ap()

        for pr in range(NP_):
            for ci, (t0, n) in enumerate(nat_blks):
                vt = vT[ci % 2]
                load_nat('pool', vt, n, t0, 1, pr, ('vT', ci % 2))
                for tt in range(n // 128):
                    ti = (t0 + tt * 128) // 128
                    x_ = ti % 2
                    b.op('pe', lambda E, vt=vt, tt=tt, x_=x_: E.transpose(
                        out=pT1[:, :], in_=vt[:, tt * 128:(tt + 1) * 128], identity=ident[:, :]),
                        reads=[('vT', ci % 2), 'ident'], writes=[('pT', 0)])
                    b.op('act', lambda E, ti=ti, x_=x_: E.activation(out=V[0][:, ti, :], in_=pT1[:, :], func=AF.Copy),
                         reads=[('pT', 0)], writes=[('V', 0)])
                    b.op('pe', lambda E, ti=ti, x_=x_: E.matmul(pS[0][x_][:, :], lhsT=Jm[:, :], rhs=V[0][:, ti, :],
                                                               start=True, stop=True),
                         reads=[('V', 0), 'Jm'], writes=[('pS', 0, x_)])
                    b.op('dve', lambda E, ti=ti, x_=x_: E.tensor_copy(out=V[1][:, ti, :], in_=pS[0][x_][:, :]),
                         reads=[('pS', 0, x_)], writes=[('V', 1)])
            for dr in range(2):
                b.op('dve', lambda E, dr=dr: E.memset(S32[dr][:], 0.0), writes=[('S32', dr)])
                b.op('dve', lambda E, dr=dr: E.memset(S16[dr][3][:], 0.0), writes=[('S16', dr, 3)])
            def emit_ps(dr, ti, c):
                cs = slice(c * 32, (c + 1) * 32)
                psl = pS[dr][c % 2][:, :]
                if c < 3:
                    b.op('pe', lambda E: E.matmul(psl, lhsT=Kb[dr][cs, :], rhs=V[dr][cs, ti, :], start=True, stop=True),
                         reads=[('Kb', dr), ('V', dr)], writes=[('pS', dr, c % 2)])
                else:
                    b.op('pe', lambda E: E.matmul(psl, lhsT=Kbz[dr][64:128, :], rhs=V[dr][64:128, ti, :],
                                                  start=True, stop=True),
                         reads=[('Kbz', dr), ('V', dr)], writes=[('pS', dr, c % 2)])

            for bi in range(len(nat_blks)):
                n = nat_blks[bi][1]
                for dr in range(2):
                    s = 2 * pr + dr
                    nt0 = sblk[dr][bi][0]
                    q, z, f, lf, k, bc, eb, enb, k32 = [Wk[x][dr] for x in
                                                        ['q', 'z', 'f', 'lf', 'k', 'bc', 'eb', 'enb', 'k32']]
                    K_ = lambda x, dr=dr: (x, dr)
                    load_nat('sp', q, n, nt0, 0, pr, K_('q'))
                    load_nat('sp', z, n, nt0, 3 + dr, pr, K_('z'))
                    if dr == 0:
                        zin, qin = z[:, :n], q[:, :n]
                    else:
                        zin, qin = z[:, :n][:, ::-1], q[:, :n][:, ::-1]
                    b.op('act', lambda E, zin=zin, f=f, n=n: E.activation(out=f[:, :n], in_=zin, func=AF.Exp, scale=-1.0),
                         reads=[K_('z')], writes=[K_('f')])
                    b.op('pool', lambda E, f=f, n=n: E.tensor_scalar(out=f[:, :n], in0=f[:, :n], scalar1=1.0, scalar2=None,
                                                                    op0=ALU.add), reads=[K_('f')], writes=[K_('f')])
                    b.op('dve', lambda E, f=f, n=n: E.reciprocal(out=f[:, :n], in_=f[:, :n]),
                         reads=[K_('f')], writes=[K_('f')])
                    if lbmode:
                        b.op('dve', lambda E, f=f, n=n, s=s: E.tensor_scalar(
                            out=f[:, :n], in0=f[:, :n], scalar1=oml[:, s:s + 1], scalar2=lb[:, s:s + 1],
                            op0=ALU.mult, op1=ALU.add), reads=[K_('f'), 'lb', 'oml'], writes=[K_('f')])
                    b.op('act', lambda E, f=f, lf=lf, n=n: E.activation(out=lf[:, :n], in_=f[:, :n], func=AF.Ln),
                         reads=[K_('f')], writes=[K_('lf')])
                    b.op('pool', lambda E, f=f, k=k, n=n: E.tensor_scalar(
                        out=k[:, :n], in0=f[:, :n], scalar1=-1.0, scalar2=1.0, op0=ALU.mult, op1=ALU.add),
                        reads=[K_('f')], writes=[K_('k')])
                    b.op('dve', lambda E, lf=lf, bc=bc, n=n: E.tensor_tensor_scan(
                        out=bc[:, :n], data0=rmask[:, :n], data1=lf[:, :n], initial=0.0, op0=ALU.mult, op1=ALU.add),
                        reads=[K_('lf'), 'cst'], writes=[K_('bc')])
                    b.op('act', lambda E, bc=bc, eb=eb, n=n: E.activation(out=eb[:, :n], in_=bc[:, :n], func=AF.Exp),
                         reads=[K_('bc')], writes=[K_('eb')])
                    b.op('act', lambda E, bc=bc, enb=enb, n=n: E.activation(out=enb[:, :n], in_=bc[:, :n], func=AF.Exp,
                                                                          scale=-1.0),
                         reads=[K_('bc')], writes=[K_('enb')])
                    b.op('pool', lambda E, qin=qin, eb=eb, dr=dr, n=n: E.tensor_tensor(
                        out=Qt[dr][:, :n], in0=qin, in1=eb[:, :n], op=ALU.mult),
                        reads=[K_('q'), K_('eb')], writes=[K_('Qt')])
                    b.op('dve', lambda E, k=k, enb=enb, k32=k32, n=n: E.tensor_tensor(
                        out=k32[:, :n], in0=k[:, :n], in1=enb[:, :n], op=ALU.mult),
                        reads=[K_('k'), K_('enb')], writes=[K_('k32')])
                    b.op('act', lambda E, k32=k32, dr=dr, n=n: E.activation(out=Kt[dr][:, :n], in_=k32[:, :n], func=AF.Copy),
                         reads=[K_('k32')], writes=[K_('Kt')])
                    ebl = eb[:, :n].rearrange("p (c j) -> p c j", j=32)[:, :, 31:32].broadcast_to([128, n // 32, 32])
                    b.op('pool', lambda E, k32=k32, ebl=ebl, dr=dr, n=n: E.tensor_tensor(
                        out=KbT[dr][:, :n].rearrange("p (c j) -> p c j", j=32),
                        in0=k32[:, :n].rearrange("p (c j) -> p c j", j=32), in1=ebl, op=ALU.mult),
                        reads=[K_('k32'), K_('eb')], writes=[K_('KbT')])
                for tt in range(n // 128):
                    info = {}
                    for dr in range(2):
                        K_ = lambda x, dr=dr: (x, dr)
                        c0 = tt * 128
                        nt0 = sblk[dr][bi][0]
                        if dr == 0:
                            ti = (nt0 + c0) // 128
                            st0 = nt0 + c0
                        else:
                            ti = (nt0 + n - c0 - 128) // 128
                            st0 = (0 if bi == 0 else SC + (bi - 1) * 512) + c0
                        info[dr] = (ti, st0)
                        b.op('pe', lambda E, dr=dr, c0=c0: E.transpose(out=pT1[:, :], in_=KbT[dr][:, c0:c0 + 128],
                                                                       identity=ident[:, :]),
                             reads=[K_('KbT'), 'ident'], writes=[('pT', 0)])
                        b.op('act', lambda E, dr=dr: E.activation(out=Kb[dr][:, :], in_=pT1[:, :], func=AF.Copy),
                             reads=[('pT', 0)], writes=[K_('Kb')])
                        b.op('pool', lambda E, dr=dr: E.tensor_scalar(
                            out=Kbz[dr][64:128, :], in0=Kb[dr][64:128, :], scalar1=cst_sb[64:128, 768:769], scalar2=None,
                            op0=ALU.mult), reads=[K_('Kb'), 'cst'], writes=[K_('Kbz')])
                        b.op('pe', lambda E, dr=dr, c0=c0: E.matmul(pA1[:, 0:128], lhsT=Kt[dr][:, c0:c0 + 128],
                                                                    rhs=Qt[dr][:, c0:c0 + 128], start=True, stop=True),
                             reads=[K_('Kt'), K_('Qt')], writes=[('pA', 0)])
                        b.op('dve', lambda E, dr=dr: E.tensor_tensor(out=Am[dr][:, :], in0=pA1[:, 0:128], in1=amask,
                                                                     op=ALU.mult),
                             reads=[('pA', 0), 'cst'], writes=[K_('Am')])
                        b.op('pe', lambda E, dr=dr, ti=ti: E.matmul(pO[dr][:, :], lhsT=V[dr][:, ti, :], rhs=Am[dr][:, :],
                                                                    start=True, stop=False),
                             reads=[('V', dr), K_('Am')], writes=[K_('pO')])
                        for c in range(2):
                            emit_ps(dr, ti, c)
                    for dr in range(2):
                        K_ = lambda x, dr=dr: (x, dr)
                        eb = Wk['eb'][dr]
                        c0 = tt * 128
                        ti, st0 = info[dr]
                        for c in range(4):
                            cs = slice(c * 32, (c + 1) * 32)
                            psl = pS[dr][c % 2][:, :]
                            sprev = S16[dr][(c - 1) % 4]
                            b.op('pe', lambda E, dr=dr, c0=c0, c=c, cs=cs, sprev=sprev: E.matmul(
                                pO[dr][:, cs], lhsT=sprev[:, :], rhs=Qt[dr][:, c0 + c * 32:c0 + (c + 1) * 32],
                                start=False, stop=(c == 3)),
                                reads=[('S16', dr, (c - 1) % 4), K_('Qt')], writes=[K_('pO')])
                            b.op('dve', lambda E, dr=dr, eb=eb, c0=c0, c=c, psl=psl: E.scalar_tensor_tensor(
                                out=S32[dr][:, :], in0=S32[dr][:, :], scalar=eb[:, c0 + c * 32 + 31:c0 + c * 32 + 32],
                                in1=psl, op0=ALU.mult, op1=ALU.add),
                                reads=[('S32', dr), K_('eb'), ('pS', dr, c % 2)], writes=[('S32', dr)])
                            b.op('dve', lambda E, dr=dr, c=c: E.tensor_copy(out=S16[dr][c][:, :], in_=S32[dr][:, :]),
                                 reads=[('S32', dr)], writes=[('S16', dr, c)])
                            if c + 2 < 4:
                                emit_ps(dr, ti, c + 2)
                        b.op('act', lambda E, dr=dr, st0=st0: E.activation(
                            out=O[dr][:, st0:st0 + 128], in_=pO[dr][:, :], func=AF.Copy),
                            reads=[K_('pO')], writes=[('O', dr)])
            for (t0, n) in nat_blks:
                lo = 0 if t0 < SC else SC + SL - (t0 - SC) - n
                sb_, sq, rs, g, mo = fb['s'], fb['sq'], fb['rs'], fb['g'], fb['mo']
                load_nat('sp', g, n, t0, 2, pr, 'fg')
                b.op('dve', lambda E, t0=t0, n=n, lo=lo: E.tensor_tensor(
                    out=sb_[:, :n], in0=O[0][:, t0:t0 + n], in1=O[1][:, lo:lo + n][:, ::-1], op=ALU.add),
                    reads=[('O', 0), ('O', 1)], writes=['fs'])
                b.op('act', lambda E, n=n: E.activation(out=sq[:, :n], in_=sb_[:, :n], func=AF.Square),
                     reads=['fs'], writes=['fsq'])
                b.op('pe', lambda E, n=n: E.matmul(pNb[:, :n], lhsT=ones32[:, :], rhs=sq[:, :n], start=True, stop=True),
                     reads=['fsq', 'ones32'], writes=[('pA', 0)])
                b.op('act', lambda E, n=n: E.activation(out=rs[:, :n], in_=pNb[:, :n], func=AF.Sqrt,
                                                        scale=1.0 / 128, bias=epsb[:, 0:1]),
                     reads=[('pA', 0), 'epsb'], writes=['frs'])
                b.op('dve', lambda E, n=n: E.reciprocal(out=rs[:, :n], in_=rs[:, :n]), reads=['frs'], writes=['frs'])
                b.op('act', lambda E, n=n: E.activation(out=g[:, :n], in_=g[:, :n], func=AF.Silu),
                     reads=['fg'], writes=['fg'])
                b.op('dve', lambda E, n=n, pr=pr: E.scalar_tensor_tensor(
                    out=sb_[:, :n], in0=sb_[:, :n], scalar=pv_sb[:, 16 + pr:17 + pr], in1=rs[:, :n],
                    op0=ALU.mult, op1=ALU.mult), reads=['fs', 'frs', 'pv'], writes=['fs'])
                b.op('pool', lambda E, n=n: E.tensor_tensor(out=mo[:, :n], in0=sb_[:, :n], in1=g[:, :n], op=ALU.mult),
                     reads=['fs', 'fg'], writes=['fmo'])
                write_nat(b, MS, pr * 128, t0, n, mo, ['fmo'], TLc, TCc)
            for dest in range(4):
                c_ = (dest * 512 + pr * 128) // CH
                gather_rows(b, C.Msend, C.Mall, c_, c_ + 2, ['Msend'], 'Mall')
        b.flush()


def _dbg_dump(b, C, oT, src_ap, rkey):
    b.dma('sp', oT, src_ap, reads=[rkey], writes=['out'])
    b.flush()


def build_fused():
    nc = bass.Bass("TRN2", target_bir_lowering=False)
    C = Ctx()
    C.TLc = SEQ_ * B_ // NCORES
    C.TCc = CTX_ * B_ // NCORES
    C.Tc = C.TLc + C.TCc
    TS = CTX_ + SEQ_

    def inp(name, shape):
        return nc.dram_tensor(name, shape, F32, kind="ExternalInput").ap()
    xT = inp("xT", [D, C.Tc])
    C.cT = inp("cT", [D, 4])
    C.wm = inp("wm", [DEPTH * D, 3072])
    C.bm = inp("bm", [128, DEPTH * 24])
    C.gains = inp("gains", [128, 9, KC])
    NA, NB_ = (NLAYERS + 1) // 2, NLAYERS // 2
    if DBG_STOP == 'mod':
        NA, NB_ = 0, 0
    awin = [inp("awin%d" % j, [D, 5 * D]) for j in range(NA)]
    awout = [inp("awout%d" % j, [D, D]) for j in range(0 if DBG_STOP else NA)]
    bwin = [inp("bwin%d" % j, [D, 2 * D]) for j in range(NB_)]
    bwout = [inp("bwout%d" % j, [D, D]) for j in range(NB_)]
    NW = 0 if DBG_STOP else NLAYERS
    w1 = [inp("w1_%d" % l, [D, DFF]) for l in range(NW)]
    w2 = [inp("w2_%d" % l, [DFF, D]) for l in range(NW)]
    C.pvA = inp("pvA", [2, 128, 20])
    C.cst = inp("cst", [128, 897])
    C.pvB = inp("pvB", [2, 128, 44])
    C.gwB = inp("gwB", [2, 8, 256, 256])
    oT = nc.dram_tensor("oT", [D, C.TLc], F32, kind="ExternalOutput").ap()
    C.modsend = nc.dram_tensor("modsend", [512, DEPTH * 24], F32)
    C.modall = nc.dram_tensor("modall", [4 * 512, DEPTH * 24], F32)
    C.modmine = nc.dram_tensor("modmine", [4 * 128, DEPTH * 24], F32)
    C.xbuf = nc.dram_tensor("xbuf", [D, C.Tc], F32)
    C.Psend = [nc.dram_tensor("Psend%d" % s_, [D, C.Tc], F32) for s_ in range(5)]
    C.Pall = [nc.dram_tensor("Pall%d" % s_, [4 * D, C.Tc], F32) for s_ in range(5)]
    C.Pmine = [nc.dram_tensor("Pmine%d" % s_, [D, C.Tc], F32) for s_ in range(5)]
    C.Msend = nc.dram_tensor("Msend", [4 * 512, C.Tc], F32)
    C.Mall = nc.dram_tensor("Mall", [4 * D, C.Tc], F32)
    C.Mmine = nc.dram_tensor("Mmine", [D, C.Tc], F32)
    C.Upad = nc.dram_tensor("Upad", [2, 2, 128, TS + 7], F32)
    C.wcache = nc.dram_tensor("wcache", [36, 128, 8192], BF16).ap()
    with ExitStack() as es:
        b = Bld(nc, es)
        C.ones32 = b.sb('ones32', [128, 128], F32)
        b.op('dve', lambda E: E.memset(C.ones32[:], 1.0), writes=['ones32'])
        C.epsb = b.sb('epsb', [128, 1], F32)
        b.op('dve', lambda E: E.memset(C.epsb[:], EPS), writes=['epsb'])
        C.modtab = b.sb('modtab', [128, 2, DEPTH, 96], F32)
        C.gains_sb = b.sb('gains', [128, 9, KC], F32)
        phase_mod(b, C)
        if DBG_STOP == 'mod':
            with ExitStack() as es2:
                b.es = es2
                t_ = b.sb('dbg', [128, 2 * DEPTH * 96], F32)
                b.op('dve', lambda E: E.tensor_copy(out=t_[:], in_=C.modtab[:].rearrange("p a l g -> p (a l g)")),
                     reads=['modtab'], writes=['dbg'])
                _dbg_dump(b, C, oT[0:256, 0:384].rearrange("(a p) n -> p a n", p=128), t_[:].rearrange("p (a n) -> p a n", a=2), 'dbg')
            return nc
        for layer in range(NLAYERS):
            j = layer // 2
            final = (layer == NLAYERS - 1)
            x_in = xT if layer == 0 else C.xbuf.ap()
            if layer % 2 == 0:
                phase_proj(b, C, layer, x_in, awin[j], 5 * D)
                if DBG_STOP == 'proj':
                    _dbg_dump(b, C, oT[:, 0:C.TLc], C.Pall[0].ap()[4096:6144, 0:C.TLc], ('Pall', 0))
                    return nc
                phase_extract(b, [(C.Pall[s_], C.Pmine[s_].ap(), 1, 2048, 2048, C.Tc, 'k', ('Pall', s_), ('Pmine', s_))
                                  for s_ in range(5)])
                if DBG_STOP == 'extract':
                    _dbg_dump(b, C, oT[:, 0:C.TLc], C.Pmine[3].ap()[:, 0:C.TLc], ('Pmine', 3))
                    return nc
                phase_scan_a(b, C, j, 1 if j > 0 else 0)
                if DBG_STOP == 'scan':
                    _dbg_dump(b, C, oT[:, 0:C.TLc], C.Mall.ap()[2048:4096, 0:C.TLc], 'Mall')
                    return nc
                wo = awout[j]
            else:
                phase_proj(b, C, layer, x_in, bwin[j], 2 * D)
                phase_extract(b, [(C.Pall[s_], C.Pmine[s_].ap(), 1, 2048, 2048, C.Tc, 'k', ('Pall', s_), ('Pmine', s_))
                                  for s_ in (0, 1)])
                phase_scan_b(b, C, j)
                wo = bwout[j]
            phase_extract(b, [(C.Mall, C.Mmine.ap(), 1, 2048, 2048, C.Tc, 'k', 'Mall', 'Mmine')])
            phase_l3(b, C, layer, x_in, wo, w1[layer], w2[layer], oT if final else C.xbuf.ap(), final)
    return nc


_NC = {}


def _f32(a):
    return np.ascontiguousarray(a, dtype=np.float32)


def _pl(v):
    return np.asarray(v, dtype=np.float32).reshape(KC, 128).T


def _consts_a():
    ident = np.eye(128, dtype=np.float32)
    s_ = np.arange(128)[:, None]
    t_ = np.arange(128)[None, :]
    am = ((s_ <= t_) & (s_ // 32 == t_ // 32)).astype(np.float32)
    rm = np.ones((128, 512), np.float32)
    rm[:, ::32] = 0
    m96 = (np.arange(128) >= 96).astype(np.float32)[:, None]
    Jm = np.ascontiguousarray(ident[::-1])
    return np.ascontiguousarray(np.concatenate([ident, am, rm, m96, Jm], axis=1))


def make_in_maps(x, c, ctx, c_ctx, w_mod, b_mod, norm1, norm2, a_w_in, a_lb_logits, a_onorm, a_w_out,
                 b_w_in, b_conv_w, b_conv_b, b_gate_w, b_gate_b, b_lambda, b_w_out, mlp_w1, mlp_w2, final_norm):
    TLc = SEQ_ * B_ // NCORES
    TCc = CTX_ * B_ // NCORES
    x = np.asarray(x, np.float32)
    ctx = np.asarray(ctx, np.float32)
    latf = x.reshape(B_ * SEQ_, D)
    ctxf = ctx.reshape(B_ * CTX_, D)
    cT = np.zeros((D, 4), np.float32)
    cT[:, 0] = c[0]
    cT[:, 1] = c[1]
    cT[:, 2] = c_ctx
    gains = np.zeros((128, 9, KC), np.float32)
    for l in range(NLAYERS):
        gains[:, l] = _pl(norm1[l])
        gains[:, 4 + l] = _pl(norm2[l])
    gains[:, 8] = _pl(final_norm)
    cst = _consts_a()
    shared = {"cT": cT, "gains": _f32(gains), "cst": cst}
    for j in range((NLAYERS + 1) // 2):
        shared["awin%d" % j] = _f32(a_w_in[j])
        shared["awout%d" % j] = _f32(a_w_out[j])
    for j in range(NLAYERS // 2):
        shared["bwin%d" % j] = _f32(b_w_in[j])
        shared["bwout%d" % j] = _f32(b_w_out[j])
    for l in range(NLAYERS):
        shared["w1_%d" % l] = _f32(mlp_w1[l])
        shared["w2_%d" % l] = _f32(mlp_w2[l])
    ims = []
    for r in range(NCORES):
        k = r % 4
        m = dict(shared)
        m["xT"] = _f32(np.concatenate([latf[r * TLc:(r + 1) * TLc], ctxf[r * TCc:(r + 1) * TCc]], axis=0).T)
        m["wm"] = _f32(np.concatenate([w_mod[l][:, k * 3072:(k + 1) * 3072] for l in range(DEPTH)], axis=0))
        m["bm"] = _f32(np.concatenate([np.asarray(b_mod[l][k * 3072:(k + 1) * 3072]).reshape(24, 128).T
                                       for l in range(DEPTH)], axis=1))
        pvA = []
        for j in range(2):
            cols = []
            for pi_ in range(4):
                hd = 4 * k + pi_
                sl = slice(hd * 128, (hd + 1) * 128)
                for d_ in range(2):
                    cols += [a_lb_logits[0, d_, sl], a_lb_logits[j, d_, sl]]
            for pi_ in range(4):
                hd = 4 * k + pi_
                cols.append(a_onorm[j][hd * 128:(hd + 1) * 128])
            pvA.append(np.stack(cols, axis=1))
        m["pvA"] = _f32(np.stack(pvA))
        pvB, gwB = [], []
        for j in range(2):
            cols, gws = [], []
            for u in range(2):
                blk = 2 * k + u
                sl = slice(blk * 256, (blk + 1) * 256)

                def h2(v_):
                    return np.asarray(v_[sl], np.float32).reshape(2, 128)
                cols += [h2(b_conv_w[j][jj])[cc] for cc in range(2) for jj in range(4)] + \
                        [h2(b_conv_b[j])[cc] for cc in range(2)] + \
                        [h2(b_gate_b[j][d_, gi_])[cc] for d_ in range(2) for gi_ in range(2) for cc in range(2)] + \
                        [h2(b_lambda[j][d_])[cc] for d_ in range(2) for cc in range(2)]
                gws += [b_gate_w[j][d_, gi_, blk] for d_ in range(2) for gi_ in range(2)]
            pvB.append(np.stack(cols, axis=1))
            gwB.append(np.stack(gws))
        m["pvB"] = _f32(np.stack(pvB))
        m["gwB"] = _f32(np.stack(gwB))
        ims.append(m)
    return ims


def kernel(**inputs):
    if 'nc' not in _NC:
        _NC['nc'] = build_fused()
    nc = _NC['nc']
    ims = make_in_maps(**inputs)
    res = run_bass_kernel_spmd(nc, ims, core_ids=list(range(NCORES))).results
    out = np.concatenate([res[r]["oT"].T for r in range(NCORES)], axis=0).reshape(B_, SEQ_, D)
    return np.ascontiguousarray(out, dtype=np.float32)
```

```python
import os
from contextlib import ExitStack
import numpy as np
import concourse.bass as bass
import concourse.mybir as mybir
from concourse.bass import ds
from concourse.bass_utils import run_bass_kernel_spmd

F32 = mybir.dt.float32
BF16 = mybir.dt.bfloat16
AF = mybir.ActivationFunctionType
ALU = mybir.AluOpType

D = 2048
KC = 16
DFF = 8192
EPS = 1e-6
NCORES = 8
B_, SEQ_, CTX_ = 2, 8192, 256
DEPTH = 4
NLAYERS = int(os.environ.get('KDBG_NLAYERS', '4'))
DBG_MOD = ''
DBG_STOP = os.environ.get('KDBG_STOP') or None
GROUPS4 = [[0, 1, 2, 3], [4, 5, 6, 7]]


class Bld:
    ENG = ['pe', 'act', 'dve', 'pool', 'sp']
    NDS = 8

    def __init__(self, nc, es):
        self.nc = nc
        self.es = es
        self.q = {e: [] for e in self.ENG}
        self.cnt = {e: 0 for e in self.ENG + ['cc']}
        self.sem = {e: es.enter_context(nc.semaphore('s_' + e)) for e in ['pe', 'act', 'dve', 'pool', 'cc']}
        self.dsem = {e: [es.enter_context(nc.semaphore('d_%s%d' % (e, i))) for i in range(self.NDS)]
                     for e in ['sp', 'pool', 'act']}
        self.dcnt = {e: 0 for e in ['sp', 'pool', 'act']}
        self.lastw = {}
        self.readers = {}
        self.waited = {e: {} for e in self.ENG}
        self.uid = 0

    def sb(self, name, shape, dtype):
        self.uid += 1
        return self.es.enter_context(self.nc.sbuf_tensor('sb%d_%s' % (self.uid, name), shape, dtype))

    def ps(self, name, shape, dtype=F32):
        self.uid += 1
        return self.es.enter_context(self.nc.psum_tensor('ps%d_%s' % (self.uid, name), shape, dtype))

    def _wait(self, eng, tok):
        kind, e2, n = tok
        if kind == 'c':
            if e2 == eng and eng == 'pe':
                return
            sem = self.sem[e2]
            val = n
            key = ('c', e2)
        else:
            sem = self.dsem[e2][n % self.NDS]
            val = 16 * (n // self.NDS + 1)
            key = ('d', e2, n % self.NDS)
        if self.waited[eng].get(key, 0) >= val:
            return
        self.waited[eng][key] = val
        self.q[eng].append(lambda E, sem=sem, val=val: E.wait_ge(sem, val))

    def _deps(self, eng, reads, writes):
        toks = set()
        for k in reads:
            if k in self.lastw:
                toks.add(self.lastw[k])
        for k in writes:
            if k in self.lastw:
                toks.add(self.lastw[k])
            for t in self.readers.get(k, {}).values():
                if t[0] == 'c' and t[1] == eng:
                    continue
                toks.add(t)
        for t in toks:
            self._wait(eng, t)

    def _commit(self, tok, reads, writes):
        for k in writes:
            self.lastw[k] = tok
            self.readers[k] = {}
        for k in reads:
            r = self.readers.setdefault(k, {})
            if tok[0] == 'c':
                r[('c', tok[1])] = tok
            else:
                r[tok] = tok

    def op(self, eng, fn, reads=(), writes=()):
        self._deps(eng, reads, writes)
        self.cnt[eng] += 1
        n = self.cnt[eng]
        sem = self.sem[eng]
        self.q[eng].append(lambda E, fn=fn, sem=sem: fn(E).then_inc(sem, 1))
        self._commit(('c', eng, n), reads, writes)

    def dma(self, qe, out, in_, reads=(), writes=()):
        i = self.dcnt[qe]
        self.dcnt[qe] += 1
        if i >= self.NDS:
            self._wait(qe, ('d', qe, i - self.NDS))
        self._deps(qe, reads, writes)
        sem = self.dsem[qe][i % self.NDS]

        def f(E, out=out, in_=in_, sem=sem):
            o = out(E) if callable(out) else out
            s = in_(E) if callable(in_) else in_
            try:
                E.dma_start(out=o, in_=s).then_inc(sem, 16)
            except Exception:
                print("DMA build failed: out=", o, " in=", s)
                raise
        self.q[qe].append(f)
        self._commit(('d', qe, i), reads, writes)

    def cc(self, kind, groups, in_ap, out_ap, reads=(), writes=()):
        self._deps('pool', reads, writes)
        self.cnt['cc'] += 1
        n = self.cnt['cc']
        sem = self.sem['cc']
        self.q['pool'].append(lambda E: E.collective_compute(
            kind, ALU.bypass, replica_groups=groups, ins=[in_ap], outs=[out_ap]).then_inc(sem, 1))
        self._commit(('c', 'cc', n), reads, writes)

    def barrier(self):
        for e in self.ENG:
            for e2 in ['pe', 'act', 'dve', 'pool', 'cc']:
                if self.cnt[e2] > 0:
                    self._wait(e, ('c', e2, self.cnt[e2]))
            for qe in ['sp', 'pool', 'act']:
                for i in range(max(0, self.dcnt[qe] - self.NDS), self.dcnt[qe]):
                    self._wait(e, ('d', qe, i))

    def flush(self):
        self.barrier()
        _PID.clear()
        q = self.q
        with self.nc.Block() as block:
            @block.tensor
            def _(E):
                for f in q['pe']:
                    f(E)

            @block.scalar
            def _(E):
                for f in q['act']:
                    f(E)

            @block.vector
            def _(E):
                for f in q['dve']:
                    f(E)

            @block.gpsimd
            def _(E):
                for f in q['pool']:
                    f(E)

            @block.sync
            def _(E):
                for f in q['sp']:
                    f(E)
        self.q = {e: [] for e in self.ENG}
        for a in ('wbuf',):
            if hasattr(self, a):
                delattr(self, a)


_PID = {}
_XQ = {'i': 0}


def _pid(E):
    key = id(E)
    if key not in _PID:
        _PID[key] = {'p': E.partition_id()}
    return _PID[key]


def phase_extract(b, items):
    qe = ['sp', 'act'][_XQ['i'] % 2]
    _XQ['i'] += 1
    for (gat, mine, nsrc, rps, rm, T, which, kin, kout) in items:
        def src(E, gat=gat, nsrc=nsrc, rps=rps, rm=rm, T=T, which=which):
            d = _pid(E)
            bk = (which, rm * T)
            if bk not in d:
                idv = (d['p'] % 4) if which == 'k' else (d['p'] // 4)
                d[bk] = E.compute_val(idv * (rm * T))
            return bass.AP(tensor=gat, offset=d[bk], ap=[[rps * T, nsrc], [T, rm], [1, T]])
        b.dma(qe, mine.rearrange("(i r) t -> i r t", i=nsrc), src, reads=[kin], writes=[kout])
    b.flush()


CH = 64


def gather_rows(b, send, gall, c0, c1, rkeys, wkey):
    for c in range(c0, c1):
        b.cc("AllGather", GROUPS4, send.ap()[c * CH:(c + 1) * CH, :], gall.ap()[c * 4 * CH:(c + 1) * 4 * CH, :],
             reads=rkeys, writes=[wkey])


def mine_rows(ap2d, i, lo, half):
    r0 = ((lo // CH + half) * 4 + i) * CH
    return ap2d[r0:r0 + CH, :]


def write_nat(b, MS, f0, t0, n, src, rkeys, TLc, TCc):
    if t0 < CTX_:
        for i in range(4):
            b.dma('sp', MS[i * 512 + f0:i * 512 + f0 + 128, TLc:TLc + TCc], src[:, i * TCc:(i + 1) * TCc],
                  reads=rkeys, writes=['Msend'])
    else:
        a_ = t0 - CTX_
        i, loc = a_ // TLc, a_ % TLc
        b.dma('sp', MS[i * 512 + f0:i * 512 + f0 + 128, loc:loc + n], src[:, :n], reads=rkeys, writes=['Msend'])


def blocks_for(T, TL):
    bl = []
    t = 0
    while t < TL:
        n = min(512, TL - t)
        bl.append((t, n, 0))
        t += n
    while t < T:
        n = min(512, T - t)
        bl.append((t, n, 1))
        t += n
    return bl


class NormMod:
    def __init__(self, b, tag, ones32, epsb, gain, sc, sh, pkeys):
        self.b = b
        self.tag = tag
        self.ones32 = ones32
        self.epsb = epsb
        self.gain = gain
        self.sh = sh
        self.pkeys = list(pkeys)
        self.sq = b.sb('sq_' + tag, [128, KC, 512], F32)
        self.rs = b.sb('rs_' + tag, [128, 512], F32)
        self.pss = b.ps('pss_' + tag, [128, 512])
        self.gm = b.sb('gm_' + tag, [128, 2, KC], F32)
        for g in range(2):
            b.op('dve', lambda E, g=g: E.scalar_tensor_tensor(
                out=self.gm[:, g, :], in0=sc[g], scalar=1.0, in1=gain, op0=ALU.add, op1=ALU.mult),
                reads=self.pkeys, writes=['gm_' + tag])

    def emit(self, xs, xkey, n, g, hout, hkey, plain=False):
        b = self.b
        tag = self.tag
        sq, rs, pss = self.sq, self.rs, self.pss
        tmp = sq
        b.op('act', lambda E: E.activation(out=sq[:, :, :n], in_=xs, func=AF.Square),
             reads=[xkey], writes=[('sq_' + tag, kc) for kc in range(KC)])
        for kc in range(KC):
            b.op('pe', lambda E, kc=kc: E.matmul(pss[:, :n], lhsT=self.ones32[:, :], rhs=sq[:, kc, :n],
                                                 start=(kc == 0), stop=(kc == KC - 1)),
                 reads=[('sq_' + tag, kc), 'ones32'], writes=['pss_' + tag])
        b.op('act', lambda E: E.activation(out=rs[:, :n], in_=pss[:, :n], func=AF.Sqrt,
                                           scale=1.0 / D, bias=self.epsb[:, 0:1]),
             reads=['pss_' + tag, 'epsb'], writes=['rs_' + tag])
        b.op('dve', lambda E: E.reciprocal(out=rs[:, :n], in_=rs[:, :n]),
             reads=['rs_' + tag], writes=['rs_' + tag])
        for kc in range(KC):
            if plain:
                gcol = self.gain[:, kc:kc + 1]
                b.op('dve', lambda E, kc=kc, gcol=gcol: E.scalar_tensor_tensor(
                    out=hout[:, kc, :], in0=xs[:, kc, :], scalar=gcol, in1=rs[:, :n],
                    op0=ALU.mult, op1=ALU.mult),
                    reads=[xkey, 'rs_' + tag] + self.pkeys, writes=[hkey])
                continue
            gcol = self.gm[:, g, kc:kc + 1]
            shcol = self.sh[g][:, kc:kc + 1]
            b.op('dve', lambda E, kc=kc, gcol=gcol: E.scalar_tensor_tensor(
                out=tmp[:, kc, :n], in0=xs[:, kc, :], scalar=gcol, in1=rs[:, :n],
                op0=ALU.mult, op1=ALU.mult),
                reads=[xkey, 'rs_' + tag, 'gm_' + tag], writes=[('sq_' + tag, kc)])
            b.op('act', lambda E, kc=kc, shcol=shcol: E.activation(
                out=hout[:, kc, :], in_=tmp[:, kc, :n], func=AF.Identity, bias=shcol, scale=1.0),
                reads=[('sq_' + tag, kc)] + self.pkeys, writes=[hkey])


def gemm_stream(b, w_dram, K, N, ngrp, rhs_fn, rhs_keys, tblocks, epi, psums, after_group=None, cache=None):
    kcs = K // 128
    if not hasattr(b, 'wbuf'):
        b.wbuf = [b.sb('wbuf%d' % i, [128, 8192], BF16) for i in range(2)]
        b.wi = 0
        b.pi = 0
    wv = w_dram.rearrange("(kc p) n -> p kc n", p=128)
    for gi in range(N // ngrp):
        wflat = b.wbuf[b.wi % 2]
        wt = wflat[:, :kcs * ngrp].rearrange("p (k n) -> p k n", k=kcs)
        wk = ('wbuf', b.wi % 2)
        b.wi += 1
        if cache is not None and cache[2] == 'use':
            ck = ('wcache', cache[1] + gi)
            b.dma('sp', wflat[:, :], cache[0][cache[1] + gi], reads=[ck], writes=[wk])
        else:
            b.dma('pool', wt[:], wv[:, :, gi * ngrp:(gi + 1) * ngrp], writes=[wk])
            if cache is not None:
                ck = ('wcache', cache[1] + gi)
                b.dma('sp', cache[0][cache[1] + gi], wflat[:, :], reads=[wk], writes=[ck])
        for (t0, n, g) in tblocks:
            for c in range(ngrp // 128):
                ps, pk = psums[b.pi % len(psums)]
                b.pi += 1
                for kc in range(kcs):
                    b.op('pe', lambda E, kc=kc, c=c, ps=ps, wt=wt, t0=t0, n=n: E.matmul(
                        ps[:, :n], lhsT=wt[:, kc, c * 128:(c + 1) * 128], rhs=rhs_fn(kc, t0, n),
                        start=(kc == 0), stop=(kc == kcs - 1)),
                        reads=[wk] + rhs_keys, writes=[pk])
                epi(gi * (ngrp // 128) + c, t0, n, g, ps, pk)
        if after_group is not None:
            after_group(gi)


class Ctx:
    pass


def phase_mod(b, C):
    NCH = 24
    with ExitStack() as es:
        b.es = es
        c_sb = b.sb('c', [128, KC, 4], F32)
        s_sb = b.sb('s', [128, KC, 4], F32)
        bm_sb = b.sb('bm', [128, DEPTH * NCH], F32)
        o_sb = b.sb('o', [128, 4, DEPTH * NCH], F32)
        b.dma('sp', c_sb[:], C.cT.rearrange("(kc p) n -> p kc n", p=128), writes=['c'])
        b.dma('sp', bm_sb[:], C.bm, writes=['bm'])
        b.dma('sp', C.gains_sb[:], C.gains, writes=['gains'])
        b.op('act', lambda E: E.activation(out=s_sb[:], in_=c_sb[:], func=AF.Silu), reads=['c'], writes=['s'])
        wb = [b.sb('wm%d' % i, [128, KC, 512], F32) for i in range(2)]
        pss = [b.ps('pm%d' % i, [128, 4]) for i in range(4)]
        wv = C.wm.rearrange("(l kc p) n -> p l kc n", p=128, kc=KC)
        gi = 0
        for l in range(DEPTH):
            for gq in range(NCH // 4):
                wt = wb[gi % 2]
                wk = ('wm', gi % 2)
                gi += 1
                b.dma('sp', wt[:], wv[:, l, :, gq * 512:(gq + 1) * 512], writes=[wk])
                for c in range(4):
                    j = l * NCH + gq * 4 + c
                    ps = pss[j % 4]
                    for kc in range(KC):
                        b.op('pe', lambda E, kc=kc, c=c, wt=wt, ps=ps: E.matmul(
                            ps[:, :], lhsT=wt[:, kc, c * 128:(c + 1) * 128], rhs=s_sb[:, kc, :],
                            start=(kc == 0), stop=(kc == KC - 1)), reads=[wk, 's'], writes=[('pm', j % 4)])
                    b.op('dve', lambda E, j=j, ps=ps: E.tensor_scalar(
                        out=o_sb[:, :, j], in0=ps[:, :], scalar1=bm_sb[:, j:j + 1], scalar2=None, op0=ALU.add),
                        reads=[('pm', j % 4), 'bm'], writes=['o'])
        b.dma('sp', C.modsend.ap().rearrange("(q p) j -> p q j", p=128), o_sb[:], reads=['o'], writes=['modsend'])
        b.cc("AllGather", GROUPS4, C.modsend.ap().opt(), C.modall.ap().opt(), reads=['modsend'], writes=['modall'])
        b.flush()
    phase_extract(b, [(C.modall, C.modmine.ap(), 4, 512, 128, DEPTH * NCH, 'b', 'modall', 'modmine')])
    MA = C.modall.ap()
    MM = C.modmine.ap()
    for l in range(DEPTH):
        for i in range(4):
            b.dma('sp', C.modtab[:, 0, l, i * NCH:(i + 1) * NCH], MM[i * 128:(i + 1) * 128, l * NCH:(l + 1) * NCH],
                  reads=['modmine'], writes=['modtab'])
            b.dma('sp', C.modtab[:, 1, l, i * NCH:(i + 1) * NCH], MA[i * 512 + 256:i * 512 + 384, l * NCH:(l + 1) * NCH],
                  reads=['modall'], writes=['modtab'])
    b.flush()


def mod_ap(C, st, layer, m):
    return C.modtab[:, st, layer, m * 16:(m + 1) * 16]


def phase_proj(b, C, layer, x_dram, w, NP):
    T, TL = C.Tc, C.TLc
    tbl = blocks_for(T, TL)
    with ExitStack() as es:
        b.es = es
        nm = NormMod(b, 'n1', C.ones32, C.epsb, C.gains_sb[:, layer, :],
                     [mod_ap(C, 0, layer, 1), mod_ap(C, 1, layer, 1)],
                     [mod_ap(C, 0, layer, 0), mod_ap(C, 1, layer, 0)], ['modtab', 'gains'])
        h = b.sb('h', [128, KC, T], BF16)
        xb = [b.sb('xb%d' % i, [128, KC, 512], F32) for i in range(1)]
        xv = x_dram.rearrange("(kc p) t -> p kc t", p=128)
        for bi, (t0, n, g) in enumerate(tbl):
            xs = xb[0]
            b.dma('sp', xs[:, :, :n], xv[:, :, t0:t0 + n], reads=['xdram'], writes=[('xb', 0)])
            nm.emit(xs[:, :, :n], ('xb', 0), n, g, h[:, :, t0:t0 + n], 'h')
        psums = [(b.ps('pp%d' % i, [128, 512]), ('pp', i)) for i in range(4)]
        ob = [b.sb('ob%d' % i, [128, 512], F32) for i in range(4)]
        pvs = [C.Psend[s_].ap().rearrange("(c p) t -> p c t", p=128) for s_ in range(5)]
        st = {'i': 0}

        def epi(nch, t0, n, g, ps, pk):
            i = st['i'] % 4
            st['i'] += 1
            o = ob[i]
            if i % 2:
                b.op('act', lambda E: E.activation(out=o[:, :n], in_=ps[:, :n], func=AF.Copy),
                     reads=[pk], writes=[('ob', i)])
            else:
                b.op('dve', lambda E: E.tensor_copy(out=o[:, :n], in_=ps[:, :n]),
                     reads=[pk], writes=[('ob', i)])
            b.dma('sp', pvs[nch // 16][:, nch % 16, t0:t0 + n], o[:, :n], reads=[('ob', i)],
                  writes=[('Psend', nch // 16)])

        def after_group(gi):
            if gi % 4 == 3:
                s_ = gi // 4
                gather_rows(b, C.Psend[s_], C.Pall[s_], 0, 2048 // CH, [('Psend', s_)], ('Pall', s_))

        gemm_stream(b, w, D, NP, 512, lambda kc, t0, n: h[:, kc, t0:t0 + n], ['h'], tbl, epi, psums, after_group)
        b.flush()


def phase_l3(b, C, layer, x_dram, wo, w1, w2, out_dram, final):
    T, TL = (C.TLc, C.TLc) if final else (C.Tc, C.TLc)
    tbl = blocks_for(T, TL)
    with ExitStack() as es:
        b.es = es
        nm = NormMod(b, 'n2', C.ones32, C.epsb, C.gains_sb[:, 4 + layer, :],
                     [mod_ap(C, 0, layer, 4), mod_ap(C, 1, layer, 4)],
                     [mod_ap(C, 0, layer, 3), mod_ap(C, 1, layer, 3)], ['modtab', 'gains'])
        if final:
            nf = NormMod.__new__(NormMod)
            nf.__dict__.update(nm.__dict__)
            nf.gain = C.gains_sb[:, 8, :]
        gate = {2: [mod_ap(C, 0, layer, 2), mod_ap(C, 1, layer, 2)],
                5: [mod_ap(C, 0, layer, 5), mod_ap(C, 1, layer, 5)]}
        xs = b.sb('xs', [128, KC, 512], F32)
        hb = b.sb('hb', [128, KC, 512], BF16)
        hid = b.sb('hid', [128, 32, 512], BF16)
        r32 = [b.sb('r32_%d' % i, [128, 512], F32) for i in range(2)]
        psums = [(b.ps('pp%d' % i, [128, 512]), ('pp', i)) for i in range(4)]
        xv = x_dram.rearrange("(kc p) t -> p kc t", p=128)
        mv = C.Mmine.ap().rearrange("(fc hf kk r) t -> hf r kk fc t", fc=4, hf=2, kk=4, r=CH)
        ov = out_dram.rearrange("(kc p) t -> p kc t", p=128)
        for bix, (t0, n, g) in enumerate(tbl):
            cm_ = 'fill' if bix == 0 else 'use'
            b.dma('sp', xs[:, :, :n], xv[:, :, t0:t0 + n], reads=['xdram'], writes=['xs'])
            for half in range(2):
                for kk in range(4):
                    b.dma('pool', hb[half * CH:(half + 1) * CH, kk * 4:(kk + 1) * 4, :n],
                          mv[half][:, kk, :, t0:t0 + n], reads=['Mmine'], writes=['hb'])

            def epi_res(m):
                def epi(nch, t0_, n_, g_, ps, pk):
                    gcol = gate[m][g_][:, nch:nch + 1]
                    b.op('dve', lambda E: E.scalar_tensor_tensor(
                        out=xs[:, nch, :n_], in0=ps[:, :n_], scalar=gcol, in1=xs[:, nch, :n_],
                        op0=ALU.mult, op1=ALU.add), reads=[pk, 'modtab', 'xs'], writes=['xs'])
                return epi
            blk = [(0, n, g)]
            gemm_stream(b, wo, D, D, 512, lambda kc, t0_, n_: hb[:, kc, :n_], ['hb'], blk, epi_res(2), psums,
                        cache=(C.wcache, 0, cm_))
            nm.emit(xs[:, :, :n], 'xs', n, g, hb[:, :, :n], 'hb')
            for half in range(2):
                def epi_h(nch, t0_, n_, g_, ps, pk):
                    ri = b.pi % 2
                    r = r32[ri]
                    b.op('act', lambda E: E.activation(out=r[:, :n_], in_=ps[:, :n_], func=AF.Relu),
                         reads=[pk], writes=[('r32', ri)])
                    b.op('dve', lambda E: E.tensor_tensor(out=hid[:, nch, :n_], in0=r[:, :n_], in1=r[:, :n_],
                                                          op=ALU.mult),
                         reads=[('r32', ri)], writes=[('hid', nch)])
                gemm_stream(b, w1[:, half * 4096:(half + 1) * 4096], D, 4096, 512,
                            lambda kc, t0_, n_: hb[:, kc, :n_], ['hb'], blk, epi_h, psums,
                            cache=(C.wcache, 4 + half * 8, cm_))
                gemm_stream(b, w2[half * 4096:(half + 1) * 4096, :], 4096, D, 256,
                            lambda kc, t0_, n_: hid[:, kc, :n_], [('hid', i) for i in range(32)], blk,
                            epi_res(5), psums, cache=(C.wcache, 20 + half * 8, cm_))
            if final:
                nf.emit(xs[:, :, :n], 'xs', n, g, xs[:, :, :n], 'xs', plain=True)
            b.dma('sp', ov[:, :, t0:t0 + n], xs[:, :, :n], reads=['xs'], writes=['xdram'])
        b.flush()


def phase_scan_b(b, C, j):
    SC, SL = CTX_, SEQ_
    TS = SC + SL
    LP = TS + 6
    NU = 2
    blks = [(0, SC, 0)]
    t = 0
    while t < SL:
        blks.append((SC + 3 + t, 512, SC + t))
        t += 512
    nctx = 1
    PAs = [C.Pmine[s_].ap() for s_ in range(2)]
    TLc, TCc = C.TLc, C.TCc


    with ExitStack() as es:
        b.es = es
        pv_sb = b.sb('pv', [128, NU * 22], F32)
        b.dma('sp', pv_sb[:], C.pvB[j], writes=['pv'])
        one = b.sb('one', [128, 1], F32)
        b.op('dve', lambda E: E.memset(one[:], 1.0), writes=['one'])
        zt = b.sb('zt', [128, 4], F32)
        b.op('dve', lambda E: E.memset(zt[:], 0.0), writes=['zt'])
        cdec = b.sb('cdec', [128, NU * 4], F32)
        for u in range(NU):
            cs = slice(u * 4, u * 4 + 4)
            b.op('act', lambda E, u=u, cs=cs: E.activation(out=cdec[:, cs], in_=pv_sb[:, u * 22 + 18:u * 22 + 22],
                                                          func=AF.Exp, scale=-1.0), reads=['pv'], writes=['cdec'])
        b.op('act', lambda E: E.activation(out=cdec[:], in_=cdec[:], func=AF.Ln, bias=one[:, 0:1], scale=1.0),
             reads=['cdec', 'one'], writes=['cdec'])
        b.op('dve', lambda E: E.tensor_scalar(out=cdec[:], in0=cdec[:], scalar1=-8.0, scalar2=None, op0=ALU.mult),
             reads=['cdec'], writes=['cdec'])
        gwb = b.sb('gwb', [128, NU * 4, 2, 256], BF16)
        b.dma('pool', gwb[:], C.gwB[j].rearrange("g (kc p) n -> p g kc n", p=128), writes=['gwb'])
        UP = C.Upad.ap()
        for u in range(NU):
            for cc in range(2):
                for (c0, w_) in ((0, 2), (SC + 2, 3), (LP - 1, 2)):
                    b.dma('sp', UP[u, cc, :, c0:c0 + w_], zt[:, :w_], reads=['zt'], writes=['Upad'])
                for i in range(4):
                    for half in range(2):
                        ps_ = slice(half * CH, (half + 1) * CH)
                        src_ = mine_rows(PAs[1], i, u * 256 + cc * 128, half)
                        b.dma('sp', UP[u, cc, ps_, 2 + i * TCc:2 + (i + 1) * TCc], src_[:, TLc:TLc + TCc],
                              reads=[('Pmine', 1)], writes=['Upad'])
                        b.dma('sp', UP[u, cc, ps_, SC + 5 + i * TLc:SC + 5 + (i + 1) * TLc], src_[:, 0:TLc],
                              reads=[('Pmine', 1)], writes=['Upad'])
        uc32 = b.sb('uc32', [128, 2, TS], F32)
        uc16 = b.sb('uc16', [128, 2, TS], BF16)
        Hf = b.sb('Hf', [128, TS], F32)
        ub = [b.sb('ub%d' % i, [128, 2, 515], F32) for i in range(2)]
        W = {}
        for nm_ in ['r', 'i', 'a', 'a2', 'xin', 'y', 's', 'mo']:
            W[nm_] = [b.sb('w_%s%d' % (nm_, i), [128, 512], F32) for i in range(2)]
        hbr = [b.sb('hbr%d' % i, [128, 512], F32) for i in range(2)]
        pz = [b.ps('pz%d' % i, [128, 512]) for i in range(4)]
        cnt = {'k': 0}
        MS = C.Msend.ap()
        for u in range(NU):
            pb = u * 22
            for bi, (p0, n, t0) in enumerate(blks):
                ub_ = ub[bi % 2]
                uk = ('ub', bi % 2)
                for cc in range(2):
                    b.dma('sp', ub_[:, cc, :n + 3], UP[u, cc, :, p0:p0 + n + 3], reads=['Upad'], writes=[uk])
                for cc in range(2):
                    dst = uc32[:, cc, t0:t0 + n]
                    b.op('dve', lambda E, ub_=ub_, cc=cc, n=n, dst=dst, pb=pb: E.tensor_scalar(
                        out=dst, in0=ub_[:, cc, 0:n], scalar1=pv_sb[:, pb + cc * 4:pb + cc * 4 + 1],
                        scalar2=pv_sb[:, pb + 8 + cc:pb + 9 + cc], op0=ALU.mult, op1=ALU.add),
                        reads=[uk, 'pv'], writes=[('uc32', cc)])
                    for jj in range(1, 4):
                        b.op('dve', lambda E, ub_=ub_, cc=cc, n=n, jj=jj, dst=dst, pb=pb: E.scalar_tensor_tensor(
                            out=dst, in0=ub_[:, cc, jj:jj + n], scalar=pv_sb[:, pb + cc * 4 + jj:pb + cc * 4 + jj + 1],
                            in1=dst, op0=ALU.mult, op1=ALU.add), reads=[uk, 'pv', ('uc32', cc)], writes=[('uc32', cc)])
                    b.op('act', lambda E, cc=cc, t0=t0, n=n, dst=dst: E.activation(
                        out=uc16[:, cc, t0:t0 + n], in_=dst, func=AF.Copy),
                        reads=[('uc32', cc)], writes=[('uc16', cc)])
            for oc in range(2):
                for d in range(2):
                    if d == 0:
                        order = list(range(len(blks)))
                    else:
                        order = list(range(nctx - 1, -1, -1)) + list(range(len(blks) - 1, nctx - 1, -1))
                    prev = None
                    for bi in order:
                        p0, n, t0 = blks[bi]
                        k = cnt['k'] % 2
                        cnt['k'] += 1
                        zr, zi = pz[2 * k], pz[2 * k + 1]
                        for gi_, zp in ((0, zr), (1, zi)):
                            for kc in range(2):
                                b.op('pe', lambda E, zp=zp, gi_=gi_, kc=kc, t0=t0, n=n, d=d, oc=oc, u=u: E.matmul(
                                    zp[:, :n], lhsT=gwb[:, u * 4 + d * 2 + gi_, kc, oc * 128:(oc + 1) * 128],
                                    rhs=uc16[:, kc, t0:t0 + n], start=(kc == 0), stop=(kc == 1)),
                                    reads=['gwb', ('uc16', 0), ('uc16', 1)], writes=[('pz', 2 * k + gi_)])
                        r, ig, a, a2, xin = W['r'][k], W['i'][k], W['a'][k], W['a2'][k], W['xin'][k]
                        c_r = pb + 10 + (d * 2 + 0) * 2 + oc
                        c_i = pb + 10 + (d * 2 + 1) * 2 + oc
                        br = pv_sb[:, c_r:c_r + 1]
                        bi_ = pv_sb[:, c_i:c_i + 1]
                        cd = cdec[:, u * 4 + d * 2 + oc: u * 4 + d * 2 + oc + 1]
                        b.op('act', lambda E, r=r, zr=zr, n=n, br=br: E.activation(
                            out=r[:, :n], in_=zr[:, :n], func=AF.Sigmoid, bias=br, scale=1.0),
                            reads=[('pz', 2 * k), 'pv'], writes=[('r', k)])
                        b.op('act', lambda E, ig=ig, zi=zi, n=n, bi_=bi_: E.activation(
                            out=ig[:, :n], in_=zi[:, :n], func=AF.Sigmoid, bias=bi_, scale=1.0),
                            reads=[('pz', 2 * k + 1), 'pv'], writes=[('i', k)])
                        b.op('act', lambda E, a=a, r=r, n=n, cd=cd: E.activation(
                            out=a[:, :n], in_=r[:, :n], func=AF.Exp, scale=cd),
                            reads=[('r', k), 'cdec'], writes=[('a', k)])
                        b.op('pool', lambda E, a=a, a2=a2, n=n: E.tensor_tensor(
                            out=a2[:, :n], in0=a[:, :n], in1=a[:, :n], op=ALU.mult),
                            reads=[('a', k)], writes=[('a2', k)])
                        b.op('act', lambda E, a2=a2, n=n: E.activation(
                            out=a2[:, :n], in_=a2[:, :n], func=AF.Sqrt, scale=-1.0, bias=one[:, 0:1]),
                            reads=[('a2', k), 'one'], writes=[('a2', k)])
                        b.op('pool', lambda E, a2=a2, ig=ig, xin=xin, n=n: E.tensor_tensor(
                            out=xin[:, :n], in0=a2[:, :n], in1=ig[:, :n], op=ALU.mult),
                            reads=[('a2', k), ('i', k)], writes=[('xin', k)])
                        b.op('dve', lambda E, xin=xin, n=n, t0=t0, oc=oc: E.tensor_tensor(
                            out=xin[:, :n], in0=xin[:, :n], in1=uc32[:, oc, t0:t0 + n], op=ALU.mult),
                            reads=[('xin', k), ('uc32', oc)], writes=[('xin', k)])
                        if d == 0:
                            init = 0.0 if prev is None else Hf[:, t0 - 1:t0]
                            b.op('dve', lambda E, a=a, xin=xin, n=n, t0=t0, init=init: E.tensor_tensor_scan(
                                out=Hf[:, t0:t0 + n], data0=a[:, :n], data1=xin[:, :n], initial=init,
                                op0=ALU.mult, op1=ALU.add), reads=[('a', k), ('xin', k), 'Hf'], writes=['Hf'])
                            prev = bi
                        else:
                            hb_ = hbr[k]
                            if prev is None:
                                init = 0.0
                                rkeys = []
                            else:
                                pk_, pn_ = prev
                                init = hbr[pk_][:, pn_ - 1:pn_]
                                rkeys = [('hbr', pk_)]
                            b.op('dve', lambda E, a=a, xin=xin, n=n, hb_=hb_, init=init: E.tensor_tensor_scan(
                                out=hb_[:, :n], data0=a[:, :n][:, ::-1], data1=xin[:, :n][:, ::-1], initial=init,
                                op0=ALU.mult, op1=ALU.add), reads=[('a', k), ('xin', k)] + rkeys, writes=[('hbr', k)])
                            prev = (k, n)
                            yb, sb_, mo_ = W['y'][k], W['s'][k], W['mo'][k]
                            for half in range(2):
                                ps_ = slice(half * CH, (half + 1) * CH)
                                if t0 < SC:
                                    for i in range(4):
                                        b.dma('sp', yb[ps_, i * TCc:(i + 1) * TCc],
                                              mine_rows(PAs[0], i, u * 256 + oc * 128, half)[:, TLc:TLc + TCc],
                                              reads=[('Pmine', 0)], writes=[('y', k)])
                                else:
                                    a_ = t0 - SC
                                    i, loc = a_ // TLc, a_ % TLc
                                    b.dma('sp', yb[ps_, :n], mine_rows(PAs[0], i, u * 256 + oc * 128, half)[:, loc:loc + n],
                                          reads=[('Pmine', 0)], writes=[('y', k)])
                            b.op('act', lambda E, yb=yb, n=n: E.activation(
                                out=yb[:, :n], in_=yb[:, :n], func=AF.Gelu_apprx_tanh),
                                reads=[('y', k)], writes=[('y', k)])
                            b.op('dve', lambda E, sb_=sb_, hb_=hb_, n=n, t0=t0: E.tensor_tensor(
                                out=sb_[:, :n], in0=Hf[:, t0:t0 + n], in1=hb_[:, :n][:, ::-1], op=ALU.add),
                                reads=['Hf', ('hbr', k)], writes=[('s', k)])
                            b.op('pool', lambda E, sb_=sb_, yb=yb, mo_=mo_, n=n: E.tensor_tensor(
                                out=mo_[:, :n], in0=sb_[:, :n], in1=yb[:, :n], op=ALU.mult),
                                reads=[('s', k), ('y', k)], writes=[('mo', k)])
                            write_nat(b, MS, u * 256 + oc * 128, t0, n, mo_, [('mo', k)], TLc, TCc)
                    if d == 1:
                        f0_ = u * 256 + oc * 128
                        for dest in range(4):
                            c_ = (dest * 512 + f0_) // CH
                            gather_rows(b, C.Msend, C.Mall, c_, c_ + 2, ['Msend'], 'Mall')
        b.flush()


def phase_scan_a(b, C, j, lbmode):
    SC, SL = CTX_, SEQ_
    TS = SC + SL
    NP_, NS = 4, 8
    NT = TS // 128
    TLc, TCc = C.TLc, C.TCc
    PAs = [C.Pmine[s_].ap() for s_ in range(5)]


    def load_nat(q, dst, n, t0, sec, pi_, wkey):
        for half in range(2):
            ps_ = slice(half * CH, (half + 1) * CH)
            if t0 < SC:
                for i in range(4):
                    b.dma(q, dst[ps_, i * TCc:(i + 1) * TCc], mine_rows(PAs[sec], i, pi_ * 128, half)[:, TLc:TLc + TCc],
                          reads=[('Pmine', sec)], writes=[wkey])
            else:
                a_ = t0 - SC
                i, loc = a_ // TLc, a_ % TLc
                b.dma(q, dst[ps_, :n], mine_rows(PAs[sec], i, pi_ * 128, half)[:, loc:loc + n],
                      reads=[('Pmine', sec)], writes=[wkey])

    nat_blks = [(0, SC)] + [(SC + t, 512) for t in range(0, SL, 512)]
    sblk = {0: nat_blks, 1: [(0, SC)] + [(SC + SL - 512 - t, 512) for t in range(0, SL, 512)]}

    with ExitStack() as es:
        b.es = es
        ones32, epsb = C.ones32, C.epsb
        pv_sb = b.sb('pv', [128, 20], F32)
        b.dma('sp', pv_sb[:], C.pvA[j], writes=['pv'])
        cst_sb = b.sb('cst', [128, 897], F32)
        b.dma('sp', cst_sb[:], C.cst, writes=['cst'])
        ident = b.sb('ident', [128, 128], BF16)
        b.op('dve', lambda E: E.tensor_copy(out=ident[:], in_=cst_sb[:, 0:128]), reads=['cst'], writes=['ident'])
        Jm = b.sb('Jm', [128, 128], BF16)
        b.op('dve', lambda E: E.tensor_copy(out=Jm[:], in_=cst_sb[:, 769:897]), reads=['cst'], writes=['Jm'])
        amask = cst_sb[:, 128:256]
        rmask = cst_sb[:, 256:768]
        lb = b.sb('lb', [128, NS], F32)
        oml = b.sb('oml', [128, NS], F32)
        if lbmode:
            pv3 = pv_sb[:, 0:2 * NS].rearrange("p (s two) -> p s two", two=2)
            b.op('dve', lambda E: E.tensor_tensor(out=lb[:], in0=pv3[:, :, 1], in1=pv3[:, :, 0], op=ALU.subtract),
                 reads=['pv'], writes=['lb'])
            b.op('act', lambda E: E.activation(out=lb[:], in_=lb[:], func=AF.Sigmoid), reads=['lb'], writes=['lb'])
            b.op('dve', lambda E: E.tensor_scalar(out=oml[:], in0=lb[:], scalar1=-1.0, scalar2=1.0,
                                                  op0=ALU.mult, op1=ALU.add), reads=['lb'], writes=['oml'])
        O = [b.sb('O%d' % i, [128, TS], F32) for i in range(2)]
        V = [b.sb('V%d' % i, [128, NT, 128], BF16) for i in range(2)]
        vT = [b.sb('vT%d' % i, [128, 512], BF16) for i in range(2)]
        Wk = {}
        for nm_ in ['q', 'z', 'f', 'lf', 'k', 'bc', 'eb', 'enb', 'k32']:
            Wk[nm_] = [b.sb('a_%s%d' % (nm_, i), [128, 512], F32) for i in range(2)]
        Qt = [b.sb('Qt%d' % i, [128, 512], BF16) for i in range(4)]
        Kt = [b.sb('Kt%d' % i, [128, 512], BF16) for i in range(4)]
        KbT = [b.sb('KbT%d' % i, [128, 512], BF16) for i in range(4)]
        ebuf = [b.sb('ebuf%d' % i, [128, 512], F32) for i in range(4)]
        Kb = [b.sb('Kb%d' % i, [128, 128], BF16) for i in range(2)]
        Am = [b.sb('Am%d' % i, [128, 128], BF16) for i in range(2)]
        Kbz = [b.sb('Kbz%d' % i, [128, 128], BF16) for i in range(2)]
        S32 = [b.sb('S32_%d' % i, [128, 128], F32) for i in range(2)]
        S16 = [[b.sb('S16_%d_%d' % (i, c), [128, 128], BF16) for c in range(4)] for i in range(2)]
        pT1 = b.ps('pT', [128, 128], BF16)
        pA1 = b.ps('pA', [128, 512])
        pO = [b.ps('pO%d' % i, [128, 128]) for i in range(2)]
        pS = [[b.ps('pS%d_%d' % (i, c), [128, 128]) for c in range(2)] for i in range(2)]
        fb = {}
        for nm_ in ['s', 'sq', 'rs', 'g', 'mo']:
            fb[nm_] = b.sb('f_%s' % nm_, [128, 512], F32)
        pNb = pA1
        MS = C.Msend.ap()

        for pr in range(NP_):
            for ci, (t0, n) in enumerate(nat_blks):
                vt = vT[ci % 2]
                load_nat('pool', vt, n, t0, 1, pr, ('vT', ci % 2))
                for tt in range(n // 128):
                    ti = (t0 + tt * 128) // 128
                    x_ = ti % 2
                    b.op('pe', lambda E, vt=vt, tt=tt, x_=x_: E.transpose(
                        out=pT1[:, :], in_=vt[:, tt * 128:(tt + 1) * 128], identity=ident[:, :]),
                        reads=[('vT', ci % 2), 'ident'], writes=[('pT', 0)])
                    b.op('act', lambda E, ti=ti, x_=x_: E.activation(out=V[0][:, ti, :], in_=pT1[:, :], func=AF.Copy),
                         reads=[('pT', 0)], writes=[('V', 0)])
                    b.op('pe', lambda E, ti=ti, x_=x_: E.matmul(pS[0][x_][:, :], lhsT=Jm[:, :], rhs=V[0][:, ti, :],
                                                               start=True, stop=True),
                         reads=[('V', 0), 'Jm'], writes=[('pS', 0, x_)])
                    b.op('dve', lambda E, ti=ti, x_=x_: E.tensor_copy(out=V[1][:, ti, :], in_=pS[0][x_][:, :]),
                         reads=[('pS', 0, x_)], writes=[('V', 1)])
            for dr in range(2):
                b.op('dve', lambda E, dr=dr: E.memset(S32[dr][:], 0.0), writes=[('S32', dr)])
                b.op('dve', lambda E, dr=dr: E.memset(S16[dr][3][:], 0.0), writes=[('S16', dr, 3)])
            def emit_ps(dr, ti, c):
                cs = slice(c * 32, (c + 1) * 32)
                psl = pS[dr][c % 2][:, :]
                if c < 3:
                    b.op('pe', lambda E: E.matmul(psl, lhsT=Kb[dr][cs, :], rhs=V[dr][cs, ti, :], start=True, stop=True),
                         reads=[('Kb', dr), ('V', dr)], writes=[('pS', dr, c % 2)])
                else:
                    b.op('pe', lambda E: E.matmul(psl, lhsT=Kbz[dr][64:128, :], rhs=V[dr][64:128, ti, :],
                                                  start=True, stop=True),
                         reads=[('Kbz', dr), ('V', dr)], writes=[('pS', dr, c % 2)])

            for bi in range(len(nat_blks)):
                n = nat_blks[bi][1]
                for dr in range(2):
                    s = 2 * pr + dr
                    nt0 = sblk[dr][bi][0]
                    q, z, f, lf, k, bc, eb, enb, k32 = [Wk[x][dr] for x in
                                                        ['q', 'z', 'f', 'lf', 'k', 'bc', 'eb', 'enb', 'k32']]
                    dq = dr + 2 * (bi % 2)
                    eb = ebuf[dq]
                    K_ = lambda x, dr=dr, dq=dq: (x, dq) if x in ('Qt', 'Kt', 'KbT', 'eb') else (x, dr)
                    load_nat('sp', q, n, nt0, 0, pr, K_('q'))
                    load_nat('sp', z, n, nt0, 3 + dr, pr, K_('z'))
                    if dr == 0:
                        zin, qin = z[:, :n], q[:, :n]
                    else:
                        zin, qin = z[:, :n][:, ::-1], q[:, :n][:, ::-1]
                    b.op('act', lambda E, zin=zin, f=f, n=n: E.activation(out=f[:, :n], in_=zin, func=AF.Exp, scale=-1.0),
                         reads=[K_('z')], writes=[K_('f')])
                    b.op('pool', lambda E, f=f, n=n: E.tensor_scalar(out=f[:, :n], in0=f[:, :n], scalar1=1.0, scalar2=None,
                                                                    op0=ALU.add), reads=[K_('f')], writes=[K_('f')])
                    b.op('dve', lambda E, f=f, n=n: E.reciprocal(out=f[:, :n], in_=f[:, :n]),
                         reads=[K_('f')], writes=[K_('f')])
                    if lbmode:
                        b.op('dve', lambda E, f=f, n=n, s=s: E.tensor_scalar(
                            out=f[:, :n], in0=f[:, :n], scalar1=oml[:, s:s + 1], scalar2=lb[:, s:s + 1],
                            op0=ALU.mult, op1=ALU.add), reads=[K_('f'), 'lb', 'oml'], writes=[K_('f')])
                    b.op('act', lambda E, f=f, lf=lf, n=n: E.activation(out=lf[:, :n], in_=f[:, :n], func=AF.Ln),
                         reads=[K_('f')], writes=[K_('lf')])
                    b.op('pool', lambda E, f=f, k=k, n=n: E.tensor_scalar(
                        out=k[:, :n], in0=f[:, :n], scalar1=-1.0, scalar2=1.0, op0=ALU.mult, op1=ALU.add),
                        reads=[K_('f')], writes=[K_('k')])
                    b.op('dve', lambda E, lf=lf, bc=bc, n=n: E.tensor_tensor_scan(
                        out=bc[:, :n], data0=rmask[:, :n], data1=lf[:, :n], initial=0.0, op0=ALU.mult, op1=ALU.add),
                        reads=[K_('lf'), 'cst'], writes=[K_('bc')])
                    b.op('act', lambda E, bc=bc, eb=eb, n=n: E.activation(out=eb[:, :n], in_=bc[:, :n], func=AF.Exp),
                         reads=[K_('bc')], writes=[K_('eb')])
                    b.op('act', lambda E, bc=bc, enb=enb, n=n: E.activation(out=enb[:, :n], in_=bc[:, :n], func=AF.Exp,
                                                                          scale=-1.0),
                         reads=[K_('bc')], writes=[K_('enb')])
                    b.op('pool', lambda E, qin=qin, eb=eb, dq=dq, n=n: E.tensor_tensor(
                        out=Qt[dq][:, :n], in0=qin, in1=eb[:, :n], op=ALU.mult),
                        reads=[K_('q'), K_('eb')], writes=[K_('Qt')])
                    b.op('dve', lambda E, k=k, enb=enb, k32=k32, n=n: E.tensor_tensor(
                        out=k32[:, :n], in0=k[:, :n], in1=enb[:, :n], op=ALU.mult),
                        reads=[K_('k'), K_('enb')], writes=[K_('k32')])
                    b.op('act', lambda E, k32=k32, dq=dq, n=n: E.activation(out=Kt[dq][:, :n], in_=k32[:, :n], func=AF.Copy),
                         reads=[K_('k32')], writes=[K_('Kt')])
                    ebl = eb[:, :n].rearrange("p (c j) -> p c j", j=32)[:, :, 31:32].broadcast_to([128, n // 32, 32])
                    b.op('pool', lambda E, k32=k32, ebl=ebl, dq=dq, n=n: E.tensor_tensor(
                        out=KbT[dq][:, :n].rearrange("p (c j) -> p c j", j=32),
                        in0=k32[:, :n].rearrange("p (c j) -> p c j", j=32), in1=ebl, op=ALU.mult),
                        reads=[K_('k32'), K_('eb')], writes=[K_('KbT')])
                for tt in range(n // 128):
                    info = {}
                    for dr in range(2):
                        dq = dr + 2 * (bi % 2)
                        K_ = lambda x, dr=dr, dq=dq: (x, dq) if x in ('Qt', 'Kt', 'KbT', 'eb') else (x, dr)
                        c0 = tt * 128
                        nt0 = sblk[dr][bi][0]
                        if dr == 0:
                            ti = (nt0 + c0) // 128
                            st0 = nt0 + c0
                        else:
                            ti = (nt0 + n - c0 - 128) // 128
                            st0 = (0 if bi == 0 else SC + (bi - 1) * 512) + c0
                        info[dr] = (ti, st0)
                        b.op('pe', lambda E, dq=dq, c0=c0: E.transpose(out=pT1[:, :], in_=KbT[dq][:, c0:c0 + 128],
                                                                       identity=ident[:, :]),
                             reads=[K_('KbT'), 'ident'], writes=[('pT', 0)])
                        b.op('act', lambda E, dr=dr: E.activation(out=Kb[dr][:, :], in_=pT1[:, :], func=AF.Copy),
                             reads=[('pT', 0)], writes=[K_('Kb')])
                        b.op('pool', lambda E, dr=dr: E.tensor_scalar(
                            out=Kbz[dr][64:128, :], in0=Kb[dr][64:128, :], scalar1=cst_sb[64:128, 768:769], scalar2=None,
                            op0=ALU.mult), reads=[K_('Kb'), 'cst'], writes=[K_('Kbz')])
                        b.op('pe', lambda E, dq=dq, c0=c0: E.matmul(pA1[:, 0:128], lhsT=Kt[dq][:, c0:c0 + 128],
                                                                    rhs=Qt[dq][:, c0:c0 + 128], start=True, stop=True),
                             reads=[K_('Kt'), K_('Qt')], writes=[('pA', 0)])
                        b.op('dve', lambda E, dr=dr: E.tensor_tensor(out=Am[dr][:, :], in0=pA1[:, 0:128], in1=amask,
                                                                     op=ALU.mult),
                             reads=[('pA', 0), 'cst'], writes=[K_('Am')])
                        b.op('pe', lambda E, dr=dr, ti=ti: E.matmul(pO[dr][:, :], lhsT=V[dr][:, ti, :], rhs=Am[dr][:, :],
                                                                    start=True, stop=False),
                             reads=[('V', dr), K_('Am')], writes=[K_('pO')])
                        for c in range(2):
                            emit_ps(dr, ti, c)
                    for dr in range(2):
                        dq = dr + 2 * (bi % 2)
                        K_ = lambda x, dr=dr, dq=dq: (x, dq) if x in ('Qt', 'Kt', 'KbT', 'eb') else (x, dr)
                        eb = ebuf[dq]
                        c0 = tt * 128
                        ti, st0 = info[dr]
                        for c in range(4):
                            cs = slice(c * 32, (c + 1) * 32)
                            psl = pS[dr][c % 2][:, :]
                            sprev = S16[dr][(c - 1) % 4]
                            b.op('pe', lambda E, dr=dr, dq=dq, c0=c0, c=c, cs=cs, sprev=sprev: E.matmul(
                                pO[dr][:, cs], lhsT=sprev[:, :], rhs=Qt[dq][:, c0 + c * 32:c0 + (c + 1) * 32],
                                start=False, stop=(c == 3)),
                                reads=[('S16', dr, (c - 1) % 4), K_('Qt')], writes=[K_('pO')])
                            b.op('dve', lambda E, dr=dr, eb=eb, c0=c0, c=c, psl=psl: E.scalar_tensor_tensor(
                                out=S32[dr][:, :], in0=S32[dr][:, :], scalar=eb[:, c0 + c * 32 + 31:c0 + c * 32 + 32],
                                in1=psl, op0=ALU.mult, op1=ALU.add),
                                reads=[('S32', dr), K_('eb'), ('pS', dr, c % 2)], writes=[('S32', dr)])
                            b.op('dve', lambda E, dr=dr, c=c: E.tensor_copy(out=S16[dr][c][:, :], in_=S32[dr][:, :]),
                                 reads=[('S32', dr)], writes=[('S16', dr, c)])
                            if c + 2 < 4:
                                emit_ps(dr, ti, c + 2)
                        b.op('act', lambda E, dr=dr, st0=st0: E.activation(
                            out=O[dr][:, st0:st0 + 128], in_=pO[dr][:, :], func=AF.Copy),
                            reads=[K_('pO')], writes=[('O', dr)])
            for (t0, n) in nat_blks:
                lo = 0 if t0 < SC else SC + SL - (t0 - SC) - n
                sb_, sq, rs, g, mo = fb['s'], fb['sq'], fb['rs'], fb['g'], fb['mo']
                load_nat('sp', g, n, t0, 2, pr, 'fg')
                b.op('dve', lambda E, t0=t0, n=n, lo=lo: E.tensor_tensor(
                    out=sb_[:, :n], in0=O[0][:, t0:t0 + n], in1=O[1][:, lo:lo + n][:, ::-1], op=ALU.add),
                    reads=[('O', 0), ('O', 1)], writes=['fs'])
                b.op('act', lambda E, n=n: E.activation(out=sq[:, :n], in_=sb_[:, :n], func=AF.Square),
                     reads=['fs'], writes=['fsq'])
                b.op('pe', lambda E, n=n: E.matmul(pNb[:, :n], lhsT=ones32[:, :], rhs=sq[:, :n], start=True, stop=True),
                     reads=['fsq', 'ones32'], writes=[('pA', 0)])
                b.op('act', lambda E, n=n: E.activation(out=rs[:, :n], in_=pNb[:, :n], func=AF.Sqrt,
                                                        scale=1.0 / 128, bias=epsb[:, 0:1]),
                     reads=[('pA', 0), 'epsb'], writes=['frs'])
                b.op('dve', lambda E, n=n: E.reciprocal(out=rs[:, :n], in_=rs[:, :n]), reads=['frs'], writes=['frs'])
                b.op('act', lambda E, n=n: E.activation(out=g[:, :n], in_=g[:, :n], func=AF.Silu),
                     reads=['fg'], writes=['fg'])
                b.op('dve', lambda E, n=n, pr=pr: E.scalar_tensor_tensor(
                    out=sb_[:, :n], in0=sb_[:, :n], scalar=pv_sb[:, 16 + pr:17 + pr], in1=rs[:, :n],
                    op0=ALU.mult, op1=ALU.mult), reads=['fs', 'frs', 'pv'], writes=['fs'])
                b.op('pool', lambda E, n=n: E.tensor_tensor(out=mo[:, :n], in0=sb_[:, :n], in1=g[:, :n], op=ALU.mult),
                     reads=['fs', 'fg'], writes=['fmo'])
                write_nat(b, MS, pr * 128, t0, n, mo, ['fmo'], TLc, TCc)
            for dest in range(4):
                c_ = (dest * 512 + pr * 128) // CH
                gather_rows(b, C.Msend, C.Mall, c_, c_ + 2, ['Msend'], 'Mall')
        b.flush()


def _dbg_dump(b, C, oT, src_ap, rkey):
    b.dma('sp', oT, src_ap, reads=[rkey], writes=['out'])
    b.flush()


def build_fused():
    nc = bass.Bass("TRN2", target_bir_lowering=False)
    C = Ctx()
    C.TLc = SEQ_ * B_ // NCORES
    C.TCc = CTX_ * B_ // NCORES
    C.Tc = C.TLc + C.TCc
    TS = CTX_ + SEQ_

    def inp(name, shape):
        return nc.dram_tensor(name, shape, F32, kind="ExternalInput").ap()
    xT = inp("xT", [D, C.Tc])
    C.cT = inp("cT", [D, 4])
    C.wm = inp("wm", [DEPTH * D, 3072])
    C.bm = inp("bm", [128, DEPTH * 24])
    C.gains = inp("gains", [128, 9, KC])
    NA, NB_ = (NLAYERS + 1) // 2, NLAYERS // 2
    if DBG_STOP == 'mod':
        NA, NB_ = 0, 0
    awin = [inp("awin%d" % j, [D, 5 * D]) for j in range(NA)]
    awout = [inp("awout%d" % j, [D, D]) for j in range(0 if DBG_STOP else NA)]
    bwin = [inp("bwin%d" % j, [D, 2 * D]) for j in range(NB_)]
    bwout = [inp("bwout%d" % j, [D, D]) for j in range(NB_)]
    NW = 0 if DBG_STOP else NLAYERS
    w1 = [inp("w1_%d" % l, [D, DFF]) for l in range(NW)]
    w2 = [inp("w2_%d" % l, [DFF, D]) for l in range(NW)]
    C.pvA = inp("pvA", [2, 128, 20])
    C.cst = inp("cst", [128, 897])
    C.pvB = inp("pvB", [2, 128, 44])
    C.gwB = inp("gwB", [2, 8, 256, 256])
    oT = nc.dram_tensor("oT", [D, C.TLc], F32, kind="ExternalOutput").ap()
    C.modsend = nc.dram_tensor("modsend", [512, DEPTH * 24], F32)
    C.modall = nc.dram_tensor("modall", [4 * 512, DEPTH * 24], F32)
    C.modmine = nc.dram_tensor("modmine", [4 * 128, DEPTH * 24], F32)
    C.xbuf = nc.dram_tensor("xbuf", [D, C.Tc], F32)
    C.Psend = [nc.dram_tensor("Psend%d" % s_, [D, C.Tc], F32) for s_ in range(5)]
    C.Pall = [nc.dram_tensor("Pall%d" % s_, [4 * D, C.Tc], F32) for s_ in range(5)]
    C.Pmine = [nc.dram_tensor("Pmine%d" % s_, [D, C.Tc], F32) for s_ in range(5)]
    C.Msend = nc.dram_tensor("Msend", [4 * 512, C.Tc], F32)
    C.Mall = nc.dram_tensor("Mall", [4 * D, C.Tc], F32)
    C.Mmine = nc.dram_tensor("Mmine", [D, C.Tc], F32)
    C.Upad = nc.dram_tensor("Upad", [2, 2, 128, TS + 7], F32)
    C.wcache = nc.dram_tensor("wcache", [36, 128, 8192], BF16).ap()
    with ExitStack() as es:
        b = Bld(nc, es)
        C.ones32 = b.sb('ones32', [128, 128], F32)
        b.op('dve', lambda E: E.memset(C.ones32[:], 1.0), writes=['ones32'])
        C.epsb = b.sb('epsb', [128, 1], F32)
        b.op('dve', lambda E: E.memset(C.epsb[:], EPS), writes=['epsb'])
        C.modtab = b.sb('modtab', [128, 2, DEPTH, 96], F32)
        C.gains_sb = b.sb('gains', [128, 9, KC], F32)
        phase_mod(b, C)
        if DBG_STOP == 'mod':
            with ExitStack() as es2:
                b.es = es2
                t_ = b.sb('dbg', [128, 2 * DEPTH * 96], F32)
                b.op('dve', lambda E: E.tensor_copy(out=t_[:], in_=C.modtab[:].rearrange("p a l g -> p (a l g)")),
                     reads=['modtab'], writes=['dbg'])
                _dbg_dump(b, C, oT[0:256, 0:384].rearrange("(a p) n -> p a n", p=128), t_[:].rearrange("p (a n) -> p a n", a=2), 'dbg')
            return nc
        for layer in range(NLAYERS):
            j = layer // 2
            final = (layer == NLAYERS - 1)
            x_in = xT if layer == 0 else C.xbuf.ap()
            if layer % 2 == 0:
                phase_proj(b, C, layer, x_in, awin[j], 5 * D)
                if DBG_STOP == 'proj':
                    _dbg_dump(b, C, oT[:, 0:C.TLc], C.Pall[0].ap()[4096:6144, 0:C.TLc], ('Pall', 0))
                    return nc
                phase_extract(b, [(C.Pall[s_], C.Pmine[s_].ap(), 1, 2048, 2048, C.Tc, 'k', ('Pall', s_), ('Pmine', s_))
                                  for s_ in range(5)])
                if DBG_STOP == 'extract':
                    _dbg_dump(b, C, oT[:, 0:C.TLc], C.Pmine[3].ap()[:, 0:C.TLc], ('Pmine', 3))
                    return nc
                phase_scan_a(b, C, j, 1 if j > 0 else 0)
                if DBG_STOP == 'scan':
                    _dbg_dump(b, C, oT[:, 0:C.TLc], C.Mall.ap()[2048:4096, 0:C.TLc], 'Mall')
                    return nc
                wo = awout[j]
            else:
                phase_proj(b, C, layer, x_in, bwin[j], 2 * D)
                phase_extract(b, [(C.Pall[s_], C.Pmine[s_].ap(), 1, 2048, 2048, C.Tc, 'k', ('Pall', s_), ('Pmine', s_))
                                  for s_ in (0, 1)])
                phase_scan_b(b, C, j)
                wo = bwout[j]
            phase_extract(b, [(C.Mall, C.Mmine.ap(), 1, 2048, 2048, C.Tc, 'k', 'Mall', 'Mmine')])
            phase_l3(b, C, layer, x_in, wo, w1[layer], w2[layer], oT if final else C.xbuf.ap(), final)
    return nc


_NC = {}


def _f32(a):
    return np.ascontiguousarray(a, dtype=np.float32)


def _pl(v):
    return np.asarray(v, dtype=np.float32).reshape(KC, 128).T


def _consts_a():
    ident = np.eye(128, dtype=np.float32)
    s_ = np.arange(128)[:, None]
    t_ = np.arange(128)[None, :]
    am = ((s_ <= t_) & (s_ // 32 == t_ // 32)).astype(np.float32)
    rm = np.ones((128, 512), np.float32)
    rm[:, ::32] = 0
    m96 = (np.arange(128) >= 96).astype(np.float32)[:, None]
    Jm = np.ascontiguousarray(ident[::-1])
    return np.ascontiguousarray(np.concatenate([ident, am, rm, m96, Jm], axis=1))


def make_in_maps(x, c, ctx, c_ctx, w_mod, b_mod, norm1, norm2, a_w_in, a_lb_logits, a_onorm, a_w_out,
                 b_w_in, b_conv_w, b_conv_b, b_gate_w, b_gate_b, b_lambda, b_w_out, mlp_w1, mlp_w2, final_norm):
    TLc = SEQ_ * B_ // NCORES
    TCc = CTX_ * B_ // NCORES
    x = np.asarray(x, np.float32)
    ctx = np.asarray(ctx, np.float32)
    latf = x.reshape(B_ * SEQ_, D)
    ctxf = ctx.reshape(B_ * CTX_, D)
    cT = np.zeros((D, 4), np.float32)
    cT[:, 0] = c[0]
    cT[:, 1] = c[1]
    cT[:, 2] = c_ctx
    gains = np.zeros((128, 9, KC), np.float32)
    for l in range(NLAYERS):
        gains[:, l] = _pl(norm1[l])
        gains[:, 4 + l] = _pl(norm2[l])
    gains[:, 8] = _pl(final_norm)
    cst = _consts_a()
    shared = {"cT": cT, "gains": _f32(gains), "cst": cst}
    for j in range((NLAYERS + 1) // 2):
        shared["awin%d" % j] = _f32(a_w_in[j])
        shared["awout%d" % j] = _f32(a_w_out[j])
    for j in range(NLAYERS // 2):
        shared["bwin%d" % j] = _f32(b_w_in[j])
        shared["bwout%d" % j] = _f32(b_w_out[j])
    for l in range(NLAYERS):
        shared["w1_%d" % l] = _f32(mlp_w1[l])
        shared["w2_%d" % l] = _f32(mlp_w2[l])
    ims = []
    for r in range(NCORES):
        k = r % 4
        m = dict(shared)
        m["xT"] = _f32(np.concatenate([latf[r * TLc:(r + 1) * TLc], ctxf[r * TCc:(r + 1) * TCc]], axis=0).T)
        m["wm"] = _f32(np.concatenate([w_mod[l][:, k * 3072:(k + 1) * 3072] for l in range(DEPTH)], axis=0))
        m["bm"] = _f32(np.concatenate([np.asarray(b_mod[l][k * 3072:(k + 1) * 3072]).reshape(24, 128).T
                                       for l in range(DEPTH)], axis=1))
        pvA = []
        for j in range(2):
            cols = []
            for pi_ in range(4):
                hd = 4 * k + pi_
                sl = slice(hd * 128, (hd + 1) * 128)
                for d_ in range(2):
                    cols += [a_lb_logits[0, d_, sl], a_lb_logits[j, d_, sl]]
            for pi_ in range(4):
                hd = 4 * k + pi_
                cols.append(a_onorm[j][hd * 128:(hd + 1) * 128])
            pvA.append(np.stack(cols, axis=1))
        m["pvA"] = _f32(np.stack(pvA))
        pvB, gwB = [], []
        for j in range(2):
            cols, gws = [], []
            for u in range(2):
                blk = 2 * k + u
                sl = slice(blk * 256, (blk + 1) * 256)

                def h2(v_):
                    return np.asarray(v_[sl], np.float32).reshape(2, 128)
                cols += [h2(b_conv_w[j][jj])[cc] for cc in range(2) for jj in range(4)] + \
                        [h2(b_conv_b[j])[cc] for cc in range(2)] + \
                        [h2(b_gate_b[j][d_, gi_])[cc] for d_ in range(2) for gi_ in range(2) for cc in range(2)] + \
                        [h2(b_lambda[j][d_])[cc] for d_ in range(2) for cc in range(2)]
                gws += [b_gate_w[j][d_, gi_, blk] for d_ in range(2) for gi_ in range(2)]
            pvB.append(np.stack(cols, axis=1))
            gwB.append(np.stack(gws))
        m["pvB"] = _f32(np.stack(pvB))
        m["gwB"] = _f32(np.stack(gwB))
        ims.append(m)
    return ims


def kernel(**inputs):
    if 'nc' not in _NC:
        _NC['nc'] = build_fused()
    nc = _NC['nc']
    ims = make_in_maps(**inputs)
    res = run_bass_kernel_spmd(nc, ims, core_ids=list(range(NCORES))).results
    out = np.concatenate([res[r]["oT"].T for r in range(NCORES)], axis=0).reshape(B_, SEQ_, D)
    return np.ascontiguousarray(out, dtype=np.float32)
```

```python
import os
from contextlib import ExitStack
import numpy as np
import concourse.bass as bass
import concourse.mybir as mybir
from concourse.bass import ds
from concourse.bass_utils import run_bass_kernel_spmd

F32 = mybir.dt.float32
BF16 = mybir.dt.bfloat16
AF = mybir.ActivationFunctionType
ALU = mybir.AluOpType

D = 2048
KC = 16
DFF = 8192
EPS = 1e-6
NCORES = 8
B_, SEQ_, CTX_ = 2, 8192, 256
DEPTH = 4
NLAYERS = int(os.environ.get('KDBG_NLAYERS', '4'))
DBG_MOD = ''
DBG_STOP = os.environ.get('KDBG_STOP') or None
GROUPS4 = [[0, 1, 2, 3], [4, 5, 6, 7]]


class Bld:
    ENG = ['pe', 'act', 'dve', 'pool', 'sp']
    NDS = 8

    def __init__(self, nc, es):
        self.nc = nc
        self.es = es
        self.q = {e: [] for e in self.ENG}
        self.cnt = {e: 0 for e in self.ENG + ['cc']}
        self.sem = {e: es.enter_context(nc.semaphore('s_' + e)) for e in ['pe', 'act', 'dve', 'pool', 'cc']}
        self.dsem = {e: [es.enter_context(nc.semaphore('d_%s%d' % (e, i))) for i in range(self.NDS)]
                     for e in ['sp', 'pool', 'act']}
        self.dcnt = {e: 0 for e in ['sp', 'pool', 'act']}
        self.lastw = {}
        self.readers = {}
        self.waited = {e: {} for e in self.ENG}
        self.uid = 0

    def sb(self, name, shape, dtype):
        self.uid += 1
        return self.es.enter_context(self.nc.sbuf_tensor('sb%d_%s' % (self.uid, name), shape, dtype))

    def ps(self, name, shape, dtype=F32):
        self.uid += 1
        return self.es.enter_context(self.nc.psum_tensor('ps%d_%s' % (self.uid, name), shape, dtype))

    def _wait(self, eng, tok):
        kind, e2, n = tok
        if kind == 'c':
            if e2 == eng and eng == 'pe':
                return
            sem = self.sem[e2]
            val = n
            key = ('c', e2)
        else:
            sem = self.dsem[e2][n % self.NDS]
            val = 16 * (n // self.NDS + 1)
            key = ('d', e2, n % self.NDS)
        if self.waited[eng].get(key, 0) >= val:
            return
        self.waited[eng][key] = val
        self.q[eng].append(lambda E, sem=sem, val=val: E.wait_ge(sem, val))

    def _deps(self, eng, reads, writes):
        toks = set()
        for k in reads:
            if k in self.lastw:
                toks.add(self.lastw[k])
        for k in writes:
            if k in self.lastw:
                toks.add(self.lastw[k])
            for t in self.readers.get(k, {}).values():
                if t[0] == 'c' and t[1] == eng:
                    continue
                toks.add(t)
        for t in toks:
            self._wait(eng, t)

    def _commit(self, tok, reads, writes):
        for k in writes:
            self.lastw[k] = tok
            self.readers[k] = {}
        for k in reads:
            r = self.readers.setdefault(k, {})
            if tok[0] == 'c':
                r[('c', tok[1])] = tok
            else:
                r[tok] = tok

    def op(self, eng, fn, reads=(), writes=()):
        self._deps(eng, reads, writes)
        self.cnt[eng] += 1
        n = self.cnt[eng]
        sem = self.sem[eng]
        self.q[eng].append(lambda E, fn=fn, sem=sem: fn(E).then_inc(sem, 1))
        self._commit(('c', eng, n), reads, writes)

    def dma(self, qe, out, in_, reads=(), writes=()):
        i = self.dcnt[qe]
        self.dcnt[qe] += 1
        if i >= self.NDS:
            self._wait(qe, ('d', qe, i - self.NDS))
        self._deps(qe, reads, writes)
        sem = self.dsem[qe][i % self.NDS]

        def f(E, out=out, in_=in_, sem=sem):
            o = out(E) if callable(out) else out
            s = in_(E) if callable(in_) else in_
            try:
                E.dma_start(out=o, in_=s).then_inc(sem, 16)
            except Exception:
                print("DMA build failed: out=", o, " in=", s)
                raise
        self.q[qe].append(f)
        self._commit(('d', qe, i), reads, writes)

    def cc(self, kind, groups, in_ap, out_ap, reads=(), writes=()):
        self._deps('pool', reads, writes)
        self.cnt['cc'] += 1
        n = self.cnt['cc']
        sem = self.sem['cc']
        self.q['pool'].append(lambda E: E.collective_compute(
            kind, ALU.bypass, replica_groups=groups, ins=[in_ap], outs=[out_ap]).then_inc(sem, 1))
        self._commit(('c', 'cc', n), reads, writes)

    def barrier(self):
        for e in self.ENG:
            for e2 in ['pe', 'act', 'dve', 'pool', 'cc']:
                if self.cnt[e2] > 0:
                    self._wait(e, ('c', e2, self.cnt[e2]))
            for qe in ['sp', 'pool', 'act']:
                for i in range(max(0, self.dcnt[qe] - self.NDS), self.dcnt[qe]):
                    self._wait(e, ('d', qe, i))

    def flush(self):
        self.barrier()
        _PID.clear()
        q = self.q
        with self.nc.Block() as block:
            @block.tensor
            def _(E):
                for f in q['pe']:
                    f(E)

            @block.scalar
            def _(E):
                for f in q['act']:
                    f(E)

            @block.vector
            def _(E):
                for f in q['dve']:
                    f(E)

            @block.gpsimd
            def _(E):
                for f in q['pool']:
                    f(E)

            @block.sync
            def _(E):
                for f in q['sp']:
                    f(E)
        self.q = {e: [] for e in self.ENG}
        for a in ('wbuf',):
            if hasattr(self, a):
                delattr(self, a)


_PID = {}
_XQ = {'i': 0}


def _pid(E):
    key = id(E)
    if key not in _PID:
        _PID[key] = {'p': E.partition_id()}
    return _PID[key]


def phase_extract(b, items):
    qe = ['sp', 'act'][_XQ['i'] % 2]
    _XQ['i'] += 1
    for (gat, mine, nsrc, rps, rm, T, which, kin, kout) in items:
        def src(E, gat=gat, nsrc=nsrc, rps=rps, rm=rm, T=T, which=which):
            d = _pid(E)
            bk = (which, rm * T)
            if bk not in d:
                idv = (d['p'] % 4) if which == 'k' else (d['p'] // 4)
                d[bk] = E.compute_val(idv * (rm * T))
            return bass.AP(tensor=gat, offset=d[bk], ap=[[rps * T, nsrc], [T, rm], [1, T]])
        b.dma(qe, mine.rearrange("(i r) t -> i r t", i=nsrc), src, reads=[kin], writes=[kout])
    b.flush()


CH = 64


def gather_rows(b, send, gall, c0, c1, rkeys, wkey):
    for c in range(c0, c1):
        b.cc("AllGather", GROUPS4, send.ap()[c * CH:(c + 1) * CH, :], gall.ap()[c * 4 * CH:(c + 1) * 4 * CH, :],
             reads=rkeys, writes=[wkey])


def mine_rows(ap2d, i, lo, half):
    r0 = ((lo // CH + half) * 4 + i) * CH
    return ap2d[r0:r0 + CH, :]


def write_nat(b, MS, f0, t0, n, src, rkeys, TLc, TCc):
    if t0 < CTX_:
        for i in range(4):
            b.dma('sp', MS[i * 512 + f0:i * 512 + f0 + 128, TLc:TLc + TCc], src[:, i * TCc:(i + 1) * TCc],
                  reads=rkeys, writes=['Msend'])
    else:
        a_ = t0 - CTX_
        i, loc = a_ // TLc, a_ % TLc
        b.dma('sp', MS[i * 512 + f0:i * 512 + f0 + 128, loc:loc + n], src[:, :n], reads=rkeys, writes=['Msend'])


def blocks_for(T, TL):
    bl = []
    t = 0
    while t < TL:
        n = min(512, TL - t)
        bl.append((t, n, 0))
        t += n
    while t < T:
        n = min(512, T - t)
        bl.append((t, n, 1))
        t += n
    return bl


class NormMod:
    def __init__(self, b, tag, ones32, epsb, gain, sc, sh, pkeys):
        self.b = b
        self.tag = tag
        self.ones32 = ones32
        self.epsb = epsb
        self.gain = gain
        self.sh = sh
        self.pkeys = list(pkeys)
        self.sq = b.sb('sq_' + tag, [128, KC, 512], F32)
        self.rs = b.sb('rs_' + tag, [128, 512], F32)
        self.pss = b.ps('pss_' + tag, [128, 512])
        self.gm = b.sb('gm_' + tag, [128, 2, KC], F32)
        for g in range(2):
            b.op('dve', lambda E, g=g: E.scalar_tensor_tensor(
                out=self.gm[:, g, :], in0=sc[g], scalar=1.0, in1=gain, op0=ALU.add, op1=ALU.mult),
                reads=self.pkeys, writes=['gm_' + tag])

    def emit(self, xs, xkey, n, g, hout, hkey, plain=False):
        b = self.b
        tag = self.tag
        sq, rs, pss = self.sq, self.rs, self.pss
        tmp = sq
        b.op('act', lambda E: E.activation(out=sq[:, :, :n], in_=xs, func=AF.Square),
             reads=[xkey], writes=[('sq_' + tag, kc) for kc in range(KC)])
        for kc in range(KC):
            b.op('pe', lambda E, kc=kc: E.matmul(pss[:, :n], lhsT=self.ones32[:, :], rhs=sq[:, kc, :n],
                                                 start=(kc == 0), stop=(kc == KC - 1)),
                 reads=[('sq_' + tag, kc), 'ones32'], writes=['pss_' + tag])
        b.op('act', lambda E: E.activation(out=rs[:, :n], in_=pss[:, :n], func=AF.Sqrt,
                                           scale=1.0 / D, bias=self.epsb[:, 0:1]),
             reads=['pss_' + tag, 'epsb'], writes=['rs_' + tag])
        b.op('dve', lambda E: E.reciprocal(out=rs[:, :n], in_=rs[:, :n]),
             reads=['rs_' + tag], writes=['rs_' + tag])
        for kc in range(KC):
            if plain:
                gcol = self.gain[:, kc:kc + 1]
                b.op('dve', lambda E, kc=kc, gcol=gcol: E.scalar_tensor_tensor(
                    out=hout[:, kc, :], in0=xs[:, kc, :], scalar=gcol, in1=rs[:, :n],
                    op0=ALU.mult, op1=ALU.mult),
                    reads=[xkey, 'rs_' + tag] + self.pkeys, writes=[hkey])
                continue
            gcol = self.gm[:, g, kc:kc + 1]
            shcol = self.sh[g][:, kc:kc + 1]
            b.op('dve', lambda E, kc=kc, gcol=gcol: E.scalar_tensor_tensor(
                out=tmp[:, kc, :n], in0=xs[:, kc, :], scalar=gcol, in1=rs[:, :n],
                op0=ALU.mult, op1=ALU.mult),
                reads=[xkey, 'rs_' + tag, 'gm_' + tag], writes=[('sq_' + tag, kc)])
            b.op('act', lambda E, kc=kc, shcol=shcol: E.activation(
                out=hout[:, kc, :], in_=tmp[:, kc, :n], func=AF.Identity, bias=shcol, scale=1.0),
                reads=[('sq_' + tag, kc)] + self.pkeys, writes=[hkey])


def gemm_stream(b, w_dram, K, N, ngrp, rhs_fn, rhs_keys, tblocks, epi, psums, after_group=None, cache=None):
    kcs = K // 128
    if not hasattr(b, 'wbuf'):
        b.wbuf = [b.sb('wbuf%d' % i, [128, 8192], BF16) for i in range(2)]
        b.wi = 0
        b.pi = 0
    wv = w_dram.rearrange("(kc p) n -> p kc n", p=128)
    for gi in range(N // ngrp):
        wflat = b.wbuf[b.wi % 2]
        wt = wflat[:, :kcs * ngrp].rearrange("p (k n) -> p k n", k=kcs)
        wk = ('wbuf', b.wi % 2)
        b.wi += 1
        if cache is not None and cache[2] == 'use':
            ck = ('wcache', cache[1] + gi)
            b.dma('sp', wflat[:, :], cache[0][cache[1] + gi], reads=[ck], writes=[wk])
        else:
            b.dma('pool', wt[:], wv[:, :, gi * ngrp:(gi + 1) * ngrp], writes=[wk])
            if cache is not None:
                ck = ('wcache', cache[1] + gi)
                b.dma('sp', cache[0][cache[1] + gi], wflat[:, :], reads=[wk], writes=[ck])
        for (t0, n, g) in tblocks:
            for c in range(ngrp // 128):
                ps, pk = psums[b.pi % len(psums)]
                b.pi += 1
                for kc in range(kcs):
                    b.op('pe', lambda E, kc=kc, c=c, ps=ps, wt=wt, t0=t0, n=n: E.matmul(
                        ps[:, :n], lhsT=wt[:, kc, c * 128:(c + 1) * 128], rhs=rhs_fn(kc, t0, n),
                        start=(kc == 0), stop=(kc == kcs - 1)),
                        reads=[wk] + rhs_keys, writes=[pk])
                epi(gi * (ngrp // 128) + c, t0, n, g, ps, pk)
        if after_group is not None:
            after_group(gi)


class Ctx:
    pass


def phase_mod(b, C):
    NCH = 24
    with ExitStack() as es:
        b.es = es
        c_sb = b.sb('c', [128, KC, 4], F32)
        s_sb = b.sb('s', [128, KC, 4], F32)
        bm_sb = b.sb('bm', [128, DEPTH * NCH], F32)
        o_sb = b.sb('o', [128, 4, DEPTH * NCH], F32)
        b.dma('sp', c_sb[:], C.cT.rearrange("(kc p) n -> p kc n", p=128), writes=['c'])
        b.dma('sp', bm_sb[:], C.bm, writes=['bm'])
        b.dma('sp', C.gains_sb[:], C.gains, writes=['gains'])
        b.op('act', lambda E: E.activation(out=s_sb[:], in_=c_sb[:], func=AF.Silu), reads=['c'], writes=['s'])
        wb = [b.sb('wm%d' % i, [128, KC, 512], F32) for i in range(2)]
        pss = [b.ps('pm%d' % i, [128, 4]) for i in range(4)]
        wv = C.wm.rearrange("(l kc p) n -> p l kc n", p=128, kc=KC)
        gi = 0
        for l in range(DEPTH):
            for gq in range(NCH // 4):
                wt = wb[gi % 2]
                wk = ('wm', gi % 2)
                gi += 1
                b.dma('sp', wt[:], wv[:, l, :, gq * 512:(gq + 1) * 512], writes=[wk])
                for c in range(4):
                    j = l * NCH + gq * 4 + c
                    ps = pss[j % 4]
                    for kc in range(KC):
                        b.op('pe', lambda E, kc=kc, c=c, wt=wt, ps=ps: E.matmul(
                            ps[:, :], lhsT=wt[:, kc, c * 128:(c + 1) * 128], rhs=s_sb[:, kc, :],
                            start=(kc == 0), stop=(kc == KC - 1)), reads=[wk, 's'], writes=[('pm', j % 4)])
                    b.op('dve', lambda E, j=j, ps=ps: E.tensor_scalar(
                        out=o_sb[:, :, j], in0=ps[:, :], scalar1=bm_sb[:, j:j + 1], scalar2=None, op0=ALU.add),
                        reads=[('pm', j % 4), 'bm'], writes=['o'])
        b.dma('sp', C.modsend.ap().rearrange("(q p) j -> p q j", p=128), o_sb[:], reads=['o'], writes=['modsend'])
        b.cc("AllGather", GROUPS4, C.modsend.ap().opt(), C.modall.ap().opt(), reads=['modsend'], writes=['modall'])
        b.flush()
    phase_extract(b, [(C.modall, C.modmine.ap(), 4, 512, 128, DEPTH * NCH, 'b', 'modall', 'modmine')])
    MA = C.modall.ap()
    MM = C.modmine.ap()
    for l in range(DEPTH):
        for i in range(4):
            b.dma('sp', C.modtab[:, 0, l, i * NCH:(i + 1) * NCH], MM[i * 128:(i + 1) * 128, l * NCH:(l + 1) * NCH],
                  reads=['modmine'], writes=['modtab'])
            b.dma('sp', C.modtab[:, 1, l, i * NCH:(i + 1) * NCH], MA[i * 512 + 256:i * 512 + 384, l * NCH:(l + 1) * NCH],
                  reads=['modall'], writes=['modtab'])
    b.flush()


def mod_ap(C, st, layer, m):
    return C.modtab[:, st, layer, m * 16:(m + 1) * 16]


def phase_proj(b, C, layer, x_dram, w, NP):
    T, TL = C.Tc, C.TLc
    tbl = blocks_for(T, TL)
    with ExitStack() as es:
        b.es = es
        nm = NormMod(b, 'n1', C.ones32, C.epsb, C.gains_sb[:, layer, :],
                     [mod_ap(C, 0, layer, 1), mod_ap(C, 1, layer, 1)],
                     [mod_ap(C, 0, layer, 0), mod_ap(C, 1, layer, 0)], ['modtab', 'gains'])
        h = b.sb('h', [128, KC, T], BF16)
        xb = [b.sb('xb%d' % i, [128, KC, 512], F32) for i in range(1)]
        xv = x_dram.rearrange("(kc p) t -> p kc t", p=128)
        for bi, (t0, n, g) in enumerate(tbl):
            xs = xb[0]
            b.dma('sp', xs[:, :, :n], xv[:, :, t0:t0 + n], reads=['xdram'], writes=[('xb', 0)])
            nm.emit(xs[:, :, :n], ('xb', 0), n, g, h[:, :, t0:t0 + n], 'h')
        psums = [(b.ps('pp%d' % i, [128, 512]), ('pp', i)) for i in range(4)]
        ob = [b.sb('ob%d' % i, [128, 512], F32) for i in range(4)]
        pvs = [C.Psend[s_].ap().rearrange("(c p) t -> p c t", p=128) for s_ in range(5)]
        st = {'i': 0}

        def epi(nch, t0, n, g, ps, pk):
            i = st['i'] % 4
            st['i'] += 1
            o = ob[i]
            if i % 2:
                b.op('act', lambda E: E.activation(out=o[:, :n], in_=ps[:, :n], func=AF.Copy),
                     reads=[pk], writes=[('ob', i)])
            else:
                b.op('dve', lambda E: E.tensor_copy(out=o[:, :n], in_=ps[:, :n]),
                     reads=[pk], writes=[('ob', i)])
            b.dma('sp', pvs[nch // 16][:, nch % 16, t0:t0 + n], o[:, :n], reads=[('ob', i)],
                  writes=[('Psend', nch // 16)])

        def after_group(gi):
            if gi % 4 == 3:
                s_ = gi // 4
                gather_rows(b, C.Psend[s_], C.Pall[s_], 0, 2048 // CH, [('Psend', s_)], ('Pall', s_))

        gemm_stream(b, w, D, NP, 512, lambda kc, t0, n: h[:, kc, t0:t0 + n], ['h'], tbl, epi, psums, after_group)
        b.flush()


def phase_l3(b, C, layer, x_dram, wo, w1, w2, out_dram, final):
    T, TL = (C.TLc, C.TLc) if final else (C.Tc, C.TLc)
    tbl = blocks_for(T, TL)
    with ExitStack() as es:
        b.es = es
        nm = NormMod(b, 'n2', C.ones32, C.epsb, C.gains_sb[:, 4 + layer, :],
                     [mod_ap(C, 0, layer, 4), mod_ap(C, 1, layer, 4)],
                     [mod_ap(C, 0, layer, 3), mod_ap(C, 1, layer, 3)], ['modtab', 'gains'])
        if final:
            nf = NormMod.__new__(NormMod)
            nf.__dict__.update(nm.__dict__)
            nf.gain = C.gains_sb[:, 8, :]
        gate = {2: [mod_ap(C, 0, layer, 2), mod_ap(C, 1, layer, 2)],
                5: [mod_ap(C, 0, layer, 5), mod_ap(C, 1, layer, 5)]}
        xs = b.sb('xs', [128, KC, 512], F32)
        hb = b.sb('hb', [128, KC, 512], BF16)
        hid = b.sb('hid', [128, 32, 512], BF16)
        r32 = [b.sb('r32_%d' % i, [128, 512], F32) for i in range(2)]
        psums = [(b.ps('pp%d' % i, [128, 512]), ('pp', i)) for i in range(4)]
        xv = x_dram.rearrange("(kc p) t -> p kc t", p=128)
        mv = C.Mmine.ap().rearrange("(fc hf kk r) t -> hf r kk fc t", fc=4, hf=2, kk=4, r=CH)
        ov = out_dram.rearrange("(kc p) t -> p kc t", p=128)
        for bix, (t0, n, g) in enumerate(tbl):
            cm_ = 'fill' if bix == 0 else 'use'
            b.dma('sp', xs[:, :, :n], xv[:, :, t0:t0 + n], reads=['xdram'], writes=['xs'])
            for half in range(2):
                for kk in range(4):
                    b.dma('pool', hb[half * CH:(half + 1) * CH, kk * 4:(kk + 1) * 4, :n],
                          mv[half][:, kk, :, t0:t0 + n], reads=['Mmine'], writes=['hb'])

            def epi_res(m):
                def epi(nch, t0_, n_, g_, ps, pk):
                    gcol = gate[m][g_][:, nch:nch + 1]
                    b.op('dve', lambda E: E.scalar_tensor_tensor(
                        out=xs[:, nch, :n_], in0=ps[:, :n_], scalar=gcol, in1=xs[:, nch, :n_],
                        op0=ALU.mult, op1=ALU.add), reads=[pk, 'modtab', 'xs'], writes=['xs'])
                return epi
            blk = [(0, n, g)]
            gemm_stream(b, wo, D, D, 512, lambda kc, t0_, n_: hb[:, kc, :n_], ['hb'], blk, epi_res(2), psums,
                        cache=(C.wcache, 0, cm_))
            nm.emit(xs[:, :, :n], 'xs', n, g, hb[:, :, :n], 'hb')
            for half in range(2):
                def epi_h(nch, t0_, n_, g_, ps, pk):
                    ri = b.pi % 2
                    r = r32[ri]
                    b.op('act', lambda E: E.activation(out=r[:, :n_], in_=ps[:, :n_], func=AF.Relu),
                         reads=[pk], writes=[('r32', ri)])
                    b.op('dve', lambda E: E.tensor_tensor(out=hid[:, nch, :n_], in0=r[:, :n_], in1=r[:, :n_],
                                                          op=ALU.mult),
                         reads=[('r32', ri)], writes=[('hid', nch)])
                gemm_stream(b, w1[:, half * 4096:(half + 1) * 4096], D, 4096, 512,
                            lambda kc, t0_, n_: hb[:, kc, :n_], ['hb'], blk, epi_h, psums,
                            cache=(C.wcache, 4 + half * 8, cm_))
                gemm_stream(b, w2[half * 4096:(half + 1) * 4096, :], 4096, D, 256,
                            lambda kc, t0_, n_: hid[:, kc, :n_], [('hid', i) for i in range(32)], blk,
                            epi_res(5), psums, cache=(C.wcache, 20 + half * 8, cm_))
            if final:
                nf.emit(xs[:, :, :n], 'xs', n, g, xs[:, :, :n], 'xs', plain=True)
            b.dma('sp', ov[:, :, t0:t0 + n], xs[:, :, :n], reads=['xs'], writes=['xdram'])
        b.flush()


def phase_scan_b(b, C, j):
    SC, SL = CTX_, SEQ_
    TS = SC + SL
    LP = TS + 6
    NU = 2
    blks = [(0, SC, 0)]
    t = 0
    while t < SL:
        blks.append((SC + 3 + t, 512, SC + t))
        t += 512
    nctx = 1
    PAs = [C.Pmine[s_].ap() for s_ in range(2)]
    TLc, TCc = C.TLc, C.TCc


    with ExitStack() as es:
        b.es = es
        pv_sb = b.sb('pv', [128, NU * 22], F32)
        b.dma('sp', pv_sb[:], C.pvB[j], writes=['pv'])
        one = b.sb('one', [128, 1], F32)
        b.op('dve', lambda E: E.memset(one[:], 1.0), writes=['one'])
        zt = b.sb('zt', [128, 4], F32)
        b.op('dve', lambda E: E.memset(zt[:], 0.0), writes=['zt'])
        cdec = b.sb('cdec', [128, NU * 4], F32)
        for u in range(NU):
            cs = slice(u * 4, u * 4 + 4)
            b.op('act', lambda E, u=u, cs=cs: E.activation(out=cdec[:, cs], in_=pv_sb[:, u * 22 + 18:u * 22 + 22],
                                                          func=AF.Exp, scale=-1.0), reads=['pv'], writes=['cdec'])
        b.op('act', lambda E: E.activation(out=cdec[:], in_=cdec[:], func=AF.Ln, bias=one[:, 0:1], scale=1.0),
             reads=['cdec', 'one'], writes=['cdec'])
        b.op('dve', lambda E: E.tensor_scalar(out=cdec[:], in0=cdec[:], scalar1=-8.0, scalar2=None, op0=ALU.mult),
             reads=['cdec'], writes=['cdec'])
        gwb = b.sb('gwb', [128, NU * 4, 2, 256], BF16)
        b.dma('pool', gwb[:], C.gwB[j].rearrange("g (kc p) n -> p g kc n", p=128), writes=['gwb'])
        UP = C.Upad.ap()
        for u in range(NU):
            for cc in range(2):
                for (c0, w_) in ((0, 2), (SC + 2, 3), (LP - 1, 2)):
                    b.dma('sp', UP[u, cc, :, c0:c0 + w_], zt[:, :w_], reads=['zt'], writes=['Upad'])
                for i in range(4):
                    for half in range(2):
                        ps_ = slice(half * CH, (half + 1) * CH)
                        src_ = mine_rows(PAs[1], i, u * 256 + cc * 128, half)
                        b.dma('sp', UP[u, cc, ps_, 2 + i * TCc:2 + (i + 1) * TCc], src_[:, TLc:TLc + TCc],
                              reads=[('Pmine', 1)], writes=['Upad'])
                        b.dma('sp', UP[u, cc, ps_, SC + 5 + i * TLc:SC + 5 + (i + 1) * TLc], src_[:, 0:TLc],
                              reads=[('Pmine', 1)], writes=['Upad'])
        uc32 = b.sb('uc32', [128, 2, TS], F32)
        uc16 = b.sb('uc16', [128, 2, TS], BF16)
        Hf = b.sb('Hf', [128, TS], F32)
        ub = [b.sb('ub%d' % i, [128, 2, 515], F32) for i in range(2)]
        W = {}
        for nm_ in ['r', 'i', 'a', 'a2', 'xin', 'y', 's', 'mo']:
            W[nm_] = [b.sb('w_%s%d' % (nm_, i), [128, 512], F32) for i in range(2)]
        hbr = [b.sb('hbr%d' % i, [128, 512], F32) for i in range(2)]
        pz = [b.ps('pz%d' % i, [128, 512]) for i in range(4)]
        cnt = {'k': 0}
        MS = C.Msend.ap()
        for u in range(NU):
            pb = u * 22
            for bi, (p0, n, t0) in enumerate(blks):
                ub_ = ub[bi % 2]
                uk = ('ub', bi % 2)
                for cc in range(2):
                    b.dma('sp', ub_[:, cc, :n + 3], UP[u, cc, :, p0:p0 + n + 3], reads=['Upad'], writes=[uk])
                for cc in range(2):
                    dst = uc32[:, cc, t0:t0 + n]
                    b.op('dve', lambda E, ub_=ub_, cc=cc, n=n, dst=dst, pb=pb: E.tensor_scalar(
                        out=dst, in0=ub_[:, cc, 0:n], scalar1=pv_sb[:, pb + cc * 4:pb + cc * 4 + 1],
                        scalar2=pv_sb[:, pb + 8 + cc:pb + 9 + cc], op0=ALU.mult, op1=ALU.add),
                        reads=[uk, 'pv'], writes=[('uc32', cc)])
                    for jj in range(1, 4):
                        b.op('dve', lambda E, ub_=ub_, cc=cc, n=n, jj=jj, dst=dst, pb=pb: E.scalar_tensor_tensor(
                            out=dst, in0=ub_[:, cc, jj:jj + n], scalar=pv_sb[:, pb + cc * 4 + jj:pb + cc * 4 + jj + 1],
                            in1=dst, op0=ALU.mult, op1=ALU.add), reads=[uk, 'pv', ('uc32', cc)], writes=[('uc32', cc)])
                    b.op('act', lambda E, cc=cc, t0=t0, n=n, dst=dst: E.activation(
                        out=uc16[:, cc, t0:t0 + n], in_=dst, func=AF.Copy),
                        reads=[('uc32', cc)], writes=[('uc16', cc)])
            for oc in range(2):
                for d in range(2):
                    if d == 0:
                        order = list(range(len(blks)))
                    else:
                        order = list(range(nctx - 1, -1, -1)) + list(range(len(blks) - 1, nctx - 1, -1))
                    prev = None
                    for bi in order:
                        p0, n, t0 = blks[bi]
                        k = cnt['k'] % 2
                        cnt['k'] += 1
                        zr, zi = pz[2 * k], pz[2 * k + 1]
                        for gi_, zp in ((0, zr), (1, zi)):
                            for kc in range(2):
                                b.op('pe', lambda E, zp=zp, gi_=gi_, kc=kc, t0=t0, n=n, d=d, oc=oc, u=u: E.matmul(
                                    zp[:, :n], lhsT=gwb[:, u * 4 + d * 2 + gi_, kc, oc * 128:(oc + 1) * 128],
                                    rhs=uc16[:, kc, t0:t0 + n], start=(kc == 0), stop=(kc == 1)),
                                    reads=['gwb', ('uc16', 0), ('uc16', 1)], writes=[('pz', 2 * k + gi_)])
                        r, ig, a, a2, xin = W['r'][k], W['i'][k], W['a'][k], W['a2'][k], W['xin'][k]
                        c_r = pb + 10 + (d * 2 + 0) * 2 + oc
                        c_i = pb + 10 + (d * 2 + 1) * 2 + oc
                        br = pv_sb[:, c_r:c_r + 1]
                        bi_ = pv_sb[:, c_i:c_i + 1]
                        cd = cdec[:, u * 4 + d * 2 + oc: u * 4 + d * 2 + oc + 1]
                        b.op('act', lambda E, r=r, zr=zr, n=n, br=br: E.activation(
                            out=r[:, :n], in_=zr[:, :n], func=AF.Sigmoid, bias=br, scale=1.0),
                            reads=[('pz', 2 * k), 'pv'], writes=[('r', k)])
                        b.op('act', lambda E, ig=ig, zi=zi, n=n, bi_=bi_: E.activation(
                            out=ig[:, :n], in_=zi[:, :n], func=AF.Sigmoid, bias=bi_, scale=1.0),
                            reads=[('pz', 2 * k + 1), 'pv'], writes=[('i', k)])
                        b.op('act', lambda E, a=a, r=r, n=n, cd=cd: E.activation(
                            out=a[:, :n], in_=r[:, :n], func=AF.Exp, scale=cd),
                            reads=[('r', k), 'cdec'], writes=[('a', k)])
                        b.op('pool', lambda E, a=a, a2=a2, n=n: E.tensor_tensor(
                            out=a2[:, :n], in0=a[:, :n], in1=a[:, :n], op=ALU.mult),
                            reads=[('a', k)], writes=[('a2', k)])
                        b.op('act', lambda E, a2=a2, n=n: E.activation(
                            out=a2[:, :n], in_=a2[:, :n], func=AF.Sqrt, scale=-1.0, bias=one[:, 0:1]),
                            reads=[('a2', k), 'one'], writes=[('a2', k)])
                        b.op('pool', lambda E, a2=a2, ig=ig, xin=xin, n=n: E.tensor_tensor(
                            out=xin[:, :n], in0=a2[:, :n], in1=ig[:, :n], op=ALU.mult),
                            reads=[('a2', k), ('i', k)], writes=[('xin', k)])
                        b.op('dve', lambda E, xin=xin, n=n, t0=t0, oc=oc: E.tensor_tensor(
                            out=xin[:, :n], in0=xin[:, :n], in1=uc32[:, oc, t0:t0 + n], op=ALU.mult),
                            reads=[('xin', k), ('uc32', oc)], writes=[('xin', k)])
                        if d == 0:
                            init = 0.0 if prev is None else Hf[:, t0 - 1:t0]
                            b.op('dve', lambda E, a=a, xin=xin, n=n, t0=t0, init=init: E.tensor_tensor_scan(
                                out=Hf[:, t0:t0 + n], data0=a[:, :n], data1=xin[:, :n], initial=init,
                                op0=ALU.mult, op1=ALU.add), reads=[('a', k), ('xin', k), 'Hf'], writes=['Hf'])
                            prev = bi
                        else:
                            hb_ = hbr[k]
                            if prev is None:
                                init = 0.0
                                rkeys = []
                            else:
                                pk_, pn_ = prev
                                init = hbr[pk_][:, pn_ - 1:pn_]
                                rkeys = [('hbr', pk_)]
                            b.op('dve', lambda E, a=a, xin=xin, n=n, hb_=hb_, init=init: E.tensor_tensor_scan(
                                out=hb_[:, :n], data0=a[:, :n][:, ::-1], data1=xin[:, :n][:, ::-1], initial=init,
                                op0=ALU.mult, op1=ALU.add), reads=[('a', k), ('xin', k)] + rkeys, writes=[('hbr', k)])
                            prev = (k, n)
                            yb, sb_, mo_ = W['y'][k], W['s'][k], W['mo'][k]
                            for half in range(2):
                                ps_ = slice(half * CH, (half + 1) * CH)
                                if t0 < SC:
                                    for i in range(4):
                                        b.dma('sp', yb[ps_, i * TCc:(i + 1) * TCc],
                                              mine_rows(PAs[0], i, u * 256 + oc * 128, half)[:, TLc:TLc + TCc],
                                              reads=[('Pmine', 0)], writes=[('y', k)])
                                else:
                                    a_ = t0 - SC
                                    i, loc = a_ // TLc, a_ % TLc
                                    b.dma('sp', yb[ps_, :n], mine_rows(PAs[0], i, u * 256 + oc * 128, half)[:, loc:loc + n],
                                          reads=[('Pmine', 0)], writes=[('y', k)])
                            b.op('act', lambda E, yb=yb, n=n: E.activation(
                                out=yb[:, :n], in_=yb[:, :n], func=AF.Gelu_apprx_tanh),
                                reads=[('y', k)], writes=[('y', k)])
                            b.op('dve', lambda E, sb_=sb_, hb_=hb_, n=n, t0=t0: E.tensor_tensor(
                                out=sb_[:, :n], in0=Hf[:, t0:t0 + n], in1=hb_[:, :n][:, ::-1], op=ALU.add),
                                reads=['Hf', ('hbr', k)], writes=[('s', k)])
                            b.op('pool', lambda E, sb_=sb_, yb=yb, mo_=mo_, n=n: E.tensor_tensor(
                                out=mo_[:, :n], in0=sb_[:, :n], in1=yb[:, :n], op=ALU.mult),
                                reads=[('s', k), ('y', k)], writes=[('mo', k)])
                            write_nat(b, MS, u * 256 + oc * 128, t0, n, mo_, [('mo', k)], TLc, TCc)
                    if d == 1:
                        f0_ = u * 256 + oc * 128
                        for dest in range(4):
                            c_ = (dest * 512 + f0_) // CH
                            gather_rows(b, C.Msend, C.Mall, c_, c_ + 2, ['Msend'], 'Mall')
        b.flush()


def phase_scan_a(b, C, j, lbmode):
    SC, SL = CTX_, SEQ_
    TS = SC + SL
    NP_, NS = 4, 8
    NT = TS // 128
    TLc, TCc = C.TLc, C.TCc
    PAs = [C.Pmine[s_].ap() for s_ in range(5)]


    def load_nat(q, dst, n, t0, sec, pi_, wkey):
        for half in range(2):
            ps_ = slice(half * CH, (half + 1) * CH)
            if t0 < SC:
                for i in range(4):
                    b.dma(q, dst[ps_, i * TCc:(i + 1) * TCc], mine_rows(PAs[sec], i, pi_ * 128, half)[:, TLc:TLc + TCc],
                          reads=[('Pmine', sec)], writes=[wkey])
            else:
                a_ = t0 - SC
                i, loc = a_ // TLc, a_ % TLc
                b.dma(q, dst[ps_, :n], mine_rows(PAs[sec], i, pi_ * 128, half)[:, loc:loc + n],
                      reads=[('Pmine', sec)], writes=[wkey])

    nat_blks = [(0, SC)] + [(SC + t, 512) for t in range(0, SL, 512)]
    sblk = {0: nat_blks, 1: [(0, SC)] + [(SC + SL - 512 - t, 512) for t in range(0, SL, 512)]}

    with ExitStack() as es:
        b.es = es
        ones32, epsb = C.ones32, C.epsb
        pv_sb = b.sb('pv', [128, 20], F32)
        b.dma('sp', pv_sb[:], C.pvA[j], writes=['pv'])
        cst_sb = b.sb('cst', [128, 897], F32)
        b.dma('sp', cst_sb[:], C.cst, writes=['cst'])
        ident = b.sb('ident', [128, 128], BF16)
        b.op('dve', lambda E: E.tensor_copy(out=ident[:], in_=cst_sb[:, 0:128]), reads=['cst'], writes=['ident'])
        Jm = b.sb('Jm', [128, 128], BF16)
        b.op('dve', lambda E: E.tensor_copy(out=Jm[:], in_=cst_sb[:, 769:897]), reads=['cst'], writes=['Jm'])
        amask = cst_sb[:, 128:256]
        rmask = cst_sb[:, 256:768]
        lb = b.sb('lb', [128, NS], F32)
        oml = b.sb('oml', [128, NS], F32)
        if lbmode:
            pv3 = pv_sb[:, 0:2 * NS].rearrange("p (s two) -> p s two", two=2)
            b.op('dve', lambda E: E.tensor_tensor(out=lb[:], in0=pv3[:, :, 1], in1=pv3[:, :, 0], op=ALU.subtract),
                 reads=['pv'], writes=['lb'])
            b.op('act', lambda E: E.activation(out=lb[:], in_=lb[:], func=AF.Sigmoid), reads=['lb'], writes=['lb'])
            b.op('dve', lambda E: E.tensor_scalar(out=oml[:], in0=lb[:], scalar1=-1.0, scalar2=1.0,
                                                  op0=ALU.mult, op1=ALU.add), reads=['lb'], writes=['oml'])
        O = [b.sb('O%d' % i, [128, TS], F32) for i in range(2)]
        V = [b.sb('V%d' % i, [128, NT, 128], BF16) for i in range(2)]
        vT = [b.sb('vT%d' % i, [128, 512], BF16) for i in range(2)]
        Wk = {}
        for nm_ in ['q', 'z', 'f', 'lf', 'k', 'bc', 'eb', 'enb', 'k32']:
            Wk[nm_] = [b.sb('a_%s%d' % (nm_, i), [128, 512], F32) for i in range(2)]
        Qt = [b.sb('Qt%d' % i, [128, 512], BF16) for i in range(2)]
        Kt = [b.sb('Kt%d' % i, [128, 512], BF16) for i in range(2)]
        KbT = [b.sb('KbT%d' % i, [128, 512], BF16) for i in range(2)]
        Kb = [b.sb('Kb%d' % i, [128, 128], BF16) for i in range(2)]
        Am = [b.sb('Am%d' % i, [128, 128], BF16) for i in range(2)]
        Kbz = [b.sb('Kbz%d' % i, [128, 128], BF16) for i in range(2)]
        S32 = [b.sb('S32_%d' % i, [128, 128], F32) for i in range(2)]
        S16 = [[b.sb('S16_%d_%d' % (i, c), [128, 128], BF16) for c in range(4)] for i in range(2)]
        pT1 = b.ps('pT', [128, 128], BF16)
        pA1 = b.ps('pA', [128, 512])
        pO = [b.ps('pO%d' % i, [128, 128]) for i in range(2)]
        pS = [[b.ps('pS%d_%d' % (i, c), [128, 128]) for c in range(2)] for i in range(2)]
        fb = {}
        for nm_ in ['s', 'sq', 'rs', 'g', 'mo']:
            fb[nm_] = b.sb('f_%s' % nm_, [128, 512], F32)
        pNb = pA1
        MS = C.Msend.ap()

        for pr in range(NP_):
            for ci, (t0, n) in enumerate(nat_blks):
                vt = vT[ci % 2]
                load_nat('pool', vt, n, t0, 1, pr, ('vT', ci % 2))
                for tt in range(n // 128):
                    ti = (t0 + tt * 128) // 128
                    x_ = ti % 2
                    b.op('pe', lambda E, vt=vt, tt=tt, x_=x_: E.transpose(
                        out=pT1[:, :], in_=vt[:, tt * 128:(tt + 1) * 128], identity=ident[:, :]),
                        reads=[('vT', ci % 2), 'ident'], writes=[('pT', 0)])
                    b.op('act', lambda E, ti=ti, x_=x_: E.activation(out=V[0][:, ti, :], in_=pT1[:, :], func=AF.Copy),
                         reads=[('pT', 0)], writes=[('V', 0)])
                    b.op('pe', lambda E, ti=ti, x_=x_: E.matmul(pS[0][x_][:, :], lhsT=Jm[:, :], rhs=V[0][:, ti, :],
                                                               start=True, stop=True),
                         reads=[('V', 0), 'Jm'], writes=[('pS', 0, x_)])
                    b.op('dve', lambda E, ti=ti, x_=x_: E.tensor_copy(out=V[1][:, ti, :], in_=pS[0][x_][:, :]),
                         reads=[('pS', 0, x_)], writes=[('V', 1)])
            for dr in range(2):
                b.op('dve', lambda E, dr=dr: E.memset(S32[dr][:], 0.0), writes=[('S32', dr)])
                b.op('dve', lambda E, dr=dr: E.memset(S16[dr][3][:], 0.0), writes=[('S16', dr, 3)])
            def emit_ps(dr, ti, c):
                cs = slice(c * 32, (c + 1) * 32)
                psl = pS[dr][c % 2][:, :]
                if c < 3:
                    b.op('pe', lambda E: E.matmul(psl, lhsT=Kb[dr][cs, :], rhs=V[dr][cs, ti, :], start=True, stop=True),
                         reads=[('Kb', dr), ('V', dr)], writes=[('pS', dr, c % 2)])
                else:
                    b.op('pe', lambda E: E.matmul(psl, lhsT=Kbz[dr][64:128, :], rhs=V[dr][64:128, ti, :],
                                                  start=True, stop=True),
                         reads=[('Kbz', dr), ('V', dr)], writes=[('pS', dr, c % 2)])

            for bi in range(len(nat_blks)):
                n = nat_blks[bi][1]
                for dr in range(2):
                    s = 2 * pr + dr
                    nt0 = sblk[dr][bi][0]
                    q, z, f, lf, k, bc, eb, enb, k32 = [Wk[x][dr] for x in
                                                        ['q', 'z', 'f', 'lf', 'k', 'bc', 'eb', 'enb', 'k32']]
                    K_ = lambda x, dr=dr: (x, dr)
                    load_nat('sp', q, n, nt0, 0, pr, K_('q'))
                    load_nat('sp', z, n, nt0, 3 + dr, pr, K_('z'))
                    if dr == 0:
                        zin, qin = z[:, :n], q[:, :n]
                    else:
                        zin, qin = z[:, :n][:, ::-1], q[:, :n][:, ::-1]
                    b.op('act', lambda E, zin=zin, f=f, n=n: E.activation(out=f[:, :n], in_=zin, func=AF.Exp, scale=-1.0),
                         reads=[K_('z')], writes=[K_('f')])
                    b.op('pool', lambda E, f=f, n=n: E.tensor_scalar(out=f[:, :n], in0=f[:, :n], scalar1=1.0, scalar2=None,
                                                                    op0=ALU.add), reads=[K_('f')], writes=[K_('f')])
                    b.op('dve', lambda E, f=f, n=n: E.reciprocal(out=f[:, :n], in_=f[:, :n]),
                         reads=[K_('f')], writes=[K_('f')])
                    if lbmode:
                        b.op('dve', lambda E, f=f, n=n, s=s: E.tensor_scalar(
                            out=f[:, :n], in0=f[:, :n], scalar1=oml[:, s:s + 1], scalar2=lb[:, s:s + 1],
                            op0=ALU.mult, op1=ALU.add), reads=[K_('f'), 'lb', 'oml'], writes=[K_('f')])
                    b.op('act', lambda E, f=f, lf=lf, n=n: E.activation(out=lf[:, :n], in_=f[:, :n], func=AF.Ln),
                         reads=[K_('f')], writes=[K_('lf')])
                    b.op('pool', lambda E, f=f, k=k, n=n: E.tensor_scalar(
                        out=k[:, :n], in0=f[:, :n], scalar1=-1.0, scalar2=1.0, op0=ALU.mult, op1=ALU.add),
                        reads=[K_('f')], writes=[K_('k')])
                    b.op('dve', lambda E, lf=lf, bc=bc, n=n: E.tensor_tensor_scan(
                        out=bc[:, :n], data0=rmask[:, :n], data1=lf[:, :n], initial=0.0, op0=ALU.mult, op1=ALU.add),
                        reads=[K_('lf'), 'cst'], writes=[K_('bc')])
                    b.op('act', lambda E, bc=bc, eb=eb, n=n: E.activation(out=eb[:, :n], in_=bc[:, :n], func=AF.Exp),
                         reads=[K_('bc')], writes=[K_('eb')])
                    b.op('act', lambda E, bc=bc, enb=enb, n=n: E.activation(out=enb[:, :n], in_=bc[:, :n], func=AF.Exp,
                                                                          scale=-1.0),
                         reads=[K_('bc')], writes=[K_('enb')])
                    b.op('pool', lambda E, qin=qin, eb=eb, dr=dr, n=n: E.tensor_tensor(
                        out=Qt[dr][:, :n], in0=qin, in1=eb[:, :n], op=ALU.mult),
                        reads=[K_('q'), K_('eb')], writes=[K_('Qt')])
                    b.op('dve', lambda E, k=k, enb=enb, k32=k32, n=n: E.tensor_tensor(
                        out=k32[:, :n], in0=k[:, :n], in1=enb[:, :n], op=ALU.mult),
                        reads=[K_('k'), K_('enb')], writes=[K_('k32')])
                    b.op('act', lambda E, k32=k32, dr=dr, n=n: E.activation(out=Kt[dr][:, :n], in_=k32[:, :n], func=AF.Copy),
                         reads=[K_('k32')], writes=[K_('Kt')])
                    ebl = eb[:, :n].rearrange("p (c j) -> p c j", j=32)[:, :, 31:32].broadcast_to([128, n // 32, 32])
                    b.op('pool', lambda E, k32=k32, ebl=ebl, dr=dr, n=n: E.tensor_tensor(
                        out=KbT[dr][:, :n].rearrange("p (c j) -> p c j", j=32),
                        in0=k32[:, :n].rearrange("p (c j) -> p c j", j=32), in1=ebl, op=ALU.mult),
                        reads=[K_('k32'), K_('eb')], writes=[K_('KbT')])
                for tt in range(n // 128):
                    info = {}
                    for dr in range(2):
                        K_ = lambda x, dr=dr: (x, dr)
                        c0 = tt * 128
                        nt0 = sblk[dr][bi][0]
                        if dr == 0:
                            ti = (nt0 + c0) // 128
                            st0 = nt0 + c0
                        else:
                            ti = (nt0 + n - c0 - 128) // 128
                            st0 = (0 if bi == 0 else SC + (bi - 1) * 512) + c0
                        info[dr] = (ti, st0)
                        b.op('pe', lambda E, dr=dr, c0=c0: E.transpose(out=pT1[:, :], in_=KbT[dr][:, c0:c0 + 128],
                                                                       identity=ident[:, :]),
                             reads=[K_('KbT'), 'ident'], writes=[('pT', 0)])
                        b.op('act', lambda E, dr=dr: E.activation(out=Kb[dr][:, :], in_=pT1[:, :], func=AF.Copy),
                             reads=[('pT', 0)], writes=[K_('Kb')])
                        b.op('act', lambda E, dr=dr: E.activation(
                            out=Kbz[dr][64:128, :], in_=pT1[64:128, :], func=AF.Identity, scale=cst_sb[64:128, 768:769]),
                            reads=[('pT', 0), 'cst'], writes=[K_('Kbz')])
                        b.op('pe', lambda E, dr=dr, c0=c0: E.matmul(pA1[:, 0:128], lhsT=Kt[dr][:, c0:c0 + 128],
                                                                    rhs=Qt[dr][:, c0:c0 + 128], start=True, stop=True),
                             reads=[K_('Kt'), K_('Qt')], writes=[('pA', 0)])
                        b.op('dve', lambda E, dr=dr: E.tensor_tensor(out=Am[dr][:, :], in0=pA1[:, 0:128], in1=amask,
                                                                     op=ALU.mult),
                             reads=[('pA', 0), 'cst'], writes=[K_('Am')])
                        b.op('pe', lambda E, dr=dr, ti=ti: E.matmul(pO[dr][:, :], lhsT=V[dr][:, ti, :], rhs=Am[dr][:, :],
                                                                    start=True, stop=False),
                             reads=[('V', dr), K_('Am')], writes=[K_('pO')])
                        for c in range(2):
                            emit_ps(dr, ti, c)
                    for dr in range(2):
                        K_ = lambda x, dr=dr: (x, dr)
                        eb = Wk['eb'][dr]
                        c0 = tt * 128
                        ti, st0 = info[dr]
                        for c in range(4):
                            cs = slice(c * 32, (c + 1) * 32)
                            psl = pS[dr][c % 2][:, :]
                            sprev = S16[dr][(c - 1) % 4]
                            b.op('pe', lambda E, dr=dr, c0=c0, c=c, cs=cs, sprev=sprev: E.matmul(
                                pO[dr][:, cs], lhsT=sprev[:, :], rhs=Qt[dr][:, c0 + c * 32:c0 + (c + 1) * 32],
                                start=False, stop=(c == 3)),
                                reads=[('S16', dr, (c - 1) % 4), K_('Qt')], writes=[K_('pO')])
                            b.op('dve', lambda E, dr=dr, eb=eb, c0=c0, c=c, psl=psl: E.scalar_tensor_tensor(
                                out=S32[dr][:, :], in0=S32[dr][:, :], scalar=eb[:, c0 + c * 32 + 31:c0 + c * 32 + 32],
                                in1=psl, op0=ALU.mult, op1=ALU.add),
                                reads=[('S32', dr), K_('eb'), ('pS', dr, c % 2)], writes=[('S32', dr)])
                            b.op('dve', lambda E, dr=dr, c=c: E.tensor_copy(out=S16[dr][c][:, :], in_=S32[dr][:, :]),
                                 reads=[('S32', dr)], writes=[('S16', dr, c)])
                            if c + 2 < 4:
                                emit_ps(dr, ti, c + 2)
                        b.op('act', lambda E, dr=dr, st0=st0: E.activation(
                            out=O[dr][:, st0:st0 + 128], in_=pO[dr][:, :], func=AF.Copy),
                            reads=[K_('pO')], writes=[('O', dr)])
            for (t0, n) in nat_blks:
                lo = 0 if t0 < SC else SC + SL - (t0 - SC) - n
                sb_, sq, rs, g, mo = fb['s'], fb['sq'], fb['rs'], fb['g'], fb['mo']
                load_nat('sp', g, n, t0, 2, pr, 'fg')
                b.op('dve', lambda E, t0=t0, n=n, lo=lo: E.tensor_tensor(
                    out=sb_[:, :n], in0=O[0][:, t0:t0 + n], in1=O[1][:, lo:lo + n][:, ::-1], op=ALU.add),
                    reads=[('O', 0), ('O', 1)], writes=['fs'])
                b.op('act', lambda E, n=n: E.activation(out=sq[:, :n], in_=sb_[:, :n], func=AF.Square),
                     reads=['fs'], writes=['fsq'])
                b.op('pe', lambda E, n=n: E.matmul(pNb[:, :n], lhsT=ones32[:, :], rhs=sq[:, :n], start=True, stop=True),
                     reads=['fsq', 'ones32'], writes=[('pA', 0)])
                b.op('act', lambda E, n=n: E.activation(out=rs[:, :n], in_=pNb[:, :n], func=AF.Sqrt,
                                                        scale=1.0 / 128, bias=epsb[:, 0:1]),
                     reads=[('pA', 0), 'epsb'], writes=['frs'])
                b.op('dve', lambda E, n=n: E.reciprocal(out=rs[:, :n], in_=rs[:, :n]), reads=['frs'], writes=['frs'])
                b.op('act', lambda E, n=n: E.activation(out=g[:, :n], in_=g[:, :n], func=AF.Silu),
                     reads=['fg'], writes=['fg'])
                b.op('dve', lambda E, n=n, pr=pr: E.scalar_tensor_tensor(
                    out=sb_[:, :n], in0=sb_[:, :n], scalar=pv_sb[:, 16 + pr:17 + pr], in1=rs[:, :n],
                    op0=ALU.mult, op1=ALU.mult), reads=['fs', 'frs', 'pv'], writes=['fs'])
                b.op('pool', lambda E, n=n: E.tensor_tensor(out=mo[:, :n], in0=sb_[:, :n], in1=g[:, :n], op=ALU.mult),
                     reads=['fs', 'fg'], writes=['fmo'])
                write_nat(b, MS, pr * 128, t0, n, mo, ['fmo'], TLc, TCc)
            for dest in range(4):
                c_ = (dest * 512 + pr * 128) // CH
                gather_rows(b, C.Msend, C.Mall, c_, c_ + 2, ['Msend'], 'Mall')
        b.flush()


def _dbg_dump(b, C, oT, src_ap, rkey):
    b.dma('sp', oT, src_ap, reads=[rkey], writes=['out'])
    b.flush()


def build_fused():
    nc = bass.Bass("TRN2", target_bir_lowering=False)
    C = Ctx()
    C.TLc = SEQ_ * B_ // NCORES
    C.TCc = CTX_ * B_ // NCORES
    C.Tc = C.TLc + C.TCc
    TS = CTX_ + SEQ_

    def inp(name, shape):
        return nc.dram_tensor(name, shape, F32, kind="ExternalInput").ap()
    xT = inp("xT", [D, C.Tc])
    C.cT = inp("cT", [D, 4])
    C.wm = inp("wm", [DEPTH * D, 3072])
    C.bm = inp("bm", [128, DEPTH * 24])
    C.gains = inp("gains", [128, 9, KC])
    NA, NB_ = (NLAYERS + 1) // 2, NLAYERS // 2
    if DBG_STOP == 'mod':
        NA, NB_ = 0, 0
    awin = [inp("awin%d" % j, [D, 5 * D]) for j in range(NA)]
    awout = [inp("awout%d" % j, [D, D]) for j in range(0 if DBG_STOP else NA)]
    bwin = [inp("bwin%d" % j, [D, 2 * D]) for j in range(NB_)]
    bwout = [inp("bwout%d" % j, [D, D]) for j in range(NB_)]
    NW = 0 if DBG_STOP else NLAYERS
    w1 = [inp("w1_%d" % l, [D, DFF]) for l in range(NW)]
    w2 = [inp("w2_%d" % l, [DFF, D]) for l in range(NW)]
    C.pvA = inp("pvA", [2, 128, 20])
    C.cst = inp("cst", [128, 897])
    C.pvB = inp("pvB", [2, 128, 44])
    C.gwB = inp("gwB", [2, 8, 256, 256])
    oT = nc.dram_tensor("oT", [D, C.TLc], F32, kind="ExternalOutput").ap()
    C.modsend = nc.dram_tensor("modsend", [512, DEPTH * 24], F32)
    C.modall = nc.dram_tensor("modall", [4 * 512, DEPTH * 24], F32)
    C.modmine = nc.dram_tensor("modmine", [4 * 128, DEPTH * 24], F32)
    C.xbuf = nc.dram_tensor("xbuf", [D, C.Tc], F32)
    C.Psend = [nc.dram_tensor("Psend%d" % s_, [D, C.Tc], F32) for s_ in range(5)]
    C.Pall = [nc.dram_tensor("Pall%d" % s_, [4 * D, C.Tc], F32) for s_ in range(5)]
    C.Pmine = [nc.dram_tensor("Pmine%d" % s_, [D, C.Tc], F32) for s_ in range(5)]
    C.Msend = nc.dram_tensor("Msend", [4 * 512, C.Tc], F32)
    C.Mall = nc.dram_tensor("Mall", [4 * D, C.Tc], F32)
    C.Mmine = nc.dram_tensor("Mmine", [D, C.Tc], F32)
    C.Upad = nc.dram_tensor("Upad", [2, 2, 128, TS + 7], F32)
    C.wcache = nc.dram_tensor("wcache", [36, 128, 8192], BF16).ap()
    with ExitStack() as es:
        b = Bld(nc, es)
        C.ones32 = b.sb('ones32', [128, 128], F32)
        b.op('dve', lambda E: E.memset(C.ones32[:], 1.0), writes=['ones32'])
        C.epsb = b.sb('epsb', [128, 1], F32)
        b.op('dve', lambda E: E.memset(C.epsb[:], EPS), writes=['epsb'])
        C.modtab = b.sb('modtab', [128, 2, DEPTH, 96], F32)
        C.gains_sb = b.sb('gains', [128, 9, KC], F32)
        phase_mod(b, C)
        if DBG_STOP == 'mod':
            with ExitStack() as es2:
                b.es = es2
                t_ = b.sb('dbg', [128, 2 * DEPTH * 96], F32)
                b.op('dve', lambda E: E.tensor_copy(out=t_[:], in_=C.modtab[:].rearrange("p a l g -> p (a l g)")),
                     reads=['modtab'], writes=['dbg'])
                _dbg_dump(b, C, oT[0:256, 0:384].rearrange("(a p) n -> p a n", p=128), t_[:].rearrange("p (a n) -> p a n", a=2), 'dbg')
            return nc
        for layer in range(NLAYERS):
            j = layer // 2
            final = (layer == NLAYERS - 1)
            x_in = xT if layer == 0 else C.xbuf.ap()
            if layer % 2 == 0:
                phase_proj(b, C, layer, x_in, awin[j], 5 * D)
                if DBG_STOP == 'proj':
                    _dbg_dump(b, C, oT[:, 0:C.TLc], C.Pall[0].ap()[4096:6144, 0:C.TLc], ('Pall', 0))
                    return nc
                phase_extract(b, [(C.Pall[s_], C.Pmine[s_].ap(), 1, 2048, 2048, C.Tc, 'k', ('Pall', s_), ('Pmine', s_))
                                  for s_ in range(5)])
                if DBG_STOP == 'extract':
                    _dbg_dump(b, C, oT[:, 0:C.TLc], C.Pmine[3].ap()[:, 0:C.TLc], ('Pmine', 3))
                    return nc
                phase_scan_a(b, C, j, 1 if j > 0 else 0)
                if DBG_STOP == 'scan':
                    _dbg_dump(b, C, oT[:, 0:C.TLc], C.Mall.ap()[2048:4096, 0:C.TLc], 'Mall')
                    return nc
                wo = awout[j]
            else:
                phase_proj(b, C, layer, x_in, bwin[j], 2 * D)
                phase_extract(b, [(C.Pall[s_], C.Pmine[s_].ap(), 1, 2048, 2048, C.Tc, 'k', ('Pall', s_), ('Pmine', s_))
                                  for s_ in (0, 1)])
                phase_scan_b(b, C, j)
                wo = bwout[j]
            phase_extract(b, [(C.Mall, C.Mmine.ap(), 1, 2048, 2048, C.Tc, 'k', 'Mall', 'Mmine')])
            phase_l3(b, C, layer, x_in, wo, w1[layer], w2[layer], oT if final else C.xbuf.ap(), final)
    return nc


_NC = {}


def _f32(a):
    return np.ascontiguousarray(a, dtype=np.float32)


def _pl(v):
    return np.asarray(v, dtype=np.float32).reshape(KC, 128).T


def _consts_a():
    ident = np.eye(128, dtype=np.float32)
    s_ = np.arange(128)[:, None]
    t_ = np.arange(128)[None, :]
    am = ((s_ <= t_) & (s_ // 32 == t_ // 32)).astype(np.float32)
    rm = np.ones((128, 512), np.float32)
    rm[:, ::32] = 0
    m96 = (np.arange(128) >= 96).astype(np.float32)[:, None]
    Jm = np.ascontiguousarray(ident[::-1])
    return np.ascontiguousarray(np.concatenate([ident, am, rm, m96, Jm], axis=1))


def make_in_maps(x, c, ctx, c_ctx, w_mod, b_mod, norm1, norm2, a_w_in, a_lb_logits, a_onorm, a_w_out,
                 b_w_in, b_conv_w, b_conv_b, b_gate_w, b_gate_b, b_lambda, b_w_out, mlp_w1, mlp_w2, final_norm):
    TLc = SEQ_ * B_ // NCORES
    TCc = CTX_ * B_ // NCORES
    x = np.asarray(x, np.float32)
    ctx = np.asarray(ctx, np.float32)
    latf = x.reshape(B_ * SEQ_, D)
    ctxf = ctx.reshape(B_ * CTX_, D)
    cT = np.zeros((D, 4), np.float32)
    cT[:, 0] = c[0]
    cT[:, 1] = c[1]
    cT[:, 2] = c_ctx
    gains = np.zeros((128, 9, KC), np.float32)
    for l in range(NLAYERS):
        gains[:, l] = _pl(norm1[l])
        gains[:, 4 + l] = _pl(norm2[l])
    gains[:, 8] = _pl(final_norm)
    cst = _consts_a()
    shared = {"cT": cT, "gains": _f32(gains), "cst": cst}
    for j in range((NLAYERS + 1) // 2):
        shared["awin%d" % j] = _f32(a_w_in[j])
        shared["awout%d" % j] = _f32(a_w_out[j])
    for j in range(NLAYERS // 2):
        shared["bwin%d" % j] = _f32(b_w_in[j])
        shared["bwout%d" % j] = _f32(b_w_out[j])
    for l in range(NLAYERS):
        shared["w1_%d" % l] = _f32(mlp_w1[l])
        shared["w2_%d" % l] = _f32(mlp_w2[l])
    ims = []
    for r in range(NCORES):
        k = r % 4
        m = dict(shared)
        m["xT"] = _f32(np.concatenate([latf[r * TLc:(r + 1) * TLc], ctxf[r * TCc:(r + 1) * TCc]], axis=0).T)
        m["wm"] = _f32(np.concatenate([w_mod[l][:, k * 3072:(k + 1) * 3072] for l in range(DEPTH)], axis=0))
        m["bm"] = _f32(np.concatenate([np.asarray(b_mod[l][k * 3072:(k + 1) * 3072]).reshape(24, 128).T
                                       for l in range(DEPTH)], axis=1))
        pvA = []
        for j in range(2):
            cols = []
            for pi_ in range(4):
                hd = 4 * k + pi_
                sl = slice(hd * 128, (hd + 1) * 128)
                for d_ in range(2):
                    cols += [a_lb_logits[0, d_, sl], a_lb_logits[j, d_, sl]]
            for pi_ in range(4):
                hd = 4 * k + pi_
                cols.append(a_onorm[j][hd * 128:(hd + 1) * 128])
            pvA.append(np.stack(cols, axis=1))
        m["pvA"] = _f32(np.stack(pvA))
        pvB, gwB = [], []
        for j in range(2):
            cols, gws = [], []
            for u in range(2):
                blk = 2 * k + u
                sl = slice(blk * 256, (blk + 1) * 256)

                def h2(v_):
                    return np.asarray(v_[sl], np.float32).reshape(2, 128)
                cols += [h2(b_conv_w[j][jj])[cc] for cc in range(2) for jj in range(4)] + \
                        [h2(b_conv_b[j])[cc] for cc in range(2)] + \
                        [h2(b_gate_b[j][d_, gi_])[cc] for d_ in range(2) for gi_ in range(2) for cc in range(2)] + \
                        [h2(b_lambda[j][d_])[cc] for d_ in range(2) for cc in range(2)]
                gws += [b_gate_w[j][d_, gi_, blk] for d_ in range(2) for gi_ in range(2)]
            pvB.append(np.stack(cols, axis=1))
            gwB.append(np.stack(gws))
        m["pvB"] = _f32(np.stack(pvB))
        m["gwB"] = _f32(np.stack(gwB))
        ims.append(m)
    return ims


def kernel(**inputs):
    if 'nc' not in _NC:
        _NC['nc'] = build_fused()
    nc = _NC['nc']
    ims = make_in_maps(**inputs)
    res = run_bass_kernel_spmd(nc, ims, core_ids=list(range(NCORES))).results
    out = np.concatenate([res[r]["oT"].T for r in range(NCORES)], axis=0).reshape(B_, SEQ_, D)
    return np.ascontiguousarray(out, dtype=np.float32)
```
